# Optimizing a Trainium2 kernel written in Bass

```python
import jax, jax.numpy as jnp
from jax import lax
import numpy as np

D_MODEL = 1024
BATCH = 16
SEQ = 4096
DEPTH = 2
DEC_BATCH = 8
DEC_SEQ = 2048
PAST_LEN = 128

N_MIXERS = 2
N_GLA_LAYERS = (DEPTH + 1) // 2
N_MLA_LAYERS = DEPTH // 2
PLE_DIM = 256
D_FF = 2816
NORM_EPS = 1e-6

GLA_HEADS = 4
GLA_DK = D_MODEL // 2 // GLA_HEADS
GLA_DV = D_MODEL // GLA_HEADS
GLA_GATE_RANK = 16
GLA_GATE_TAU = 16.0
GLA_CHUNK = 64
GLA_QK_W = GLA_HEADS * GLA_DK
GLA_V_W = GLA_HEADS * GLA_DV
GLA_IN = 2 * GLA_QK_W + 2 * GLA_V_W + 2 * GLA_GATE_RANK

MLA_HEADS = 8
MLA_Q_RANK = 384
MLA_KV_RANK = 256
MLA_NOPE = 128
MLA_ROPE = 64
MLA_V = 128
MLA_IN = MLA_Q_RANK + MLA_KV_RANK + MLA_ROPE
ROPE_THETA = 10000.0
Q_BLOCK = 128

kernel_name = 'hybrid_gla_mla_macaron_encoder'


def rms_norm(x, g):
    xf = x.astype(jnp.float32)
    y = xf * lax.rsqrt(jnp.mean(xf * xf, axis=-1, keepdims=True) + NORM_EPS)
    return (y * g.astype(jnp.float32)).astype(x.dtype)


def swiglu(h, w_in, w_out):
    gate, up = jnp.split(h @ w_in, 2, axis=-1)
    return (jax.nn.silu(gate) * up) @ w_out


def gla_chunked(q, k, v, g, strict):
    B, H, L, DK = q.shape
    DV = v.shape[-1]
    C = GLA_CHUNK
    N = L // C
    q = q.reshape(B, H, N, C, DK)
    k = k.reshape(B, H, N, C, DK)
    v = v.reshape(B, H, N, C, DV)
    b = jnp.cumsum(g.reshape(B, H, N, C, DK), axis=3)
    b_last = b[:, :, :, -1:, :]
    q_dec = q * jnp.exp(b)
    k_inv = k * jnp.exp(-b)
    k_end = k * jnp.exp(b_last - b)
    mask = jnp.tril(jnp.ones((C, C), dtype=bool), -1 if strict else 0)
    a = jnp.where(mask, jnp.einsum('bhnid,bhnjd->bhnij', q_dec, k_inv), 0.0)
    o_intra = jnp.einsum('bhnij,bhnjv->bhniv', a, v)
    ds = jnp.einsum('bhnjd,bhnjv->bhndv', k_end, v)
    chunk_decay = jnp.exp(b_last[:, :, :, 0, :])

    def step(s, inp):
        q_n, dec_n, ds_n = inp
        o_n = jnp.einsum('bhid,bhdv->bhiv', q_n, s)
        return dec_n[..., None] * s + ds_n, o_n

    s0 = jnp.zeros((B, H, DK, DV), q.dtype)
    _, o_inter = lax.scan(step, s0, (jnp.moveaxis(q_dec, 2, 0), jnp.moveaxis(chunk_decay, 2, 0), jnp.moveaxis(ds, 2, 0)))
    o = o_intra + jnp.moveaxis(o_inter, 0, 2)
    return o.reshape(B, H, L, DV)


def gla_mixer(h, w_in, w_gf_up, b_gf, w_gb_up, b_gb, out_norm, w_out):
    B, S, _ = h.shape
    idx = [GLA_QK_W, 2 * GLA_QK_W, 2 * GLA_QK_W + GLA_V_W, 2 * GLA_QK_W + 2 * GLA_V_W,
           2 * GLA_QK_W + 2 * GLA_V_W + GLA_GATE_RANK]
    q, k, v, r, gf_lo, gb_lo = jnp.split(h @ w_in, idx, axis=-1)

    def heads(t, d):
        return t.reshape(B, S, GLA_HEADS, d).transpose(0, 2, 1, 3).astype(jnp.float32)

    q = heads(q, GLA_DK) * (GLA_DK ** -0.5)
    k = heads(k, GLA_DK)
    v = heads(v, GLA_DV)
    gf = heads(jax.nn.log_sigmoid((gf_lo @ w_gf_up + b_gf).astype(jnp.float32)) / GLA_GATE_TAU, GLA_DK)
    gb = heads(jax.nn.log_sigmoid((gb_lo @ w_gb_up + b_gb).astype(jnp.float32)) / GLA_GATE_TAU, GLA_DK)
    flip = lambda t: jnp.flip(t, axis=2)
    o_f = gla_chunked(q, k, v, gf, False)
    o_b = flip(gla_chunked(flip(q), flip(k), flip(v), flip(gb), True))
    o = (o_f + o_b).transpose(0, 2, 1, 3)
    o = rms_norm(o, out_norm).astype(h.dtype) * jax.nn.silu(r).reshape(B, S, GLA_HEADS, GLA_DV)
    return o.reshape(B, S, GLA_V_W) @ w_out


def rope_tables(S):
    inv_freq = ROPE_THETA ** (-jnp.arange(0, MLA_ROPE, 2, dtype=jnp.float32) / MLA_ROPE)
    ang = jnp.arange(S, dtype=jnp.float32)[:, None] * inv_freq[None, :]
    return jnp.cos(ang), jnp.sin(ang)


def rope(x, cos, sin):
    x1, x2 = jnp.split(x, 2, axis=-1)
    cos = cos.astype(x.dtype)
    sin = sin.astype(x.dtype)
    return jnp.concatenate([x1 * cos - x2 * sin, x2 * cos + x1 * sin], axis=-1)


def mla_mixer(h, w_in, q_norm, kv_norm, w_uq, w_ukv, w_out):
    B, S, _ = h.shape
    cq, ckv, kr = jnp.split(h @ w_in, [MLA_Q_RANK, MLA_Q_RANK + MLA_KV_RANK], axis=-1)
    q = (rms_norm(cq, q_norm) @ w_uq).reshape(B, S, MLA_HEADS, MLA_NOPE + MLA_ROPE)
    q_nope, q_rope = jnp.split(q, [MLA_NOPE], axis=-1)
    kv = (rms_norm(ckv, kv_norm) @ w_ukv).reshape(B, S, MLA_HEADS, MLA_NOPE + MLA_V)
    k_nope, v = jnp.split(kv, [MLA_NOPE], axis=-1)
    cos, sin = rope_tables(S)
    q_rope = rope(q_rope, cos[:, None, :], sin[:, None, :])
    kr = rope(kr, cos, sin)
    scale = (MLA_NOPE + MLA_ROPE) ** -0.5
    nb = S // Q_BLOCK

    def to_blocks(t):
        return jnp.moveaxis(t.reshape(B, nb, Q_BLOCK, *t.shape[2:]), 1, 0)

    def attend(blk):
        qn, qr = blk
        s = jnp.einsum('bqhd,bkhd->bhqk', qn, k_nope) + jnp.einsum('bqhr,bkr->bhqk', qr, kr)
        p = jax.nn.softmax(s.astype(jnp.float32) * scale, axis=-1).astype(v.dtype)
        return jnp.einsum('bhqk,bkhd->bqhd', p, v)

    o = lax.map(attend, (to_blocks(q_nope), to_blocks(q_rope)))
    o = jnp.moveaxis(o, 0, 1).reshape(B, S, MLA_HEADS * MLA_V)
    return o @ w_out


def trunk(x, p, ffn_norm, ffn_w_in, ffn_w_out, mix_norm, ple_norm, ple_w_gate, ple_w_proj,
          gla_w_in, gla_w_gf_up, gla_b_gf, gla_w_gb_up, gla_b_gb, gla_out_norm, gla_w_out,
          mla_w_in, mla_q_norm, mla_kv_norm, mla_w_uq, mla_w_ukv, mla_w_out, final_norm):
    for i in range(DEPTH):
        x = x + 0.5 * swiglu(rms_norm(x, ffn_norm[i, 0]), ffn_w_in[i, 0], ffn_w_out[i, 0])
        h = rms_norm(x, mix_norm[i])
        j = i // N_MIXERS
        if i % N_MIXERS == 0:
            x = x + gla_mixer(h, gla_w_in[j], gla_w_gf_up[j], gla_b_gf[j], gla_w_gb_up[j], gla_b_gb[j],
                              gla_out_norm[j], gla_w_out[j])
        else:
            x = x + mla_mixer(h, mla_w_in[j], mla_q_norm[j], mla_kv_norm[j], mla_w_uq[j], mla_w_ukv[j], mla_w_out[j])
        x = x + 0.5 * swiglu(rms_norm(x, ffn_norm[i, 1]), ffn_w_in[i, 1], ffn_w_out[i, 1])
        gate = jax.nn.sigmoid(rms_norm(x, ple_norm[i]) @ ple_w_gate[i])
        x = x + gate * (p[i] @ ple_w_proj[i])
    return rms_norm(x, final_norm)


def _dense(key, shape, fan_in):
    return jax.random.normal(key, shape, jnp.float32) * (fan_in ** -0.5)


def _gain(key, shape):
    return 1.0 + 0.02 * jax.random.normal(key, shape, jnp.float32)


def setup_inputs(seed: int = 0) -> dict:
    key = jax.random.key(seed)
    ks = jax.random.split(key, 26)
    NG, NM = N_GLA_LAYERS, N_MLA_LAYERS
    return {
        'x_prompt': jax.random.normal(ks[0], (BATCH, SEQ, D_MODEL), jnp.float32),
        'x_sample': jax.random.normal(ks[1], (DEC_BATCH, DEC_SEQ, D_MODEL), jnp.float32),
        'p_prompt': jax.random.normal(ks[2], (DEPTH, BATCH, SEQ, PLE_DIM), jnp.float32),
        'p_sample': jax.random.normal(ks[3], (DEPTH, DEC_BATCH, DEC_SEQ, PLE_DIM), jnp.float32),
        'ffn_norm': _gain(ks[4], (DEPTH, 2, D_MODEL)),
        'ffn_w_in': _dense(ks[5], (DEPTH, 2, D_MODEL, 2 * D_FF), D_MODEL),
        'ffn_w_out': _dense(ks[6], (DEPTH, 2, D_FF, D_MODEL), D_FF),
        'mix_norm': _gain(ks[7], (DEPTH, D_MODEL)),
        'ple_norm': _gain(ks[8], (DEPTH, D_MODEL)),
        'ple_w_gate': _dense(ks[9], (DEPTH, D_MODEL, D_MODEL), D_MODEL),
        'ple_w_proj': _dense(ks[10], (DEPTH, PLE_DIM, D_MODEL), PLE_DIM),
        'gla_w_in': _dense(ks[11], (NG, D_MODEL, GLA_IN), D_MODEL),
        'gla_w_gf_up': _dense(ks[12], (NG, GLA_GATE_RANK, GLA_QK_W), GLA_GATE_RANK),
        'gla_b_gf': 0.1 * jax.random.normal(ks[13], (NG, GLA_QK_W), jnp.float32),
        'gla_w_gb_up': _dense(ks[14], (NG, GLA_GATE_RANK, GLA_QK_W), GLA_GATE_RANK),
        'gla_b_gb': 0.1 * jax.random.normal(ks[15], (NG, GLA_QK_W), jnp.float32),
        'gla_out_norm': _gain(ks[16], (NG, GLA_DV)),
        'gla_w_out': _dense(ks[17], (NG, GLA_V_W, D_MODEL), GLA_V_W),
        'mla_w_in': _dense(ks[18], (NM, D_MODEL, MLA_IN), D_MODEL),
        'mla_q_norm': _gain(ks[19], (NM, MLA_Q_RANK)),
        'mla_kv_norm': _gain(ks[20], (NM, MLA_KV_RANK)),
        'mla_w_uq': _dense(ks[21], (NM, MLA_Q_RANK, MLA_HEADS * (MLA_NOPE + MLA_ROPE)), MLA_Q_RANK),
        'mla_w_ukv': _dense(ks[22], (NM, MLA_KV_RANK, MLA_HEADS * (MLA_NOPE + MLA_V)), MLA_KV_RANK),
        'mla_w_out': _dense(ks[23], (NM, MLA_HEADS * MLA_V, D_MODEL), MLA_HEADS * MLA_V),
        'final_norm': _gain(ks[24], (D_MODEL,)),
    }


def reference(x_prompt, x_sample, p_prompt, p_sample, ffn_norm, ffn_w_in, ffn_w_out, mix_norm, ple_norm,
              ple_w_gate, ple_w_proj, gla_w_in, gla_w_gf_up, gla_b_gf, gla_w_gb_up, gla_b_gb, gla_out_norm,
              gla_w_out, mla_w_in, mla_q_norm, mla_kv_norm, mla_w_uq, mla_w_ukv, mla_w_out, final_norm):
    weights = (ffn_norm, ffn_w_in, ffn_w_out, mix_norm, ple_norm, ple_w_gate, ple_w_proj,
               gla_w_in, gla_w_gf_up, gla_b_gf, gla_w_gb_up, gla_b_gb, gla_out_norm, gla_w_out,
               mla_w_in, mla_q_norm, mla_kv_norm, mla_w_uq, mla_w_ukv, mla_w_out, final_norm)
    y_prompt = trunk(x_prompt, p_prompt, *weights)
    y_sample = trunk(x_sample, p_sample, *weights)
    return (y_prompt, y_sample)
```

```python
import numpy as np
import concourse.bass as bass
import concourse.mybir as mybir
from concourse.bass_utils import run_bass_kernel_spmd

F32 = mybir.dt.float32
BF16 = mybir.dt.bfloat16
AF = mybir.ActivationFunctionType
ALU = mybir.AluOpType

N_DMA_SEMS = 24
D = 1024
DFF = 2816
NJ = 22
EPS = 1e-6
NCORES = 8
SEQS_FULL = [4096, 4096, 2048]


class Buf:
    __slots__ = ("name", "w", "r")

    def __init__(self, name=""):
        self.name = name
        self.w = ()
        self.r = {}


class Prog:
    COMPUTE = ("pe", "act", "dve", "pool")
    ALL = ("pe", "act", "dve", "pool", "sp")

    def __init__(self, nc):
        self.nc = nc
        self.streams = {e: [] for e in self.ALL}
        self.dma_count = {e: 0 for e in self.ALL}
        self.dma_last = {}

    def _deps(self, reads, writes, join=False):
        deps = set()
        for b in reads:
            deps.update(b.w)
        for b in writes:
            if not join:
                deps.update(b.w)
            deps.update(b.r.values())
        return deps

    def op(self, eng, fn, reads=(), writes=()):
        st = self.streams[eng]
        ev = (eng, len(st))
        deps = self._deps(reads, writes)
        st.append({"fn": fn, "deps": deps, "dma": None})
        for b in reads:
            b.r[eng] = ev
        for b in writes:
            b.w = (ev,)
            b.r = {}
        return ev

    def dma(self, q, out, in_, reads=(), writes=(), join=False, **kw):
        st = self.streams[q]
        k = self.dma_count[q]
        self.dma_count[q] += 1
        slot = k % N_DMA_SEMS
        val = 16 * (k // N_DMA_SEMS + 1)
        ev = ("dma", q, slot, val)
        deps = self._deps(reads, writes, join)
        prev = self.dma_last.get((q, slot))
        if prev is not None:
            deps.add(prev)
        self.dma_last[(q, slot)] = ev
        st.append({"fn": (lambda e, out=out, in_=in_, kw=kw: e.dma_start(out=out, in_=in_, **kw)),
                   "deps": deps, "dma": ev})
        for b in reads:
            b.r[ev] = ev
        for b in writes:
            b.w = (b.w + (ev,)) if join else (ev,)
            b.r = {}
        return ev

    def barrier(self):
        deps = set()
        for e in self.COMPUTE:
            if self.streams[e]:
                j = len(self.streams[e]) - 1
                while j >= 0 and self.streams[e][j]["fn"] is None:
                    j -= 1
                if j >= 0:
                    deps.add((e, j))
        deps.update(self.dma_last.values())
        for e in self.ALL:
            self.streams[e].append({"fn": None, "deps": set(deps), "dma": None})

    def emit(self):
        nc = self.nc
        marked = {e: set() for e in self.COMPUTE}
        for e in self.ALL:
            for ent in self.streams[e]:
                for d in ent["deps"]:
                    if d[0] != "dma":
                        if d[0] == "pe" and e == "pe":
                            continue
                        marked[d[0]].add(d[1])
        rank = {}
        for e in self.COMPUTE:
            rank[e] = {idx: i + 1 for i, idx in enumerate(sorted(marked[e]))}
        from contextlib import ExitStack
        with ExitStack() as es:
            csem = {e: es.enter_context(nc.semaphore("s_" + e)) for e in self.COMPUTE}
            dsem = {}
            for q in self.ALL:
                for s in range(min(N_DMA_SEMS, self.dma_count[q])):
                    dsem[(q, s)] = es.enter_context(nc.semaphore("d_%s_%d" % (q, s)))
            block = es.enter_context(nc.Block())
            streams = self.streams
            final_dma = list(self.dma_last.values())

            def run(ename, eng):
                known = {}
                for idx, ent in enumerate(streams[ename]):
                    waits = {}
                    for d in ent["deps"]:
                        if d[0] == "dma":
                            key = ("dma", d[1], d[2])
                            sem = dsem[(d[1], d[2])]
                            val = d[3]
                        else:
                            if d[0] == "pe" and ename == "pe":
                                continue
                            key = d[0]
                            sem = csem[d[0]]
                            val = rank[d[0]][d[1]]
                        if known.get(key, 0) >= val:
                            continue
                        if key not in waits or waits[key][1] < val:
                            waits[key] = (sem, val)
                    for key, (sem, val) in waits.items():
                        eng.wait_ge(sem, val)
                        known[key] = val
                    if ent["fn"] is None:
                        continue
                    ins = ent["fn"](eng)
                    if ent["dma"] is not None:
                        d = ent["dma"]
                        ins.then_inc(dsem[(d[1], d[2])], 16)
                    elif idx in rank.get(ename, {}):
                        ins.then_inc(csem[ename], 1)
                if ename == "sp":
                    for d in final_dma:
                        key = ("dma", d[1], d[2])
                        if known.get(key, 0) < d[3]:
                            eng.wait_ge(dsem[(d[1], d[2])], d[3])

            @block.tensor
            def _(e):
                run("pe", e)

            @block.scalar
            def _(e):
                run("act", e)

            @block.vector
            def _(e):
                run("dve", e)

            @block.gpsimd
            def _(e):
                run("pool", e)

            @block.sync
            def _(e):
                run("sp", e)


class Arena:
    def __init__(self, nc, limit=229376):
        self.nc = nc
        self.off = 16640
        self.limit = limit
        self.n = 0

    def alloc(self, shape, dtype):
        nbytes = int(np.prod(shape[1:])) * (2 if dtype == BF16 else 4)
        nbytes = (nbytes + 63) // 64 * 64
        off = self.off
        assert off + nbytes <= self.limit, ("SBUF overflow", off, nbytes)
        self.off += nbytes
        self.n += 1
        return self.nc.alloc_sbuf_tensor_at("t%d" % self.n, list(shape), dtype, offset=off)

    def mark(self):
        return self.off

    def reset(self, m):
        self.off = m


WEIGHT_SHAPES = {
    "ffn_w_in": (2, 2, D, 2 * DFF), "ffn_w_out": (2, 2, DFF, D),
    "ple_w_gate": (2, D, D), "ple_w_proj": (2, 256, D),
    "gla_w_in": (1, D, 3104), "gla_w_out": (1, D, D),
    "mla_w_in": (1, D, 704), "mla_w_uq": (1, 384, 1536), "mla_w_ukv": (1, 256, 2048), "mla_w_out": (1, D, D),
}
SMALL_SHAPES = {
    "ffn_norm": (2, 2, D), "mix_norm": (2, D), "ple_norm": (2, D),
    "gla_w_gf_up": (1, 16, 512), "gla_b_gf": (1, 512), "gla_w_gb_up": (1, 16, 512), "gla_b_gb": (1, 512),
    "gla_out_norm": (1, 256), "mla_q_norm": (1, 384), "mla_kv_norm": (1, 256), "final_norm": (D,),
}


class Builder:
    def __init__(self, seqs, dbg=False, stages=99):
        self.seqs = list(seqs)
        self.NT = sum(seqs)
        self.NTILE = self.NT // 512
        self.offs = [sum(seqs[:i]) for i in range(len(seqs))]
        self.dbg = dbg
        self.stages = stages
        nc = bass.Bass("TRN2", target_bir_lowering=False)
        self.nc = nc
        self.P = Prog(nc)
        self.A = Arena(nc)
        NT = self.NT
        inp = lambda n, s, dt=F32: nc.dram_tensor(n, list(s), dt, kind="ExternalInput").ap()
        self.xin = inp("xin", (NT, D))
        self.pin = inp("pin", (2, NT, 256))
        self.win = {n: inp(n, s) for n, s in WEIGHT_SHAPES.items()}
        self.sin = {n: inp(n, s) for n, s in SMALL_SHAPES.items()}
        self.c_ident = inp("c_ident", (128, 128))
        self.c_tri = inp("c_tri", (4, 128, 128))
        self.c_mask = inp("c_mask", (2, 128, 128))
        self.c_rope = inp("c_rope", (4096, 64))
        self.y = nc.dram_tensor("y", [NT, D], F32, kind="ExternalOutput").ap()
        kind = "ExternalOutput" if dbg else "Internal"
        scr = lambda n, s, dt=BF16: nc.dram_tensor(n, list(s), dt, kind=kind).ap()
        self.wbf = {n: nc.dram_tensor("bf_" + n, list(s), BF16).ap() for n, s in WEIGHT_SHAPES.items()}
        self.QD = scr("QD", (2, 4, 128, NT)); self.KI = scr("KI", (2, 4, 128, NT))
        self.KE = scr("KE", (2, NT, 512)); self.DEC = scr("DEC", (2, 4, 128, NT // 128), F32)
        self.VG = scr("VG", (NT, D)); self.SR = scr("SR", (NT, D)); self.OG = scr("OG", (NT, D))
        self.QN = scr("QN", (8, 128, NT)); self.QR = scr("QR", (4, 128, NT)); self.KN = scr("KN", (8, 128, NT))
        self.KR2 = scr("KR2", (128, NT)); self.VM = scr("VM", (NT, D)); self.OT = scr("OT", (8, 128, NT))
        nt = self.NTILE
        self.yB = [Buf("y%d" % t) for t in range(nt)]
        self.glaB = [Buf("gla%d" % t) for t in range(nt)]
        self.glaB2 = [[Buf() for t in range(nt)] for _ in range(6)]
        self.ogB = [[Buf() for h in range(4)] for _ in range(nt)]
        self.mlaB = [[Buf() for t in range(nt)] for _ in range(5)]
        self.otB = [[Buf() for h in range(8)] for _ in range(nt)]
        self.wB = {}
        self.psF = [nc.alloc_psum_tensor("psF%d" % i, [128, 512], F32) for i in range(6)]
        self.psFB = [Buf("psF%d" % i) for i in range(6)]
        self.psT = [nc.alloc_psum_tensor("psT%d" % i, [128, 1024], BF16) for i in range(2)]
        self.psTB = [Buf("psT%d" % i) for i in range(2)]
        self.ps_i = 0
        self.pst_i = 0
        self.cast_i = 0

    def ps(self):
        i = self.ps_i % 6
        self.ps_i += 1
        return self.psF[i], self.psFB[i]

    def pst(self):
        i = self.pst_i % 2
        self.pst_i += 1
        return self.psT[i], self.psTB[i]

    def mm(self, out, lhsT, rhs, start, stop, reads, writes):
        self.P.op("pe", lambda e: e.matmul(out, lhsT=lhsT, rhs=rhs, start=start, stop=stop), reads=reads, writes=writes)

    def tp(self, out, in_, reads, writes):
        idb = self.idb
        self.P.op("pe", lambda e: e.transpose(out=out, in_=in_, identity=idb[:]), reads=list(reads) + [self.constB], writes=writes)

    def act(self, out, in_, func, reads, writes, **kw):
        self.P.op("act", lambda e: e.activation(out=out, in_=in_, func=func, **kw), reads=reads, writes=writes)

    def copy_any(self, out, in_, reads, writes):
        self.cast_i += 1
        if self.cast_i % 2 == 0:
            self.P.op("act", lambda e: e.copy(out=out, in_=in_), reads=reads, writes=writes)
        else:
            self.P.op("dve", lambda e: e.tensor_copy(out=out, in_=in_), reads=reads, writes=writes)

    def tt(self, eng, out, in0, in1, op, reads, writes):
        self.P.op(eng, lambda e: e.tensor_tensor(out=out, in0=in0, in1=in1, op=op), reads=reads, writes=writes)

    def stt(self, eng, out, in0, scalar, in1, op0, op1, reads, writes):
        self.P.op(eng, lambda e: e.scalar_tensor_tensor(out=out, in0=in0, scalar=scalar, in1=in1, op0=op0, op1=op1),
                  reads=reads, writes=writes)

    def dma(self, out, in_, reads=(), writes=(), q="sp", **kw):
        self.P.dma(q, out, in_, reads=reads, writes=writes, **kw)

    def setup_consts(self):
        A, P, nc = self.A, self.P, self.nc
        self.constB = Buf("const")
        cB = self.constB
        tmpf = nc.alloc_sbuf_tensor_at("tmpf", [128, 2560], F32, offset=170048)
        tB = Buf()
        self.idb = A.alloc([128, 128], BF16)
        self.tri = A.alloc([128, 4, 128], F32)
        self.mask = A.alloc([128, 2, 128], BF16)
        self.ones = A.alloc([128, 128], BF16)
        self.dma(tmpf[:, 0:128], self.c_ident[:, :], writes=[tB])
        self.dma(tmpf[:, 128:384].rearrange("p (a b) -> p a b", a=2), self.c_mask.rearrange("a p b -> p a b"), writes=[tB])
        self.dma(self.tri[:], self.c_tri.rearrange("a p b -> p a b"), writes=[cB])
        P.op("dve", lambda e: e.tensor_copy(out=self.idb[:], in_=tmpf[:, 0:128]), reads=[tB], writes=[cB])
        P.op("dve", lambda e: e.tensor_copy(out=self.mask[:], in_=tmpf[:, 128:384].rearrange("p (a b) -> p a b", a=2)), reads=[tB], writes=[cB])
        P.op("dve", lambda e: e.memset(self.ones[:], 1.0), writes=[cB])
        self.gfm = {}
        def fm(name, ap, n):
            t = A.alloc([128, n // 128], F32)
            self.dma(t[:], ap.rearrange("(c p) -> p c", p=128), writes=[cB], allow_slow_non_contiguous=True)
            self.gfm[name] = t
        for l in range(2):
            for a in range(2):
                fm(("ffn", l, a), self.sin["ffn_norm"][l, a], D)
            fm(("mix", l), self.sin["mix_norm"][l], D)
            fm(("ple", l), self.sin["ple_norm"][l], D)
        fm("qn", self.sin["mla_q_norm"][0], 384)
        fm("kvn", self.sin["mla_kv_norm"][0], 256)
        self.g_final = A.alloc([128, D], F32)
        self.dma(self.g_final[:], self.sin["final_norm"].partition_broadcast(128), writes=[cB])
        self.g_out = A.alloc([128, 256], F32)
        self.dma(self.g_out[:], self.sin["gla_out_norm"][0].partition_broadcast(128), writes=[cB])
        self.wup = []
        for nm, bn in (("gla_w_gf_up", "gla_b_gf"), ("gla_w_gb_up", "gla_b_gb")):
            self.dma(tmpf[0:16, 1024:1536], self.sin[nm][0], writes=[tB])
            self.dma(tmpf[16:17, 1024:1536], self.sin[bn][0:1, :], writes=[tB])
            t = A.alloc([32, 512], BF16)
            P.op("dve", lambda e, t=t: e.tensor_copy(out=t[0:17, :], in_=tmpf[0:17, 1024:1536]), reads=[tB], writes=[cB])
            self.wup.append(t)
        self.lo_aug = [A.alloc([32, 512], BF16) for _ in range(2)]
        self.loB = [Buf(), Buf()]
        for i in range(2):
            P.op("dve", lambda e, i=i: e.memset(self.lo_aug[i][:], 1.0), writes=[self.loB[i]])

    CONV_ORDER = [("ffn_w_in", 0), ("ffn_w_out", 0), ("gla_w_in", 0),
                  ("gla_w_out", 0), ("ffn_w_in", 1), ("ffn_w_out", 1), ("ple_w_proj", 0), ("ple_w_gate", 0), ("ffn_w_in", 2), ("ffn_w_out", 2),
                  ("mla_w_in", 0), ("mla_w_uq", 0), ("mla_w_ukv", 0),
                  ("mla_w_out", 0), ("ffn_w_in", 3), ("ffn_w_out", 3), ("ple_w_proj", 1), ("ple_w_gate", 1)]

    def conv_gen(self, order, CH, engs, q="sp"):
        A, P = self.A, self.P
        stf = [A.alloc([128, CH], F32) for _ in range(2)]
        stb = [A.alloc([128, CH], BF16) for _ in range(2)]
        sfB = [Buf(), Buf()]
        sbB = [Buf(), Buf()]
        k = 0
        for name, idx in order:
            self.wB[(name, idx)] = []
        for name, idx in order:
            shp = WEIGHT_SHAPES[name]
            src = self.win[name]; dst = self.wbf[name]
            if len(shp) == 4:
                src = src[idx // 2, idx % 2]; dst = dst[idx // 2, idx % 2]
            else:
                src = src[idx]; dst = dst[idx]
            R, C = shp[-2], shp[-1]
            sv = src.rearrange("(p a) c -> p (a c)", p=128)
            dv = dst.rearrange("(p a) c -> p (a c)", p=128)
            n = R * C // 128
            for c0 in range(0, n, CH):
                cn = min(CH, n - c0)
                i = k % 2
                self.dma(stf[i][:, 0:cn], sv[:, c0:c0 + cn], writes=[sfB[i]], q=q)
                eng = engs[k % len(engs)]
                if eng == "act":
                    P.op("act", lambda e, i=i, cn=cn: e.copy(out=stb[i][:, 0:cn], in_=stf[i][:, 0:cn]), reads=[sfB[i]], writes=[sbB[i]])
                else:
                    P.op(eng, lambda e, i=i, cn=cn: e.tensor_copy(out=stb[i][:, 0:cn], in_=stf[i][:, 0:cn]), reads=[sfB[i]], writes=[sbB[i]])
                b = Buf()
                self.dma(dv[:, c0:c0 + cn], stb[i][:, 0:cn], reads=[sbB[i]], writes=[b], q=q)
                self.wB[(name, idx)].append(b)
                k += 1
                yield

    def ring_init(self, nslots=6):
        self.ring_n = nslots
        self.ring_slots = [self.A.alloc([128, 4096], BF16) for _ in range(nslots)]
        self.ring_bufs = [Buf("ring%d" % i) for i in range(nslots)]
        self.ring_descs = []
        self.ring_issued = 0
        self.ring_cur = 0

    def ring_view(self, slot, shape):
        n = int(np.prod(shape[1:]))
        v = self.ring_slots[slot][:, 0:n]
        if len(shape) == 3:
            v = v.rearrange("p (a b) -> p a b", a=shape[1])
        elif len(shape) == 4:
            v = v.rearrange("p (a b c) -> p a b c", a=shape[1], b=shape[2])
        return v

    def ring_next(self):
        while self.ring_issued < len(self.ring_descs) and self.ring_issued < self.ring_cur + self.ring_n - 2:
            k = self.ring_issued
            src, shape, bl = self.ring_descs[k]
            assert int(np.prod(shape[1:])) <= 4096, shape
            rv = self.ring_view(k % self.ring_n, shape)
            if len(shape) == 4:
                for g in range(shape[2]):
                    self.dma(rv[:, :, g, :], src[:, :, g, :], reads=bl, writes=[self.ring_bufs[k % self.ring_n]], join=(g > 0))
            else:
                self.dma(rv, src, reads=bl, writes=[self.ring_bufs[k % self.ring_n]])
            self.ring_issued += 1
        k = self.ring_cur
        self.ring_cur += 1
        assert k < self.ring_issued
        return self.ring_view(k % self.ring_n, self.ring_descs[k][1]), self.ring_bufs[k % self.ring_n]

    def alloc_token_bufs(self):
        A = self.A
        self.xt = [A.alloc([128, 4, D], F32) for _ in range(2)]
        self.xB = [[Buf() for s in range(4)] for _ in range(2)]
        self.xn = A.alloc([128, 4, D], BF16)
        self.xnB = [Buf() for s in range(4)]
        self.hT = A.alloc([128, 8, 512], BF16)
        self.hTB = [Buf() for s in range(4)]
        self.aT = A.alloc([128, NJ, 512], BF16)
        self.aTB = [Buf() for j in range(NJ)]
        self.sg = [A.alloc([128, 512], F32) for _ in range(2)]
        self.sgB = [Buf(), Buf()]
        self.sg_i = 0
        self.ss = A.alloc([128, 8], F32)
        self.ssB = [Buf() for _ in range(4)]

    def rms_pre(self, s, src, srcB_s, ncols, ss=None, ssB=None, junk=None, junkB=None):
        P = self.P
        ss = self.ss if ss is None else ss
        ssB = self.ssB if ssB is None else ssB
        junk = self.xn[:, s, 0:ncols] if junk is None else junk
        junkB = self.xnB[s] if junkB is None else junkB
        P.op("pool", lambda e: e.memset(ss[:, s:s + 1], 0.0), writes=[ssB[s]])
        self.act(junk, src, AF.Square, reads=[srcB_s], writes=[junkB, ssB[s]], accum_out=ss[:, s:s + 1])
        self.act(ss[:, 4 + s:5 + s], ss[:, s:s + 1], AF.Ln, reads=[ssB[s]], writes=[ssB[s]], scale=1.0 / ncols, bias=EPS)
        self.act(ss[:, 4 + s:5 + s], ss[:, 4 + s:5 + s], AF.Exp, reads=[ssB[s]], writes=[ssB[s]], scale=-0.5)

    def norm_pre(self, s, src, srcB_s, ncols, ss=None, ssB=None, xn=None, xnB=None):
        ss = self.ss if ss is None else ss
        ssB = self.ssB if ssB is None else ssB
        xn = self.xn[:, s, 0:ncols] if xn is None else xn
        xnB = self.xnB[s] if xnB is None else xnB
        self.rms_pre(s, src, srcB_s, ncols, ss, ssB, xn, xnB)
        self.act(xn, src, AF.Copy, reads=[srcB_s, ssB[s]], writes=[xnB], scale=ss[:, 4 + s:5 + s])

    def norm_post(self, s, gain, ncols, dstT, dstB_s, xn=None, xnB=None):
        nch = ncols // 128
        xn = self.xn[:, s, 0:ncols] if xn is None else xn
        xnB = self.xnB[s] if xnB is None else xnB
        pt, ptB = self.pst()
        for c in range(nch):
            self.tp(pt[:, c * 128:(c + 1) * 128], xn[:, c * 128:(c + 1) * 128], reads=[xnB], writes=[ptB])
        self.tt("dve", dstT[:, 0:nch, s * 128:(s + 1) * 128], pt[:, 0:nch * 128].rearrange("p (c t) -> p c t", c=nch),
                gain[:].unsqueeze(2).to_broadcast([128, nch, 128]), ALU.mult, reads=[ptB, self.constB], writes=[dstB_s])

    def xnorm_pre(self, xi, s):
        self.norm_pre(s, self.xt[xi][:, s, :], self.xB[xi][s], D)

    def xnorm_post(self, gain):
        for s in range(4):
            self.norm_post(s, gain, D, self.hT, self.hTB[s])

    def ffn_descs(self, l, a):
        d = []
        idx = l * 2 + a
        win = self.wbf["ffn_w_in"][l, a].rearrange("(c p) (g f) -> p c g f", p=128, g=2)
        for j0 in range(0, NJ, 2):
            nj = 2
            d.append((win[:, :, :, j0 * 128:(j0 + nj) * 128], [128, 8, 2, nj * 128], self.wB[("ffn_w_in", idx)]))
        wout = self.wbf["ffn_w_out"][l, a].rearrange("(j p) d -> p j d", p=128)
        for half in range(2):
            for j0 in range(0, NJ, 8):
                nj = min(8, NJ - j0)
                d.append((wout[:, j0:j0 + nj, half * 512:(half + 1) * 512], [128, nj, 512], self.wB[("ffn_w_out", idx)]))
        return d

    def ffn(self, xi, l, a, after=None, mid=None):
        xt, xB = self.xt[xi], self.xB[xi]
        self.xnorm_post(self.gfm[("ffn", l, a)])
        hT, hTB = self.hT, self.hTB
        for j0 in range(0, NJ, 2):
            nj = 2
            w, wb = self.ring_next()
            for jj in range(nj):
                j = j0 + jj
                pg, pgB = self.ps()
                for c in range(8):
                    self.mm(pg[:], w[:, c, 0, jj * 128:(jj + 1) * 128], hT[:, c, :], c == 0, c == 7, reads=[wb] + hTB, writes=[pgB])
                pu, puB = self.ps()
                for c in range(8):
                    self.mm(pu[:], w[:, c, 1, jj * 128:(jj + 1) * 128], hT[:, c, :], c == 0, c == 7, reads=[wb] + hTB, writes=[puB])
                si = self.sg_i % 2
                self.sg_i += 1
                self.act(self.sg[si][:], pg[:], AF.Silu, reads=[pgB], writes=[self.sgB[si]])
                self.tt("dve", self.aT[:, j, :], self.sg[si][:], pu[:], ALU.mult, reads=[self.sgB[si], puB], writes=[self.aTB[j]])
        if mid is not None:
            mid()
        for half in range(2):
            ws = [self.ring_next() for _ in range(3)]
            for s in range(4):
                py, pyB = self.ps()
                for j in range(NJ):
                    w, wb = ws[j // 8]
                    self.mm(py[:], self.aT[:, j, s * 128:(s + 1) * 128], w[:, j % 8, :],
                            j == 0, j == NJ - 1, reads=[self.aTB[j], wb], writes=[pyB])
                xs = xt[:, s, half * 512:(half + 1) * 512]
                self.stt("dve", xs, py[:], 0.5, xs, ALU.mult, ALU.add, reads=[pyB, xB[s]], writes=[xB[s]])
                if half == 1 and after is not None:
                    after(s)

    def proj_add(self, xi, srcT, srcTB, after=None):
        xt, xB = self.xt[xi], self.xB[xi]
        for half in range(2):
            w, wb = self.ring_next()
            for s in range(4):
                py, pyB = self.ps()
                for c in range(8):
                    self.mm(py[:], srcT[:, c, s * 128:(s + 1) * 128], w[:, c, :], c == 0, c == 7,
                            reads=list(srcTB) + [wb], writes=[pyB])
                xs = xt[:, s, half * 512:(half + 1) * 512]
                self.tt("dve", xs, py[:], xs, ALU.add, reads=[pyB, xB[s]], writes=[xB[s]])
                if half == 1 and after is not None:
                    after(s)

    def ple_descs(self, l):
        wg = self.wbf["ple_w_gate"][l].rearrange("(c p) d -> p c d", p=128)
        return [(self.wbf["ple_w_proj"][l].rearrange("(c p) d -> p c d", p=128), [128, 2, D], self.wB[("ple_w_proj", l)])] + \
               [(wg[:, :, hf * 512:(hf + 1) * 512], [128, 8, 512], self.wB[("ple_w_gate", l)]) for hf in range(2)]

    def ple_load(self, l, t):
        t0 = t * 512
        self.dma(self.pt[:], self.pin[l, t0:t0 + 512, :].rearrange("(s p) d -> p s d", p=128), writes=[self.ptB])

    def ple_prep(self, l, t):
        self.P.op("act", lambda e: e.copy(out=self.ptb[:], in_=self.pt[:]), reads=[self.ptB], writes=[self.ptbB])
        for s in range(4):
            pt, ptB = self.pst()
            for c in range(2):
                self.tp(pt[:, c * 128:(c + 1) * 128], self.ptb[:, s, c * 128:(c + 1) * 128], reads=[self.ptbB], writes=[ptB])
            self.copy_any(self.pT[:, :, s * 128:(s + 1) * 128], pt[:, 0:256].rearrange("p (c t) -> p c t", c=2), reads=[ptB], writes=[self.pTB])

    def ple(self, xi, l, t, after=None):
        xt, xB = self.xt[xi], self.xB[xi]
        t0 = t * 512
        self.xnorm_post(self.gfm[("ple", l)])
        wp, wpB = self.ring_next()
        for half in range(2):
            wg, wgB = self.ring_next()
            for s in range(4):
                pg, pgB = self.ps()
                for c in range(8):
                    self.mm(pg[:], self.hT[:, c, s * 128:(s + 1) * 128], wg[:, c, :], c == 0, c == 7,
                            reads=self.hTB + [wgB], writes=[pgB])
                pp, ppB = self.ps()
                for c in range(2):
                    self.mm(pp[:], self.pT[:, c, s * 128:(s + 1) * 128], wp[:, c, half * 512:(half + 1) * 512], c == 0, c == 1,
                            reads=[self.pTB, wpB], writes=[ppB])
                si = self.sg_i % 2
                self.sg_i += 1
                self.act(self.sg[si][:], pg[:], AF.Sigmoid, reads=[pgB], writes=[self.sgB[si]])
                self.tt("dve", self.sg[si][:], self.sg[si][:], pp[:], ALU.mult, reads=[self.sgB[si], ppB], writes=[self.sgB[si]])
                xs = xt[:, s, half * 512:(half + 1) * 512]
                self.tt("pool", xs, xs, self.sg[si][:], ALU.add, reads=[self.sgB[si], xB[s]], writes=[xB[s]])
                if half == 1 and after is not None:
                    after(s)

    def final_pre(self, xi, s):
        self.rms_pre(s, self.xt[xi][:, s, :], self.xB[xi][s], D)

    def final_norm(self, xi):
        xt, xB = self.xt[xi], self.xB[xi]
        for s in range(4):
            self.stt("dve", xt[:, s, :], xt[:, s, :], self.ss[:, 4 + s:5 + s], self.g_final[:], ALU.mult, ALU.mult,
                     reads=[xB[s], self.ssB[s], self.constB], writes=[xB[s]])

    def gla_proj_descs(self):
        w = self.wbf["gla_w_in"][0].rearrange("(c p) f -> p c f", p=128)
        bl = self.wB[("gla_w_in", 0)]
        return [(w[:, :, 3072:3104], [128, 8, 32], bl)] + [(w[:, :, i * 512:(i + 1) * 512], [128, 8, 512], bl) for i in range(6)]

    def alloc_gla_proj(self):
        A = self.A
        self.sp = [A.alloc([128, 4, 512], F32) for _ in range(2)]
        self.spB = [[Buf() for s in range(4)] for _ in range(2)]
        self.etmp = [A.alloc([128, 512], F32) for _ in range(4)]
        self.etmpB = [Buf() for _ in range(4)]
        self.et_i = 0
        self.qd_st = A.alloc([128, 2, 4, 512], BF16); self.qdB = Buf()
        self.ki_st = A.alloc([128, 2, 4, 512], BF16); self.kiB = Buf()
        self.ke_st = A.alloc([128, 2, 4, 512], BF16); self.keB = Buf()
        self.dec_st = A.alloc([128, 2, 4, 4], F32); self.decB = Buf()
        self.v_st = A.alloc([128, 4, D], BF16); self.vB = Buf()
        self.sr_st = A.alloc([128, 4, D], BF16); self.srB = Buf()

    def etmp_next(self):
        i = self.et_i % 4
        self.et_i += 1
        return self.etmp[i], self.etmpB[i]

    def gla_proj(self, xi, t):
        P = self.P
        xt, xB = self.xt[xi], self.xB[xi]
        t0 = t * 512
        self.xnorm_post(self.gfm[("mix", 0)])
        hT, hTB = self.hT, self.hTB
        wlo, wloB = self.ring_next()
        for d in range(2):
            pl, plB = self.ps()
            for c in range(8):
                self.mm(pl[0:16, :], wlo[:, c, d * 16:(d + 1) * 16], hT[:, c, :], c == 0, c == 7, reads=[wloB] + hTB, writes=[plB])
            self.copy_any(self.lo_aug[d][0:16, :], pl[0:16, :], reads=[plB], writes=[self.loB[d]])
            for s in range(4):
                pz, pzB = self.ps()
                self.mm(pz[:], self.lo_aug[d][0:17, s * 128:(s + 1) * 128], self.wup[d][0:17, :], True, True,
                        reads=[self.loB[d], self.constB], writes=[pzB])
                et, etB = self.etmp_next()
                self.act(et[:], pz[:], AF.Exp, reads=[pzB], writes=[etB], scale=-1.0)
                self.act(self.sp[d][:, s, :], et[:], AF.Ln, reads=[etB], writes=[self.spB[d][s]], bias=1.0)
        wq, wqB = self.ring_next()
        wk, wkB = self.ring_next()
        for h in range(4):
            pq, pqB = self.ps()
            for c in range(8):
                self.mm(pq[:], wq[:, c, h * 128:(h + 1) * 128], hT[:, c, :], c == 0, c == 7, reads=[wqB] + hTB, writes=[pqB])
            pk, pkB = self.ps()
            for c in range(8):
                self.mm(pk[:], wk[:, c, h * 128:(h + 1) * 128], hT[:, c, :], c == 0, c == 7, reads=[wkB] + hTB, writes=[pkB])
            for d in range(2):
                pb, pbB = self.ps()
                for s in range(4):
                    self.mm(pb[:, s * 128:(s + 1) * 128], self.sp[d][:, s, h * 128:(h + 1) * 128], self.tri[:, d, :], True, True,
                            reads=[self.spB[d][s], self.constB], writes=[pbB])
                eb, ebB = self.etmp_next()
                self.act(eb[:], pb[:], AF.Exp, reads=[pbB], writes=[ebB])
                ei, eiB = self.etmp_next()
                self.act(ei[:], pb[:], AF.Exp, reads=[pbB], writes=[eiB], scale=-1.0)
                self.stt("dve", self.qd_st[:, d, h, :], pq[:], 128.0 ** -0.5, eb[:], ALU.mult, ALU.mult, reads=[pqB, ebB], writes=[self.qdB])
                self.tt("dve", self.ki_st[:, d, h, :], pk[:], ei[:], ALU.mult, reads=[pkB, eiB], writes=[self.kiB])
                col = 127 if d == 0 else 0
                ebv = eb[:].rearrange("p (s t) -> p s t", s=4)[:, :, col]
                P.op("pool", lambda e, d=d, h=h, ebv=ebv: e.tensor_copy(out=self.dec_st[:, d, h, :], in_=ebv), reads=[ebB], writes=[self.decB])
        for s in range(4):
            pk, pkB = self.ps()
            for c in range(8):
                self.mm(pk[:], hT[:, c, s * 128:(s + 1) * 128], wk[:, c, :], c == 0, c == 7, reads=[wkB] + hTB, writes=[pkB])
            for d in range(2):
                pe_, peB = self.ps()
                self.mm(pe_[:], self.tri[:, 2 + d, :], self.sp[d][:, s, :], True, True, reads=[self.spB[d][s], self.constB], writes=[peB])
                ee, eeB = self.etmp_next()
                self.act(ee[:], pe_[:], AF.Exp, reads=[peB], writes=[eeB])
                self.tt("dve", self.ke_st[:, d, s, :], pk[:], ee[:], ALU.mult, reads=[pkB, eeB], writes=[self.keB])
        for half in range(2):
            wv, wvB = self.ring_next()
            for s in range(4):
                pv, pvB = self.ps()
                for c in range(8):
                    self.mm(pv[:], hT[:, c, s * 128:(s + 1) * 128], wv[:, c, :], c == 0, c == 7, reads=[wvB] + hTB, writes=[pvB])
                self.copy_any(self.v_st[:, s, half * 512:(half + 1) * 512], pv[:], reads=[pvB], writes=[self.vB])
        for half in range(2):
            wr, wrB = self.ring_next()
            for s in range(4):
                pr, prB = self.ps()
                for c in range(8):
                    self.mm(pr[:], hT[:, c, s * 128:(s + 1) * 128], wr[:, c, :], c == 0, c == 7, reads=[wrB] + hTB, writes=[prB])
                self.act(self.sr_st[:, s, half * 512:(half + 1) * 512], pr[:], AF.Silu, reads=[prB], writes=[self.srB])
        g2 = self.glaB2
        self.dma(self.QD.rearrange("d h p t -> p (d h) t")[:, :, t0:t0 + 512], self.qd_st[:].rearrange("p d h t -> p (d h) t"),
                 reads=[self.qdB], writes=[g2[0][t]])
        self.dma(self.KI.rearrange("d h p t -> p (d h) t")[:, :, t0:t0 + 512], self.ki_st[:].rearrange("p d h t -> p (d h) t"),
                 reads=[self.kiB], writes=[g2[1][t]])
        for d in range(2):
            self.dma(self.KE[d, t0:t0 + 512, :].rearrange("(s p) f -> p s f", p=128), self.ke_st[:, d, :, :], reads=[self.keB],
                     writes=[g2[2][t]], join=(d > 0))
        self.dma(self.DEC.rearrange("d h p n -> p (d h) n")[:, :, t * 4:t * 4 + 4], self.dec_st[:].rearrange("p d h n -> p (d h) n"),
                 reads=[self.decB], writes=[g2[3][t]])
        self.dma(self.VG[t0:t0 + 512, :].rearrange("(s p) f -> p s f", p=128), self.v_st[:], reads=[self.vB], writes=[g2[4][t]])
        self.dma(self.SR[t0:t0 + 512, :].rearrange("(s p) f -> p s f", p=128), self.sr_st[:], reads=[self.srB], writes=[g2[5][t]])

    def gla_scan(self, bg=None, bg_every=5):
        A, P = self.A, self.P
        Lmax = max(self.seqs)
        NBm = Lmax // 128
        qd = [[A.alloc([128, Lmax], BF16) for _ in range(2)] for _ in range(2)]
        ki = [[A.alloc([128, Lmax], BF16) for _ in range(2)] for _ in range(2)]
        ke = [[A.alloc([128, NBm, 128], BF16) for _ in range(2)] for _ in range(2)]
        dec = [[A.alloc([128, NBm], F32) for _ in range(2)] for _ in range(2)]
        inB = [Buf("scan_in0"), Buf("scan_in1")]
        vv = A.alloc([128, NBm, 256], BF16); vvB = Buf()
        sr = A.alloc([128, NBm, 256], BF16); srB = Buf()
        oacc = A.alloc([128, NBm, 256], F32)
        oB = [Buf() for _ in range(NBm)]
        ogst = A.alloc([128, NBm, 256], BF16)
        ogB = Buf()
        S32 = [A.alloc([128, 256], F32) for _ in range(2)]
        S32B = [Buf(), Buf()]
        Sbf = [[A.alloc([128, 256], BF16) for _ in range(3)] for _ in range(2)]
        SbfB = [[Buf(), Buf(), Buf()], [Buf(), Buf(), Buf()]]
        atsb = [A.alloc([128, 128], BF16) for _ in range(4)]
        atB = [Buf() for _ in range(4)]
        ssn = A.alloc([128, 2 * NBm], F32)
        ssB = Buf()
        junk = Sbf[0][0]
        junkB = SbfB[0][0]
        g2 = self.glaB2
        units = [(si, h) for si in range(len(self.seqs)) for h in range(4)]

        def load_big(u):
            si, h = units[u]
            L = self.seqs[si]; off = self.offs[si]; NB = L // 128
            tiles = list(range(off // 512, (off + L) // 512))
            b = u % 2
            first = True
            for d in range(2):
                self.dma(qd[b][d][:, 0:L], self.QD[d, h, :, off:off + L], reads=[g2[0][t] for t in tiles], writes=[inB[b]], join=not first)
                first = False
                self.dma(ki[b][d][:, 0:L], self.KI[d, h, :, off:off + L], reads=[g2[1][t] for t in tiles], writes=[inB[b]], join=True)
                self.dma(ke[b][d][:, 0:NB, :], self.KE[d, off:off + L, h * 128:(h + 1) * 128].rearrange("(n p) f -> p n f", p=128),
                         reads=[g2[2][t] for t in tiles], writes=[inB[b]], join=True)
                self.dma(dec[b][d][:, 0:NB], self.DEC[d, h, :, off // 128:off // 128 + NB], reads=[g2[3][t] for t in tiles], writes=[inB[b]], join=True)

        it = 0
        load_big(0)
        for u, (si, h) in enumerate(units):
            L = self.seqs[si]; off = self.offs[si]; NB = L // 128
            tiles = list(range(off // 512, (off + L) // 512))
            b = u % 2
            self.dma(vv[:, 0:NB, :], self.VG[off:off + L, h * 256:(h + 1) * 256].rearrange("(n p) f -> p n f", p=128),
                     reads=[g2[4][t] for t in tiles], writes=[vvB])
            self.dma(sr[:, 0:NB, :], self.SR[off:off + L, h * 256:(h + 1) * 256].rearrange("(n p) f -> p n f", p=128),
                     reads=[g2[5][t] for t in tiles], writes=[srB])
            if u + 1 < len(units):
                load_big(u + 1)
            touched = set()
            chunk = lambda i, d: i if d == 0 else NB - 1 - i
            pend = {}
            for i in range(NB + 1):
                if i < NB:
                    for d in range(2):
                        n = chunk(i, d)
                        blk = slice(n * 128, (n + 1) * 128)
                        pa, paB = self.ps()
                        self.mm(pa[:, 0:128], ki[b][d][:, blk], qd[b][d][:, blk], True, True, reads=[inB[b]], writes=[paB])
                        ai = (i % 2) * 2 + d
                        self.tt("dve", atsb[ai][:], pa[:, 0:128], self.mask[:, d, :], ALU.mult, reads=[paB, self.constB], writes=[atB[ai]])
                        if i < NB - 1:
                            pd, pdB = self.ps()
                            self.mm(pd[:, 0:256], ke[b][d][:, n, :], vv[:, n, :], True, True, reads=[inB[b], vvB], writes=[pdB])
                            if i == 0:
                                P.op("dve", lambda e, d=d, pd=pd: e.tensor_copy(out=S32[d][:], in_=pd[:, 0:256]), reads=[pdB], writes=[S32B[d]])
                            else:
                                self.stt("dve", S32[d][:], S32[d][:], dec[b][d][:, n:n + 1], pd[:, 0:256], ALU.mult, ALU.add,
                                         reads=[S32B[d], pdB, inB[b]], writes=[S32B[d]])
                            P.op("act", lambda e, d=d, i=i: e.copy(out=Sbf[d][i % 3][:], in_=S32[d][:]), reads=[S32B[d]], writes=[SbfB[d][i % 3]])
                if i >= 1:
                    j = i - 1
                    for d in range(2):
                        n = chunk(j, d)
                        blk = slice(n * 128, (n + 1) * 128)
                        ai = (j % 2) * 2 + d
                        po, poB = self.ps()
                        self.mm(po[:, 0:256], atsb[ai][:], vv[:, n, :], True, j == 0, reads=[atB[ai], vvB], writes=[poB])
                        if j > 0:
                            self.mm(po[:, 0:256], qd[b][d][:, blk], Sbf[d][(j - 1) % 3][:], False, True, reads=[inB[b], SbfB[d][(j - 1) % 3]], writes=[poB])
                        if n not in touched:
                            touched.add(n)
                            P.op("act", lambda e, n=n, po=po: e.copy(out=oacc[:, n, :], in_=po[:, 0:256]), reads=[poB], writes=[oB[n]])
                        else:
                            self.tt("dve", oacc[:, n, :], oacc[:, n, :], po[:, 0:256], ALU.add, reads=[poB, oB[n]], writes=[oB[n]])
                it += 1
                if bg is not None and it % bg_every == 0:
                    next(bg, None)
            P.op("pool", lambda e: e.memset(ssn[:], 0.0), writes=[ssB])
            for n in range(NB):
                self.act(junk[:], oacc[:, n, :], AF.Square, reads=[oB[n]], writes=[junkB, ssB], accum_out=ssn[:, n:n + 1])
            self.act(ssn[:, NBm:NBm + NB], ssn[:, 0:NB], AF.Sqrt, reads=[ssB], writes=[ssB], scale=1.0 / 256, bias=EPS)
            P.op("dve", lambda e, NB=NB: e.reciprocal(out=ssn[:, NBm:NBm + NB], in_=ssn[:, NBm:NBm + NB]), reads=[ssB], writes=[ssB])
            for n in range(NB):
                self.stt("dve", oacc[:, n, :], oacc[:, n, :], ssn[:, NBm + n:NBm + n + 1], self.g_out[:], ALU.mult, ALU.mult,
                         reads=[oB[n], ssB, self.constB], writes=[oB[n]])
                self.tt("pool", ogst[:, n, :], oacc[:, n, :], sr[:, n, :], ALU.mult, reads=[oB[n], srB], writes=[ogB])
            self.dma(self.OG[off:off + L, h * 256:(h + 1) * 256].rearrange("(n p) f -> p n f", p=128), ogst[:, 0:NB, :],
                     reads=[ogB], writes=[self.ogB[t][h] for t in tiles])
        if bg is not None:
            for _ in bg:
                pass

    def gla_out_descs(self):
        w = self.wbf["gla_w_out"][0].rearrange("(c p) d -> p c d", p=128)
        return [(w[:, :, hf * 512:(hf + 1) * 512], [128, 8, 512], self.wB[("gla_w_out", 0)]) for hf in range(2)]

    def gla_out_load(self, t):
        t0 = t * 512
        self.dma(self.ogt[t % 2][:], self.OG[t0:t0 + 512, :].rearrange("(s p) f -> p s f", p=128), reads=self.ogB[t], writes=[self.ogtB[t % 2]])

    def gla_out(self, xi, t, after=None):
        og, ogB_ = self.ogt[t % 2], self.ogtB[t % 2]
        for s in range(4):
            pt, ptB = self.pst()
            for c in range(8):
                self.tp(pt[:, c * 128:(c + 1) * 128], og[:, s, c * 128:(c + 1) * 128], reads=[ogB_], writes=[ptB])
            self.copy_any(self.hT[:, :, s * 128:(s + 1) * 128], pt[:].rearrange("p (c t) -> p c t", c=8), reads=[ptB], writes=[self.hTB[s]])
        if t + 1 < self.NTILE:
            self.gla_out_load(t + 1)
        self.proj_add(xi, self.hT, self.hTB, after)

    def mla_proj_descs(self):
        wi = self.wbf["mla_w_in"][0].rearrange("(c p) f -> p c f", p=128)
        wq = self.wbf["mla_w_uq"][0].rearrange("(c p) f -> p c f", p=128)
        return [(wi[:, :, 0:384], [128, 8, 384], self.wB[("mla_w_in", 0)]), (wi[:, :, 384:704], [128, 8, 320], self.wB[("mla_w_in", 0)]),
                (wq[:, :, 0:768], [128, 3, 768], self.wB[("mla_w_uq", 0)]), (wq[:, :, 768:1536], [128, 3, 768], self.wB[("mla_w_uq", 0)]),
                (self.wbf["mla_w_ukv"][0].rearrange("(c p) f -> p c f", p=128), [128, 2, 2048], self.wB[("mla_w_ukv", 0)])]

    def alloc_mla_proj(self):
        A = self.A
        self.cq = A.alloc([128, 4, 384], F32); self.cqB = [Buf() for _ in range(4)]
        self.ckv = A.alloc([128, 4, 256], F32); self.ckvB = [Buf() for _ in range(4)]
        self.krs = A.alloc([128, 4, 64], F32); self.krsB = Buf()
        self.cqT = A.alloc([128, 3, 512], BF16); self.cqTB = [Buf() for _ in range(4)]
        self.ckvT = A.alloc([128, 2, 512], BF16); self.ckvTB = [Buf() for _ in range(4)]
        self.cs = A.alloc([128, 4, 64], F32); self.csB = Buf()
        self.ssq = A.alloc([128, 8], F32); self.ssqB = [Buf() for _ in range(4)]
        self.sskv = A.alloc([128, 8], F32); self.sskvB = [Buf() for _ in range(4)]
        self.xnB2 = [Buf() for _ in range(4)]
        self.rt = [A.alloc([128, 8, 32], F32) for _ in range(4)]; self.rtB = [Buf() for _ in range(4)]
        self.qr_tok = A.alloc([128, 4, 512], BF16); self.qrtB = [Buf() for _ in range(4)]
        self.kr_tok = A.alloc([128, 4, 128], BF16); self.krtB = Buf()
        self.qn_st = self.aT[:, 0:8, :]
        self.qr_st = A.alloc([128, 4, 512], BF16); self.qrB = Buf()
        self.kn_st = self.aT[:, 8:16, :]
        self.kr2_st = A.alloc([128, 512], BF16); self.kr2B = Buf()
        self.vm_st = A.alloc([128, 4, D], BF16); self.vmB = Buf()

    def rope(self, x1, x2, s, nh, out1, out2, reads, writes):
        cos = self.cs[:, s, 0:32].unsqueeze(1).to_broadcast([128, nh, 32])
        sin = self.cs[:, s, 32:64].unsqueeze(1).to_broadcast([128, nh, 32])
        r = self.rt
        rB = self.rtB
        rd = list(reads) + [self.csB]
        v = lambda i: r[i][:, 0:nh, :]
        self.tt("dve", v(0), x1, cos, ALU.mult, reads=rd, writes=[rB[0]])
        self.tt("dve", v(1), x2, sin, ALU.mult, reads=rd, writes=[rB[1]])
        self.tt("dve", v(2), x2, cos, ALU.mult, reads=rd, writes=[rB[2]])
        self.tt("dve", v(3), x1, sin, ALU.mult, reads=rd, writes=[rB[3]])
        for o1, o2 in zip(out1, out2):
            self.tt("pool", o1, v(0), v(1), ALU.subtract, reads=[rB[0], rB[1]], writes=writes)
            self.tt("pool", o2, v(2), v(3), ALU.add, reads=[rB[2], rB[3]], writes=writes)

    def mla_proj_load(self, t):
        t0 = t * 512
        si = max(i for i in range(len(self.seqs)) if self.offs[i] <= t0)
        pos0 = t0 - self.offs[si]
        self.dma(self.cs[:], self.c_rope[pos0:pos0 + 512, :].rearrange("(s p) f -> p s f", p=128), writes=[self.csB])

    def mla_proj(self, xi, t):
        xt, xB = self.xt[xi], self.xB[xi]
        t0 = t * 512
        self.xnorm_post(self.gfm[("mix", 1)])
        hT, hTB = self.hT, self.hTB
        win, winB = self.ring_next()
        win2, win2B = self.ring_next()
        for s in range(4):
            p1, p1B = self.ps()
            for c in range(8):
                self.mm(p1[:, 0:384], hT[:, c, s * 128:(s + 1) * 128], win[:, c, :], c == 0, c == 7, reads=[winB] + hTB, writes=[p1B])
            self.copy_any(self.cq[:, s, :], p1[:, 0:384], reads=[p1B], writes=[self.cqB[s]])
            self.norm_pre(s, self.cq[:, s, :], self.cqB[s], 384, self.ssq, self.ssqB, self.xn[:, s, 0:384], self.xnB[s])
            p2, p2B = self.ps()
            for c in range(8):
                self.mm(p2[:, 0:320], hT[:, c, s * 128:(s + 1) * 128], win2[:, c, :], c == 0, c == 7, reads=[win2B] + hTB, writes=[p2B])
            self.P.op("dve", lambda e, s=s, p2=p2: e.tensor_copy(out=self.ckv[:, s, :], in_=p2[:, 0:256]), reads=[p2B], writes=[self.ckvB[s]])
            self.P.op("dve", lambda e, s=s, p2=p2: e.tensor_copy(out=self.krs[:, s, :], in_=p2[:, 256:320]), reads=[p2B], writes=[self.krsB])
            self.norm_pre(s, self.ckv[:, s, :], self.ckvB[s], 256, self.sskv, self.sskvB, self.xn[:, s, 512:768], self.xnB2[s])
        for s in range(4):
            self.norm_post(s, self.gfm["qn"], 384, self.cqT, self.cqTB[s], self.xn[:, s, 0:384], self.xnB[s])
            self.norm_post(s, self.gfm["kvn"], 256, self.ckvT, self.ckvTB[s], self.xn[:, s, 512:768], self.xnB2[s])
        wuqs = [self.ring_next(), self.ring_next()]
        for h in range(8):
            pq, pqB = self.ps()
            wuq, wuqB = wuqs[h // 4]
            hh = h % 4
            for c in range(3):
                self.mm(pq[:], wuq[:, c, hh * 192:hh * 192 + 128], self.cqT[:, c, :], c == 0, c == 2, reads=[wuqB] + self.cqTB, writes=[pqB])
            self.copy_any(self.qn_st[:, h, :], pq[:], reads=[pqB], writes=[self.aTB[h]])
        for s in range(4):
            pr, prB = self.ps()
            for g4 in range(2):
                wuq, wuqB = wuqs[g4]
                for c in range(3):
                    rhs = wuq[:, c, :].rearrange("p (h f) -> p h f", f=192)[:, :, 128:192]
                    self.mm(pr[:, g4 * 256:(g4 + 1) * 256].rearrange("p (h f) -> p h f", f=64), self.cqT[:, c, s * 128:(s + 1) * 128], rhs,
                            c == 0, c == 2, reads=[wuqB] + self.cqTB, writes=[prB])
            prv = pr[:].rearrange("p (h f) -> p h f", f=64)
            qv = self.qr_tok[:, s, :].rearrange("p (h f) -> p h f", f=64)
            self.rope(prv[:, :, 0:32], prv[:, :, 32:64], s, 8, [qv[:, :, 0:32]], [qv[:, :, 32:64]], reads=[prB], writes=[self.qrtB[s]])
            pt, ptB = self.pst()
            for m_ in range(4):
                self.tp(pt[:, m_ * 128:(m_ + 1) * 128], self.qr_tok[:, s, m_ * 128:(m_ + 1) * 128], reads=[self.qrtB[s]], writes=[ptB])
            self.copy_any(self.qr_st[:, :, s * 128:(s + 1) * 128], pt[:, 0:512].rearrange("p (c t) -> p c t", c=4), reads=[ptB], writes=[self.qrB])
        wkv, wkvB = self.ring_next()
        for h in range(8):
            pk, pkB = self.ps()
            for c in range(2):
                self.mm(pk[:], wkv[:, c, h * 256:h * 256 + 128], self.ckvT[:, c, :], c == 0, c == 1, reads=[wkvB] + self.ckvTB, writes=[pkB])
            self.copy_any(self.kn_st[:, h, :], pk[:], reads=[pkB], writes=[self.aTB[8 + h]])
        for s in range(4):
            for half in range(2):
                pv, pvB = self.ps()
                for c in range(2):
                    rhs = wkv[:, c, :].rearrange("p (h f) -> p h f", f=256)[:, 4 * half:4 * half + 4, 128:256]
                    self.mm(pv[:].rearrange("p (h f) -> p h f", f=128), self.ckvT[:, c, s * 128:(s + 1) * 128], rhs, c == 0, c == 1,
                            reads=[wkvB] + self.ckvTB, writes=[pvB])
                self.copy_any(self.vm_st[:, s, half * 512:(half + 1) * 512], pv[:], reads=[pvB], writes=[self.vmB])
            kv_ = self.krs[:, s, :].rearrange("p (h f) -> p h f", h=1)
            ko = self.kr_tok[:, s, :].rearrange("p (h f) -> p h f", h=1)
            self.rope(kv_[:, :, 0:32], kv_[:, :, 32:64], s, 1, [ko[:, :, 0:32], ko[:, :, 64:96]], [ko[:, :, 32:64], ko[:, :, 96:128]],
                      reads=[self.krsB], writes=[self.krtB])
            pt, ptB = self.pst()
            self.tp(pt[:, 0:128], self.kr_tok[:, s, :], reads=[self.krtB], writes=[ptB])
            self.copy_any(self.kr2_st[:, s * 128:(s + 1) * 128], pt[:, 0:128], reads=[ptB], writes=[self.kr2B])
        mb = self.mlaB
        self.dma(self.QN.rearrange("h p t -> p h t")[:, :, t0:t0 + 512], self.qn_st, reads=self.aTB[0:8], writes=[mb[0][t]])
        self.dma(self.QR.rearrange("h p t -> p h t")[:, :, t0:t0 + 512], self.qr_st[:], reads=[self.qrB], writes=[mb[1][t]])
        self.dma(self.KN.rearrange("h p t -> p h t")[:, :, t0:t0 + 512], self.kn_st, reads=self.aTB[8:16], writes=[mb[2][t]])
        self.dma(self.KR2[:, t0:t0 + 512], self.kr2_st[:], reads=[self.kr2B], writes=[mb[3][t]])
        self.dma(self.VM[t0:t0 + 512, :].rearrange("(s p) f -> p s f", p=128), self.vm_st[:], reads=[self.vmB], writes=[mb[4][t]])

    def mla_attn(self, bg=None):
        A, P = self.A, self.P
        m = A.mark()
        Lmax = max(self.seqs)
        NBm = Lmax // 128
        kr2 = [A.alloc([128, Lmax], BF16) for _ in range(2)]; kr2B = [Buf(), Buf()]
        qr = [A.alloc([128, Lmax], BF16) for _ in range(2)]; qrB = [Buf(), Buf()]
        kn = [A.alloc([128, Lmax], BF16) for _ in range(2)]; knB = [Buf(), Buf()]
        qn = [A.alloc([128, Lmax], BF16) for _ in range(2)]; qnB = [Buf(), Buf()]
        vv = [A.alloc([128, NBm, 128], BF16) for _ in range(2)]; vB = [Buf(), Buf()]
        ot = [A.alloc([128, Lmax], BF16) for _ in range(2)]; otB = [Buf(), Buf()]
        pT = [A.alloc([128, 512], BF16) for _ in range(4)]; pTB = [Buf() for _ in range(4)]
        rden = A.alloc([128, 512], F32); rdB = Buf()
        kra = A.alloc([128, Lmax], BF16); krb = A.alloc([128, Lmax], BF16); kraB = Buf(); krbB = Buf()
        accP = [A.alloc([128, 512], F32) for _ in range(2)]; accPB = [Buf(), Buf()]
        accD = [A.alloc([128, 512], F32) for _ in range(2)]; accDB = [Buf(), Buf()]
        ones32 = A.alloc([128, 128], F32); o32B = Buf()
        P.op("pool", lambda e: e.memset(ones32[:], 1.0), writes=[o32B])
        pending = []
        P.op("pool", lambda e: e.memset(kra[:], 0.0), writes=[kraB])
        P.op("pool", lambda e: e.memset(krb[:], 0.0), writes=[krbB])
        scale = 192.0 ** -0.5
        mb = self.mlaB
        hcount = 0
        pcount = 0
        pi = 0
        qt_count = 0
        for si, L in enumerate(self.seqs):
            off = self.offs[si]
            NB = L // 128
            NQ = L // 512
            tiles = list(range(off // 512, (off + L) // 512))
            ks = si % 2
            self.dma(kr2[ks][:, 0:L], self.KR2[:, off:off + L], reads=[mb[3][t] for t in tiles], writes=[kr2B[ks]])
            P.op("dve", lambda e, ks=ks, L=L: e.tensor_copy(out=kra[0:64, 0:L], in_=kr2[ks][0:64, 0:L]), reads=[kr2B[ks]], writes=[kraB])
            P.op("pool", lambda e, ks=ks, L=L: e.tensor_copy(out=krb[64:128, 0:L], in_=kr2[ks][64:128, 0:L]), reads=[kr2B[ks]], writes=[krbB])
            for h in range(8):
                hs = hcount % 2
                hcount += 1
                if h % 2 == 0:
                    ps_ = pcount % 2
                    pcount += 1
                    self.dma(qr[ps_][:, 0:L], self.QR[h // 2, :, off:off + L], reads=[mb[1][t] for t in tiles], writes=[qrB[ps_]])
                self.dma(kn[hs][:, 0:L], self.KN[h, :, off:off + L], reads=[mb[2][t] for t in tiles], writes=[knB[hs]])
                self.dma(qn[hs][:, 0:L], self.QN[h, :, off:off + L], reads=[mb[0][t] for t in tiles], writes=[qnB[hs]])
                self.dma(vv[hs][:, 0:NB, :], self.VM[off:off + L, h * 128:(h + 1) * 128].rearrange("(n p) f -> p n f", p=128),
                         reads=[mb[4][t] for t in tiles], writes=[vB[hs]])
                r0 = 64 * (h % 2)
                for qt in range(NQ):
                    qsl = slice(qt * 512, (qt + 1) * 512)
                    po, poB = self.psF[3 + qt_count % 2], self.psFB[3 + qt_count % 2]
                    aP, aPB = accP[qt_count % 2], accPB[qt_count % 2]
                    aD, aDB = accD[qt_count % 2], accDB[qt_count % 2]
                    qt_count += 1
                    pd, pdB = self.psT[0][:, 0:1024].bitcast(F32), self.psTB[0]
                    krx, krxB = (kra, kraB) if h % 2 == 0 else (krb, krbB)

                    sbank = [0, 1, 2, 5]

                    def qk(kb):
                        j = sbank[kb % 4]
                        psb, psB = self.psF[j], self.psFB[j]
                        ksl = slice(kb * 128, (kb + 1) * 128)
                        self.mm(psb[:], kn[hs][:, ksl], qn[hs][:, qsl], True, False, reads=[knB[hs], qnB[hs]], writes=[psB])
                        self.mm(psb[:], krx[:, ksl], qr[ps_][:, qsl], False, True, reads=[krxB, qrB[ps_]], writes=[psB])

                    def pv(kb):
                        nonlocal pi
                        j = sbank[kb % 4]
                        psb, psB = self.psF[j], self.psFB[j]
                        pt_, ptB_ = pT[pi % 4], pTB[pi % 4]
                        pi += 1
                        self.act(pt_[:], psb[:], AF.Exp, reads=[psB], writes=[ptB_], scale=scale)
                        self.mm(po[:], vv[hs][:, kb, :], pt_[:], kb == 0, kb == NB - 1, reads=[vB[hs], ptB_], writes=[poB])
                        eng, acc, accB = ("pool", aP, aPB) if kb % 2 == 0 else ("dve", aD, aDB)
                        if kb < 2:
                            P.op(eng, lambda e, acc=acc, pt_=pt_: e.tensor_copy(out=acc[:], in_=pt_[:]), reads=[ptB_], writes=[accB])
                        else:
                            self.tt(eng, acc[:], acc[:], pt_[:], ALU.add, reads=[ptB_, accB], writes=[accB])

                    qk(0)
                    if NB > 1:
                        qk(1)
                    for kb in range(NB):
                        if kb + 2 < NB:
                            qk(kb + 2)
                        pv(kb)
                        if kb == min(3, NB - 1) and pending:
                            pending.pop()()
                    def make_ep(aP=aP, aPB=aPB, aD=aD, aDB=aDB, po=po, poB=poB, ot_ap=ot[hs][:, qsl], otB_h=otB[hs], pd=pd, pdB=pdB,
                                last=(qt == NQ - 1), h=h, hs=hs, off=off, L=L, tiles=tiles):
                        def ep():
                            self.tt("dve", aD[:], aD[:], aP[:], ALU.add, reads=[aPB, aDB], writes=[aDB])
                            self.mm(pd, ones32[:], aD[:], True, True, reads=[o32B, aDB], writes=[pdB])
                            P.op("dve", lambda e: e.reciprocal(out=rden[:], in_=pd), reads=[pdB], writes=[rdB])
                            self.tt("dve", ot_ap, po[:], rden[:], ALU.mult, reads=[poB, rdB], writes=[otB_h])
                            if last:
                                self.dma(self.OT[h, :, off:off + L], ot[hs][:, 0:L], reads=[otB_h], writes=[self.otB[t][h] for t in tiles])
                        return ep
                    pending.append(make_ep())
                    if bg is not None and qt_count % 3 == 0:
                        next(bg, None)
        while pending:
            pending.pop()()
        if bg is not None:
            for _ in bg:
                pass
        A.reset(m)

    def mla_out_descs(self):
        w = self.wbf["mla_w_out"][0].rearrange("(c p) d -> p c d", p=128)
        return [(w[:, :, hf * 512:(hf + 1) * 512], [128, 8, 512], self.wB[("mla_w_out", 0)]) for hf in range(2)]

    def mla_out_load(self, t):
        t0 = t * 512
        self.dma(self.ott[t % 2][:], self.OT.rearrange("h p t -> p h t")[:, :, t0:t0 + 512], reads=self.otB[t], writes=[self.ottB[t % 2]])

    def mla_out(self, xi, t, after=None):
        self.proj_add(xi, self.ott[t % 2], [self.ottB[t % 2]], after)

    def token_phase(self, src, ops, descs_fn, tile_loads=(), next_loads=(), first_loads=()):
        for t in range(self.NTILE):
            self.ring_descs.extend(descs_fn())
        srcap = self.xin if src == "xin" else self.y

        def load(t):
            rd = [] if src == "xin" else [self.yB[t]]
            self.dma(self.xt[t % 2][:], srcap[t * 512:t * 512 + 512, :].rearrange("(s p) d -> p s d", p=128), reads=rd, writes=self.xB[t % 2])
        load(0)
        for f in list(next_loads) + list(first_loads):
            f(0)
        for t in range(self.NTILE):
            xi = t % 2
            t0 = t * 512
            for f in tile_loads:
                f(t)
            if t + 1 < self.NTILE:
                load(t + 1)
                for f in next_loads:
                    f(t + 1)
            for k, (pre, body, _tail) in enumerate(ops):
                if pre is not None and (k == 0 or not ops[k - 1][2]):
                    for s in range(4):
                        pre(xi, s)
                nxt = ops[k + 1][0] if k + 1 < len(ops) else None
                after = (lambda s, nxt=nxt, xi=xi: nxt(xi, s)) if (nxt is not None and ops[k][2]) else None
                body(xi, t, after)
            self.dma(self.y[t0:t0 + 512, :].rearrange("(s p) d -> p s d", p=128), self.xt[xi][:], reads=self.xB[xi], writes=[self.yB[t]])

    def build(self):
        A = self.A
        self.setup_consts()
        if self.stages == -1:
            self.P.emit(); return self.nc
        self.P.barrier()
        if self.stages == -2:
            self.P.emit(); return self.nc
        m0 = A.mark()
        for _ in self.conv_gen(self.CONV_ORDER[0:3], 4096, ["pool", "dve", "act"]):
            pass
        A.reset(m0)
        self.P.barrier()
        base = A.mark()
        if self.stages <= 0:
            self.P.emit(); return self.nc
        self.alloc_token_bufs()
        self.ring_init()
        tok_mark = A.mark()
        self.alloc_gla_proj()
        xp = self.xnorm_pre
        self.token_phase("xin", [(xp, lambda xi, t, af: self.ffn(xi, 0, 0, af), True), (xp, lambda xi, t, af: self.gla_proj(xi, t), False)],
                         lambda: self.ffn_descs(0, 0) + self.gla_proj_descs())
        if self.stages <= 1:
            self.P.emit(); return self.nc
        self.P.barrier()
        A.reset(base)
        bg = self.conv_gen(self.CONV_ORDER[3:13], 1024, ["act", "dve"], q="sp")
        next(bg)
        self.gla_scan(bg, bg_every=2)
        if self.stages <= 2:
            self.P.emit(); return self.nc
        self.P.barrier()
        A.reset(tok_mark)
        self.pt = A.alloc([128, 4, 256], F32); self.ptB = Buf()
        self.ptb = A.alloc([128, 4, 256], BF16); self.ptbB = Buf()
        self.pT = A.alloc([128, 2, 512], BF16); self.pTB = Buf()
        self.alloc_mla_proj()
        og1 = A.alloc([128, 4, D], BF16); ogb1 = Buf()
        self.ogt = [og1, og1]; self.ogtB = [ogb1, ogb1]
        xp = self.xnorm_pre
        self.token_phase("y", [(None, self.gla_out, True), (xp, lambda xi, t, af: self.ffn(xi, 0, 1, af, mid=lambda: self.ple_prep(0, t)), True),
                               (xp, lambda xi, t, af: self.ple(xi, 0, t, af), True), (xp, lambda xi, t, af: self.ffn(xi, 1, 0, af), True),
                               (xp, lambda xi, t, af: self.mla_proj(xi, t), False)],
                         lambda: self.gla_out_descs() + self.ffn_descs(0, 1) + self.ple_descs(0) + self.ffn_descs(1, 0) + self.mla_proj_descs(),
                         tile_loads=[lambda t: self.ple_load(0, t), self.mla_proj_load], first_loads=[self.gla_out_load])
        if self.stages <= 3:
            self.P.emit(); return self.nc
        self.P.barrier()
        A.reset(base)
        bg = self.conv_gen(self.CONV_ORDER[13:18], 2048, ["act"], q="sp")
        next(bg)
        self.mla_attn(bg)
        if self.stages <= 4:
            self.P.emit(); return self.nc
        self.P.barrier()
        A.reset(tok_mark)
        self.pt = A.alloc([128, 4, 256], F32); self.ptb = A.alloc([128, 4, 256], BF16); self.pT = A.alloc([128, 2, 512], BF16)
        self.ott = [A.alloc([128, 8, 512], BF16) for _ in range(2)]; self.ottB = [Buf(), Buf()]
        xp = self.xnorm_pre
        self.token_phase("y", [(None, self.mla_out, True), (xp, lambda xi, t, af: self.ffn(xi, 1, 1, af, mid=lambda: self.ple_prep(1, t)), True),
                               (xp, lambda xi, t, af: self.ple(xi, 1, t, af), True),
                               (self.final_pre, lambda xi, t, af: self.final_norm(xi), False)],
                         lambda: self.mla_out_descs() + self.ffn_descs(1, 1) + self.ple_descs(1),
                         tile_loads=[lambda t: self.ple_load(1, t)], next_loads=[self.mla_out_load])
        self.P.emit()
        return self.nc


def make_consts():
    c = {}
    c["c_ident"] = np.eye(128, dtype=np.float32)
    s = np.arange(128)[:, None]
    t = np.arange(128)[None, :]
    g = np.float32(-1.0 / 16.0)
    tri = np.zeros((4, 128, 128), np.float32)
    tri[0] = (s <= t) * g
    tri[1] = (s >= t) * g
    tri[2] = (s > t) * g
    tri[3] = (s < t) * g
    c["c_tri"] = tri
    mask = np.zeros((2, 128, 128), np.float32)
    mask[0] = (s <= t)
    mask[1] = (s > t)
    c["c_mask"] = mask
    inv_freq = (np.float32(10000.0) ** (-np.arange(0, 64, 2, dtype=np.float32) / np.float32(64))).astype(np.float32)
    ang = np.arange(4096, dtype=np.float32)[:, None] * inv_freq[None, :]
    c["c_rope"] = np.concatenate([np.cos(ang), np.sin(ang)], axis=1).astype(np.float32)
    return c


_ALL_W = list(WEIGHT_SHAPES) + list(SMALL_SHAPES)


def kernel(**inputs):
    inputs = {k: np.asarray(v) for k, v in inputs.items()}
    xp, xs = inputs["x_prompt"], inputs["x_sample"]
    pp, psm = inputs["p_prompt"], inputs["p_sample"]
    nc = Builder(SEQS_FULL).build()
    consts = make_consts()
    in_maps = []
    for i in range(NCORES):
        m = {}
        m["xin"] = np.ascontiguousarray(np.concatenate([xp[2 * i], xp[2 * i + 1], xs[i]], axis=0))
        m["pin"] = np.ascontiguousarray(np.concatenate([pp[:, 2 * i], pp[:, 2 * i + 1], psm[:, i]], axis=1))
        for n in _ALL_W:
            m[n] = np.ascontiguousarray(inputs[n]).reshape(WEIGHT_SHAPES.get(n, SMALL_SHAPES.get(n)))
        m.update(consts)
        in_maps.append(m)
    res = run_bass_kernel_spmd(nc, in_maps, core_ids=list(range(NCORES)))
    yp = np.empty((16, 4096, D), np.float32)
    ys = np.empty((8, 2048, D), np.float32)
    for i in range(NCORES):
        y = np.asarray(res.results[i]["y"]).reshape(-1, D)
        yp[2 * i] = y[0:4096]
        yp[2 * i + 1] = y[4096:8192]
        ys[i] = y[8192:10240]
    return (yp, ys)
```

```python
import numpy as np
import concourse.bass as bass
import concourse.mybir as mybir
from concourse.bass_utils import run_bass_kernel_spmd

F32 = mybir.dt.float32
BF16 = mybir.dt.bfloat16
AF = mybir.ActivationFunctionType
ALU = mybir.AluOpType

N_DMA_SEMS = 24
D = 1024
DFF = 2816
NJ = 22
EPS = 1e-6
NCORES = 8
SEQS_FULL = [4096, 4096, 2048]


class Buf:
    __slots__ = ("name", "w", "r")

    def __init__(self, name=""):
        self.name = name
        self.w = ()
        self.r = {}


class Prog:
    COMPUTE = ("pe", "act", "dve", "pool")
    ALL = ("pe", "act", "dve", "pool", "sp")

    def __init__(self, nc):
        self.nc = nc
        self.streams = {e: [] for e in self.ALL}
        self.dma_count = {e: 0 for e in self.ALL}
        self.dma_last = {}

    def _deps(self, reads, writes, join=False):
        deps = set()
        for b in reads:
            deps.update(b.w)
        for b in writes:
            if not join:
                deps.update(b.w)
            deps.update(b.r.values())
        return deps

    def op(self, eng, fn, reads=(), writes=()):
        st = self.streams[eng]
        ev = (eng, len(st))
        deps = self._deps(reads, writes)
        st.append({"fn": fn, "deps": deps, "dma": None})
        for b in reads:
            b.r[eng] = ev
        for b in writes:
            b.w = (ev,)
            b.r = {}
        return ev

    def dma(self, q, out, in_, reads=(), writes=(), join=False, **kw):
        st = self.streams[q]
        k = self.dma_count[q]
        self.dma_count[q] += 1
        slot = k % N_DMA_SEMS
        val = 16 * (k // N_DMA_SEMS + 1)
        ev = ("dma", q, slot, val)
        deps = self._deps(reads, writes, join)
        prev = self.dma_last.get((q, slot))
        if prev is not None:
            deps.add(prev)
        self.dma_last[(q, slot)] = ev
        st.append({"fn": (lambda e, out=out, in_=in_, kw=kw: e.dma_start(out=out, in_=in_, **kw)),
                   "deps": deps, "dma": ev})
        for b in reads:
            b.r[ev] = ev
        for b in writes:
            b.w = (b.w + (ev,)) if join else (ev,)
            b.r = {}
        return ev

    def barrier(self):
        deps = set()
        for e in self.COMPUTE:
            if self.streams[e]:
                j = len(self.streams[e]) - 1
                while j >= 0 and self.streams[e][j]["fn"] is None:
                    j -= 1
                if j >= 0:
                    deps.add((e, j))
        deps.update(self.dma_last.values())
        for e in self.ALL:
            self.streams[e].append({"fn": None, "deps": set(deps), "dma": None})

    def emit(self):
        nc = self.nc
        marked = {e: set() for e in self.COMPUTE}
        for e in self.ALL:
            for ent in self.streams[e]:
                for d in ent["deps"]:
                    if d[0] != "dma":
                        if d[0] == "pe" and e == "pe":
                            continue
                        marked[d[0]].add(d[1])
        rank = {}
        for e in self.COMPUTE:
            rank[e] = {idx: i + 1 for i, idx in enumerate(sorted(marked[e]))}
        from contextlib import ExitStack
        with ExitStack() as es:
            csem = {e: es.enter_context(nc.semaphore("s_" + e)) for e in self.COMPUTE}
            dsem = {}
            for q in self.ALL:
                for s in range(min(N_DMA_SEMS, self.dma_count[q])):
                    dsem[(q, s)] = es.enter_context(nc.semaphore("d_%s_%d" % (q, s)))
            block = es.enter_context(nc.Block())
            streams = self.streams
            final_dma = list(self.dma_last.values())

            def run(ename, eng):
                known = {}
                for idx, ent in enumerate(streams[ename]):
                    waits = {}
                    for d in ent["deps"]:
                        if d[0] == "dma":
                            key = ("dma", d[1], d[2])
                            sem = dsem[(d[1], d[2])]
                            val = d[3]
                        else:
                            if d[0] == "pe" and ename == "pe":
                                continue
                            key = d[0]
                            sem = csem[d[0]]
                            val = rank[d[0]][d[1]]
                        if known.get(key, 0) >= val:
                            continue
                        if key not in waits or waits[key][1] < val:
                            waits[key] = (sem, val)
                    for key, (sem, val) in waits.items():
                        eng.wait_ge(sem, val)
                        known[key] = val
                    if ent["fn"] is None:
                        continue
                    ins = ent["fn"](eng)
                    if ent["dma"] is not None:
                        d = ent["dma"]
                        ins.then_inc(dsem[(d[1], d[2])], 16)
                    elif idx in rank.get(ename, {}):
                        ins.then_inc(csem[ename], 1)
                if ename == "sp":
                    for d in final_dma:
                        key = ("dma", d[1], d[2])
                        if known.get(key, 0) < d[3]:
                            eng.wait_ge(dsem[(d[1], d[2])], d[3])

            @block.tensor
            def _(e):
                run("pe", e)

            @block.scalar
            def _(e):
                run("act", e)

            @block.vector
            def _(e):
                run("dve", e)

            @block.gpsimd
            def _(e):
                run("pool", e)

            @block.sync
            def _(e):
                run("sp", e)


class Arena:
    def __init__(self, nc, limit=229376):
        self.nc = nc
        self.off = 16640
        self.limit = limit
        self.n = 0

    def alloc(self, shape, dtype):
        nbytes = int(np.prod(shape[1:])) * (2 if dtype == BF16 else 4)
        nbytes = (nbytes + 63) // 64 * 64
        off = self.off
        assert off + nbytes <= self.limit, ("SBUF overflow", off, nbytes)
        self.off += nbytes
        self.n += 1
        return self.nc.alloc_sbuf_tensor_at("t%d" % self.n, list(shape), dtype, offset=off)

    def mark(self):
        return self.off

    def reset(self, m):
        self.off = m


WEIGHT_SHAPES = {
    "ffn_w_in": (2, 2, D, 2 * DFF), "ffn_w_out": (2, 2, DFF, D),
    "ple_w_gate": (2, D, D), "ple_w_proj": (2, 256, D),
    "gla_w_in": (1, D, 3104), "gla_w_out": (1, D, D),
    "mla_w_in": (1, D, 704), "mla_w_uq": (1, 384, 1536), "mla_w_ukv": (1, 256, 2048), "mla_w_out": (1, D, D),
}
SMALL_SHAPES = {
    "ffn_norm": (2, 2, D), "mix_norm": (2, D), "ple_norm": (2, D),
    "gla_w_gf_up": (1, 16, 512), "gla_b_gf": (1, 512), "gla_w_gb_up": (1, 16, 512), "gla_b_gb": (1, 512),
    "gla_out_norm": (1, 256), "mla_q_norm": (1, 384), "mla_kv_norm": (1, 256), "final_norm": (D,),
}


class Builder:
    def __init__(self, seqs, dbg=False, stages=99):
        self.seqs = list(seqs)
        self.NT = sum(seqs)
        self.NTILE = self.NT // 512
        self.offs = [sum(seqs[:i]) for i in range(len(seqs))]
        self.dbg = dbg
        self.stages = stages
        nc = bass.Bass("TRN2", target_bir_lowering=False)
        self.nc = nc
        self.P = Prog(nc)
        self.A = Arena(nc)
        NT = self.NT
        inp = lambda n, s, dt=F32: nc.dram_tensor(n, list(s), dt, kind="ExternalInput").ap()
        self.xin = inp("xin", (NT, D))
        self.pin = inp("pin", (2, NT, 256))
        self.win = {n: inp(n, s) for n, s in WEIGHT_SHAPES.items()}
        self.sin = {n: inp(n, s) for n, s in SMALL_SHAPES.items()}
        self.c_ident = inp("c_ident", (128, 128))
        self.c_tri = inp("c_tri", (4, 128, 128))
        self.c_mask = inp("c_mask", (2, 128, 128))
        self.c_rope = inp("c_rope", (4096, 64))
        self.y = nc.dram_tensor("y", [NT, D], F32, kind="ExternalOutput").ap()
        kind = "ExternalOutput" if dbg else "Internal"
        scr = lambda n, s, dt=BF16: nc.dram_tensor(n, list(s), dt, kind=kind).ap()
        self.wbf = {n: nc.dram_tensor("bf_" + n, list(s), BF16).ap() for n, s in WEIGHT_SHAPES.items()}
        self.QD = scr("QD", (2, 4, 128, NT)); self.KI = scr("KI", (2, 4, 128, NT))
        self.KE = scr("KE", (2, NT, 512)); self.DEC = scr("DEC", (2, 4, 128, NT // 128), F32)
        self.VG = scr("VG", (NT, D)); self.SR = scr("SR", (NT, D)); self.OG = scr("OG", (NT, D))
        self.QN = scr("QN", (8, 128, NT)); self.QR = scr("QR", (4, 128, NT)); self.KN = scr("KN", (8, 128, NT))
        self.KR2 = scr("KR2", (128, NT)); self.VM = scr("VM", (NT, D)); self.OT = scr("OT", (8, 128, NT))
        nt = self.NTILE
        self.yB = [Buf("y%d" % t) for t in range(nt)]
        self.glaB = [Buf("gla%d" % t) for t in range(nt)]
        self.glaB2 = [[Buf() for t in range(nt)] for _ in range(6)]
        self.ogB = [[Buf() for h in range(4)] for _ in range(nt)]
        self.mlaB = [[Buf() for t in range(nt)] for _ in range(5)]
        self.otB = [[Buf() for h in range(8)] for _ in range(nt)]
        self.wB = {}
        self.psF = [nc.alloc_psum_tensor("psF%d" % i, [128, 512], F32) for i in range(6)]
        self.psFB = [Buf("psF%d" % i) for i in range(6)]
        self.psT = [nc.alloc_psum_tensor("psT%d" % i, [128, 1024], BF16) for i in range(2)]
        self.psTB = [Buf("psT%d" % i) for i in range(2)]
        self.ps_i = 0
        self.pst_i = 0
        self.cast_i = 0

    def ps(self):
        i = self.ps_i % 6
        self.ps_i += 1
        return self.psF[i], self.psFB[i]

    def pst(self):
        i = self.pst_i % 2
        self.pst_i += 1
        return self.psT[i], self.psTB[i]

    def mm(self, out, lhsT, rhs, start, stop, reads, writes):
        self.P.op("pe", lambda e: e.matmul(out, lhsT=lhsT, rhs=rhs, start=start, stop=stop), reads=reads, writes=writes)

    def tp(self, out, in_, reads, writes):
        idb = self.idb
        self.P.op("pe", lambda e: e.transpose(out=out, in_=in_, identity=idb[:]), reads=list(reads) + [self.constB], writes=writes)

    def act(self, out, in_, func, reads, writes, **kw):
        self.P.op("act", lambda e: e.activation(out=out, in_=in_, func=func, **kw), reads=reads, writes=writes)

    def copy_any(self, out, in_, reads, writes):
        self.cast_i += 1
        if self.cast_i % 2 == 0:
            self.P.op("act", lambda e: e.copy(out=out, in_=in_), reads=reads, writes=writes)
        else:
            self.P.op("dve", lambda e: e.tensor_copy(out=out, in_=in_), reads=reads, writes=writes)

    def tt(self, eng, out, in0, in1, op, reads, writes):
        self.P.op(eng, lambda e: e.tensor_tensor(out=out, in0=in0, in1=in1, op=op), reads=reads, writes=writes)

    def stt(self, eng, out, in0, scalar, in1, op0, op1, reads, writes):
        self.P.op(eng, lambda e: e.scalar_tensor_tensor(out=out, in0=in0, scalar=scalar, in1=in1, op0=op0, op1=op1),
                  reads=reads, writes=writes)

    def dma(self, out, in_, reads=(), writes=(), q="sp", **kw):
        self.P.dma(q, out, in_, reads=reads, writes=writes, **kw)

    def setup_consts(self):
        A, P, nc = self.A, self.P, self.nc
        self.constB = Buf("const")
        cB = self.constB
        tmpf = nc.alloc_sbuf_tensor_at("tmpf", [128, 2560], F32, offset=170048)
        tB = Buf()
        self.idb = A.alloc([128, 128], BF16)
        self.tri = A.alloc([128, 4, 128], F32)
        self.mask = A.alloc([128, 2, 128], BF16)
        self.ones = A.alloc([128, 128], BF16)
        self.dma(tmpf[:, 0:128], self.c_ident[:, :], writes=[tB])
        self.dma(tmpf[:, 128:384].rearrange("p (a b) -> p a b", a=2), self.c_mask.rearrange("a p b -> p a b"), writes=[tB])
        self.dma(self.tri[:], self.c_tri.rearrange("a p b -> p a b"), writes=[cB])
        P.op("dve", lambda e: e.tensor_copy(out=self.idb[:], in_=tmpf[:, 0:128]), reads=[tB], writes=[cB])
        P.op("dve", lambda e: e.tensor_copy(out=self.mask[:], in_=tmpf[:, 128:384].rearrange("p (a b) -> p a b", a=2)), reads=[tB], writes=[cB])
        P.op("dve", lambda e: e.memset(self.ones[:], 1.0), writes=[cB])
        self.gfm = {}
        def fm(name, ap, n):
            t = A.alloc([128, n // 128], F32)
            self.dma(t[:], ap.rearrange("(c p) -> p c", p=128), writes=[cB], allow_slow_non_contiguous=True)
            self.gfm[name] = t
        for l in range(2):
            for a in range(2):
                fm(("ffn", l, a), self.sin["ffn_norm"][l, a], D)
            fm(("mix", l), self.sin["mix_norm"][l], D)
            fm(("ple", l), self.sin["ple_norm"][l], D)
        fm("qn", self.sin["mla_q_norm"][0], 384)
        fm("kvn", self.sin["mla_kv_norm"][0], 256)
        self.g_final = A.alloc([128, D], F32)
        self.dma(self.g_final[:], self.sin["final_norm"].partition_broadcast(128), writes=[cB])
        self.g_out = A.alloc([128, 256], F32)
        self.dma(self.g_out[:], self.sin["gla_out_norm"][0].partition_broadcast(128), writes=[cB])
        self.wup = []
        for nm, bn in (("gla_w_gf_up", "gla_b_gf"), ("gla_w_gb_up", "gla_b_gb")):
            self.dma(tmpf[0:16, 1024:1536], self.sin[nm][0], writes=[tB])
            self.dma(tmpf[16:17, 1024:1536], self.sin[bn][0:1, :], writes=[tB])
            t = A.alloc([32, 512], BF16)
            P.op("dve", lambda e, t=t: e.tensor_copy(out=t[0:17, :], in_=tmpf[0:17, 1024:1536]), reads=[tB], writes=[cB])
            self.wup.append(t)
        self.lo_aug = [A.alloc([32, 512], BF16) for _ in range(2)]
        self.loB = [Buf(), Buf()]
        for i in range(2):
            P.op("dve", lambda e, i=i: e.memset(self.lo_aug[i][:], 1.0), writes=[self.loB[i]])

    CONV_ORDER = [("ffn_w_in", 0), ("ffn_w_out", 0), ("gla_w_in", 0),
                  ("gla_w_out", 0), ("ffn_w_in", 1), ("ffn_w_out", 1), ("ple_w_proj", 0), ("ple_w_gate", 0), ("ffn_w_in", 2), ("ffn_w_out", 2),
                  ("mla_w_in", 0), ("mla_w_uq", 0), ("mla_w_ukv", 0),
                  ("mla_w_out", 0), ("ffn_w_in", 3), ("ffn_w_out", 3), ("ple_w_proj", 1), ("ple_w_gate", 1)]

    def conv_gen(self, order, CH, engs, q="sp"):
        A, P = self.A, self.P
        stf = [A.alloc([128, CH], F32) for _ in range(2)]
        stb = [A.alloc([128, CH], BF16) for _ in range(2)]
        sfB = [Buf(), Buf()]
        sbB = [Buf(), Buf()]
        k = 0
        for name, idx in order:
            self.wB[(name, idx)] = []
        for name, idx in order:
            shp = WEIGHT_SHAPES[name]
            src = self.win[name]; dst = self.wbf[name]
            if len(shp) == 4:
                src = src[idx // 2, idx % 2]; dst = dst[idx // 2, idx % 2]
            else:
                src = src[idx]; dst = dst[idx]
            R, C = shp[-2], shp[-1]
            sv = src.rearrange("(p a) c -> p (a c)", p=128)
            dv = dst.rearrange("(p a) c -> p (a c)", p=128)
            n = R * C // 128
            for c0 in range(0, n, CH):
                cn = min(CH, n - c0)
                i = k % 2
                self.dma(stf[i][:, 0:cn], sv[:, c0:c0 + cn], writes=[sfB[i]], q=q)
                eng = engs[k % len(engs)]
                if eng == "act":
                    P.op("act", lambda e, i=i, cn=cn: e.copy(out=stb[i][:, 0:cn], in_=stf[i][:, 0:cn]), reads=[sfB[i]], writes=[sbB[i]])
                else:
                    P.op(eng, lambda e, i=i, cn=cn: e.tensor_copy(out=stb[i][:, 0:cn], in_=stf[i][:, 0:cn]), reads=[sfB[i]], writes=[sbB[i]])
                b = Buf()
                self.dma(dv[:, c0:c0 + cn], stb[i][:, 0:cn], reads=[sbB[i]], writes=[b], q=q)
                self.wB[(name, idx)].append(b)
                k += 1
                yield

    def ring_init(self, nslots=6):
        self.ring_n = nslots
        self.ring_slots = [self.A.alloc([128, 4096], BF16) for _ in range(nslots)]
        self.ring_bufs = [Buf("ring%d" % i) for i in range(nslots)]
        self.ring_descs = []
        self.ring_issued = 0
        self.ring_cur = 0

    def ring_view(self, slot, shape):
        n = int(np.prod(shape[1:]))
        v = self.ring_slots[slot][:, 0:n]
        if len(shape) == 3:
            v = v.rearrange("p (a b) -> p a b", a=shape[1])
        elif len(shape) == 4:
            v = v.rearrange("p (a b c) -> p a b c", a=shape[1], b=shape[2])
        return v

    def ring_next(self):
        while self.ring_issued < len(self.ring_descs) and self.ring_issued < self.ring_cur + self.ring_n - 2:
            k = self.ring_issued
            src, shape, bl = self.ring_descs[k]
            assert int(np.prod(shape[1:])) <= 4096, shape
            rv = self.ring_view(k % self.ring_n, shape)
            if len(shape) == 4:
                for g in range(shape[2]):
                    self.dma(rv[:, :, g, :], src[:, :, g, :], reads=bl, writes=[self.ring_bufs[k % self.ring_n]], join=(g > 0))
            else:
                self.dma(rv, src, reads=bl, writes=[self.ring_bufs[k % self.ring_n]])
            self.ring_issued += 1
        k = self.ring_cur
        self.ring_cur += 1
        assert k < self.ring_issued
        return self.ring_view(k % self.ring_n, self.ring_descs[k][1]), self.ring_bufs[k % self.ring_n]

    def alloc_token_bufs(self):
        A = self.A
        self.xt = [A.alloc([128, 4, D], F32) for _ in range(2)]
        self.xB = [[Buf() for s in range(4)] for _ in range(2)]
        self.xn = A.alloc([128, 4, D], BF16)
        self.xnB = [Buf() for s in range(4)]
        self.hT = A.alloc([128, 8, 512], BF16)
        self.hTB = [Buf() for s in range(4)]
        self.aT = A.alloc([128, NJ, 512], BF16)
        self.aTB = [Buf() for j in range(NJ)]
        self.sg = [A.alloc([128, 512], F32) for _ in range(2)]
        self.sgB = [Buf(), Buf()]
        self.sg_i = 0
        self.ss = A.alloc([128, 8], F32)
        self.ssB = [Buf() for _ in range(4)]

    def rms_pre(self, s, src, srcB_s, ncols, ss=None, ssB=None, junk=None, junkB=None):
        P = self.P
        ss = self.ss if ss is None else ss
        ssB = self.ssB if ssB is None else ssB
        junk = self.xn[:, s, 0:ncols] if junk is None else junk
        junkB = self.xnB[s] if junkB is None else junkB
        P.op("pool", lambda e: e.memset(ss[:, s:s + 1], 0.0), writes=[ssB[s]])
        self.act(junk, src, AF.Square, reads=[srcB_s], writes=[junkB, ssB[s]], accum_out=ss[:, s:s + 1])
        self.act(ss[:, 4 + s:5 + s], ss[:, s:s + 1], AF.Ln, reads=[ssB[s]], writes=[ssB[s]], scale=1.0 / ncols, bias=EPS)
        self.act(ss[:, 4 + s:5 + s], ss[:, 4 + s:5 + s], AF.Exp, reads=[ssB[s]], writes=[ssB[s]], scale=-0.5)

    def norm_pre(self, s, src, srcB_s, ncols, ss=None, ssB=None, xn=None, xnB=None):
        ss = self.ss if ss is None else ss
        ssB = self.ssB if ssB is None else ssB
        xn = self.xn[:, s, 0:ncols] if xn is None else xn
        xnB = self.xnB[s] if xnB is None else xnB
        self.rms_pre(s, src, srcB_s, ncols, ss, ssB, xn, xnB)
        self.act(xn, src, AF.Copy, reads=[srcB_s, ssB[s]], writes=[xnB], scale=ss[:, 4 + s:5 + s])

    def norm_post(self, s, gain, ncols, dstT, dstB_s, xn=None, xnB=None):
        nch = ncols // 128
        xn = self.xn[:, s, 0:ncols] if xn is None else xn
        xnB = self.xnB[s] if xnB is None else xnB
        pt, ptB = self.pst()
        for c in range(nch):
            self.tp(pt[:, c * 128:(c + 1) * 128], xn[:, c * 128:(c + 1) * 128], reads=[xnB], writes=[ptB])
        self.tt("dve", dstT[:, 0:nch, s * 128:(s + 1) * 128], pt[:, 0:nch * 128].rearrange("p (c t) -> p c t", c=nch),
                gain[:].unsqueeze(2).to_broadcast([128, nch, 128]), ALU.mult, reads=[ptB, self.constB], writes=[dstB_s])

    def xnorm_pre(self, xi, s):
        self.norm_pre(s, self.xt[xi][:, s, :], self.xB[xi][s], D)

    def xnorm_post(self, gain):
        for s in range(4):
            self.norm_post(s, gain, D, self.hT, self.hTB[s])

    def ffn_descs(self, l, a):
        d = []
        idx = l * 2 + a
        win = self.wbf["ffn_w_in"][l, a].rearrange("(c p) (g f) -> p c g f", p=128, g=2)
        for j0 in range(0, NJ, 2):
            nj = 2
            d.append((win[:, :, :, j0 * 128:(j0 + nj) * 128], [128, 8, 2, nj * 128], self.wB[("ffn_w_in", idx)]))
        wout = self.wbf["ffn_w_out"][l, a].rearrange("(j p) d -> p j d", p=128)
        for half in range(2):
            for j0 in range(0, NJ, 8):
                nj = min(8, NJ - j0)
                d.append((wout[:, j0:j0 + nj, half * 512:(half + 1) * 512], [128, nj, 512], self.wB[("ffn_w_out", idx)]))
        return d

    def ffn(self, xi, l, a, after=None, mid=None):
        xt, xB = self.xt[xi], self.xB[xi]
        self.xnorm_post(self.gfm[("ffn", l, a)])
        hT, hTB = self.hT, self.hTB
        for j0 in range(0, NJ, 2):
            nj = 2
            w, wb = self.ring_next()
            for jj in range(nj):
                j = j0 + jj
                pg, pgB = self.ps()
                for c in range(8):
                    self.mm(pg[:], w[:, c, 0, jj * 128:(jj + 1) * 128], hT[:, c, :], c == 0, c == 7, reads=[wb] + hTB, writes=[pgB])
                pu, puB = self.ps()
                for c in range(8):
                    self.mm(pu[:], w[:, c, 1, jj * 128:(jj + 1) * 128], hT[:, c, :], c == 0, c == 7, reads=[wb] + hTB, writes=[puB])
                si = self.sg_i % 2
                self.sg_i += 1
                self.act(self.sg[si][:], pg[:], AF.Silu, reads=[pgB], writes=[self.sgB[si]])
                self.tt("dve", self.aT[:, j, :], self.sg[si][:], pu[:], ALU.mult, reads=[self.sgB[si], puB], writes=[self.aTB[j]])
        if mid is not None:
            mid()
        for half in range(2):
            ws = [self.ring_next() for _ in range(3)]
            for s in range(4):
                py, pyB = self.ps()
                for j in range(NJ):
                    w, wb = ws[j // 8]
                    self.mm(py[:], self.aT[:, j, s * 128:(s + 1) * 128], w[:, j % 8, :],
                            j == 0, j == NJ - 1, reads=[self.aTB[j], wb], writes=[pyB])
                xs = xt[:, s, half * 512:(half + 1) * 512]
                self.stt("dve", xs, py[:], 0.5, xs, ALU.mult, ALU.add, reads=[pyB, xB[s]], writes=[xB[s]])
                if half == 1 and after is not None:
                    after(s)

    def proj_add(self, xi, srcT, srcTB, after=None):
        xt, xB = self.xt[xi], self.xB[xi]
        for half in range(2):
            w, wb = self.ring_next()
            for s in range(4):
                py, pyB = self.ps()
                for c in range(8):
                    self.mm(py[:], srcT[:, c, s * 128:(s + 1) * 128], w[:, c, :], c == 0, c == 7,
                            reads=list(srcTB) + [wb], writes=[pyB])
                xs = xt[:, s, half * 512:(half + 1) * 512]
                self.tt("dve", xs, py[:], xs, ALU.add, reads=[pyB, xB[s]], writes=[xB[s]])
                if half == 1 and after is not None:
                    after(s)

    def ple_descs(self, l):
        wg = self.wbf["ple_w_gate"][l].rearrange("(c p) d -> p c d", p=128)
        return [(self.wbf["ple_w_proj"][l].rearrange("(c p) d -> p c d", p=128), [128, 2, D], self.wB[("ple_w_proj", l)])] + \
               [(wg[:, :, hf * 512:(hf + 1) * 512], [128, 8, 512], self.wB[("ple_w_gate", l)]) for hf in range(2)]

    def ple_load(self, l, t):
        t0 = t * 512
        self.dma(self.pt[:], self.pin[l, t0:t0 + 512, :].rearrange("(s p) d -> p s d", p=128), writes=[self.ptB])

    def ple_prep(self, l, t):
        self.P.op("act", lambda e: e.copy(out=self.ptb[:], in_=self.pt[:]), reads=[self.ptB], writes=[self.ptbB])
        for s in range(4):
            pt, ptB = self.pst()
            for c in range(2):
                self.tp(pt[:, c * 128:(c + 1) * 128], self.ptb[:, s, c * 128:(c + 1) * 128], reads=[self.ptbB], writes=[ptB])
            self.copy_any(self.pT[:, :, s * 128:(s + 1) * 128], pt[:, 0:256].rearrange("p (c t) -> p c t", c=2), reads=[ptB], writes=[self.pTB])

    def ple(self, xi, l, t, after=None):
        xt, xB = self.xt[xi], self.xB[xi]
        t0 = t * 512
        self.xnorm_post(self.gfm[("ple", l)])
        wp, wpB = self.ring_next()
        for half in range(2):
            wg, wgB = self.ring_next()
            for s in range(4):
                pg, pgB = self.ps()
                for c in range(8):
                    self.mm(pg[:], self.hT[:, c, s * 128:(s + 1) * 128], wg[:, c, :], c == 0, c == 7,
                            reads=self.hTB + [wgB], writes=[pgB])
                pp, ppB = self.ps()
                for c in range(2):
                    self.mm(pp[:], self.pT[:, c, s * 128:(s + 1) * 128], wp[:, c, half * 512:(half + 1) * 512], c == 0, c == 1,
                            reads=[self.pTB, wpB], writes=[ppB])
                si = self.sg_i % 2
                self.sg_i += 1
                self.act(self.sg[si][:], pg[:], AF.Sigmoid, reads=[pgB], writes=[self.sgB[si]])
                self.tt("dve", self.sg[si][:], self.sg[si][:], pp[:], ALU.mult, reads=[self.sgB[si], ppB], writes=[self.sgB[si]])
                xs = xt[:, s, half * 512:(half + 1) * 512]
                self.tt("pool", xs, xs, self.sg[si][:], ALU.add, reads=[self.sgB[si], xB[s]], writes=[xB[s]])
                if half == 1 and after is not None:
                    after(s)

    def final_pre(self, xi, s):
        self.rms_pre(s, self.xt[xi][:, s, :], self.xB[xi][s], D)

    def final_norm(self, xi):
        xt, xB = self.xt[xi], self.xB[xi]
        for s in range(4):
            self.stt("dve", xt[:, s, :], xt[:, s, :], self.ss[:, 4 + s:5 + s], self.g_final[:], ALU.mult, ALU.mult,
                     reads=[xB[s], self.ssB[s], self.constB], writes=[xB[s]])

    def gla_proj_descs(self):
        w = self.wbf["gla_w_in"][0].rearrange("(c p) f -> p c f", p=128)
        bl = self.wB[("gla_w_in", 0)]
        return [(w[:, :, 3072:3104], [128, 8, 32], bl)] + [(w[:, :, i * 512:(i + 1) * 512], [128, 8, 512], bl) for i in range(6)]

    def alloc_gla_proj(self):
        A = self.A
        self.sp = [A.alloc([128, 4, 512], F32) for _ in range(2)]
        self.spB = [[Buf() for s in range(4)] for _ in range(2)]
        self.etmp = [A.alloc([128, 512], F32) for _ in range(4)]
        self.etmpB = [Buf() for _ in range(4)]
        self.et_i = 0
        self.qd_st = A.alloc([128, 2, 4, 512], BF16); self.qdB = Buf()
        self.ki_st = A.alloc([128, 2, 4, 512], BF16); self.kiB = Buf()
        self.ke_st = A.alloc([128, 2, 4, 512], BF16); self.keB = Buf()
        self.dec_st = A.alloc([128, 2, 4, 4], F32); self.decB = Buf()
        self.v_st = A.alloc([128, 4, D], BF16); self.vB = Buf()
        self.sr_st = A.alloc([128, 4, D], BF16); self.srB = Buf()

    def etmp_next(self):
        i = self.et_i % 4
        self.et_i += 1
        return self.etmp[i], self.etmpB[i]

    def gla_proj(self, xi, t):
        P = self.P
        xt, xB = self.xt[xi], self.xB[xi]
        t0 = t * 512
        self.xnorm_post(self.gfm[("mix", 0)])
        hT, hTB = self.hT, self.hTB
        wlo, wloB = self.ring_next()
        for d in range(2):
            pl, plB = self.ps()
            for c in range(8):
                self.mm(pl[0:16, :], wlo[:, c, d * 16:(d + 1) * 16], hT[:, c, :], c == 0, c == 7, reads=[wloB] + hTB, writes=[plB])
            self.copy_any(self.lo_aug[d][0:16, :], pl[0:16, :], reads=[plB], writes=[self.loB[d]])
            for s in range(4):
                pz, pzB = self.ps()
                self.mm(pz[:], self.lo_aug[d][0:17, s * 128:(s + 1) * 128], self.wup[d][0:17, :], True, True,
                        reads=[self.loB[d], self.constB], writes=[pzB])
                et, etB = self.etmp_next()
                self.act(et[:], pz[:], AF.Exp, reads=[pzB], writes=[etB], scale=-1.0)
                self.act(self.sp[d][:, s, :], et[:], AF.Ln, reads=[etB], writes=[self.spB[d][s]], bias=1.0)
        wq, wqB = self.ring_next()
        wk, wkB = self.ring_next()
        for h in range(4):
            pq, pqB = self.ps()
            for c in range(8):
                self.mm(pq[:], wq[:, c, h * 128:(h + 1) * 128], hT[:, c, :], c == 0, c == 7, reads=[wqB] + hTB, writes=[pqB])
            pk, pkB = self.ps()
            for c in range(8):
                self.mm(pk[:], wk[:, c, h * 128:(h + 1) * 128], hT[:, c, :], c == 0, c == 7, reads=[wkB] + hTB, writes=[pkB])
            for d in range(2):
                pb, pbB = self.ps()
                for s in range(4):
                    self.mm(pb[:, s * 128:(s + 1) * 128], self.sp[d][:, s, h * 128:(h + 1) * 128], self.tri[:, d, :], True, True,
                            reads=[self.spB[d][s], self.constB], writes=[pbB])
                eb, ebB = self.etmp_next()
                self.act(eb[:], pb[:], AF.Exp, reads=[pbB], writes=[ebB])
                ei, eiB = self.etmp_next()
                self.act(ei[:], pb[:], AF.Exp, reads=[pbB], writes=[eiB], scale=-1.0)
                self.stt("dve", self.qd_st[:, d, h, :], pq[:], 128.0 ** -0.5, eb[:], ALU.mult, ALU.mult, reads=[pqB, ebB], writes=[self.qdB])
                self.tt("dve", self.ki_st[:, d, h, :], pk[:], ei[:], ALU.mult, reads=[pkB, eiB], writes=[self.kiB])
                col = 127 if d == 0 else 0
                ebv = eb[:].rearrange("p (s t) -> p s t", s=4)[:, :, col]
                P.op("pool", lambda e, d=d, h=h, ebv=ebv: e.tensor_copy(out=self.dec_st[:, d, h, :], in_=ebv), reads=[ebB], writes=[self.decB])
        for s in range(4):
            pk, pkB = self.ps()
            for c in range(8):
                self.mm(pk[:], hT[:, c, s * 128:(s + 1) * 128], wk[:, c, :], c == 0, c == 7, reads=[wkB] + hTB, writes=[pkB])
            for d in range(2):
                pe_, peB = self.ps()
                self.mm(pe_[:], self.tri[:, 2 + d, :], self.sp[d][:, s, :], True, True, reads=[self.spB[d][s], self.constB], writes=[peB])
                ee, eeB = self.etmp_next()
                self.act(ee[:], pe_[:], AF.Exp, reads=[peB], writes=[eeB])
                self.tt("dve", self.ke_st[:, d, s, :], pk[:], ee[:], ALU.mult, reads=[pkB, eeB], writes=[self.keB])
        for half in range(2):
            wv, wvB = self.ring_next()
            for s in range(4):
                pv, pvB = self.ps()
                for c in range(8):
                    self.mm(pv[:], hT[:, c, s * 128:(s + 1) * 128], wv[:, c, :], c == 0, c == 7, reads=[wvB] + hTB, writes=[pvB])
                self.copy_any(self.v_st[:, s, half * 512:(half + 1) * 512], pv[:], reads=[pvB], writes=[self.vB])
        for half in range(2):
            wr, wrB = self.ring_next()
            for s in range(4):
                pr, prB = self.ps()
                for c in range(8):
                    self.mm(pr[:], hT[:, c, s * 128:(s + 1) * 128], wr[:, c, :], c == 0, c == 7, reads=[wrB] + hTB, writes=[prB])
                self.act(self.sr_st[:, s, half * 512:(half + 1) * 512], pr[:], AF.Silu, reads=[prB], writes=[self.srB])
        g2 = self.glaB2
        self.dma(self.QD.rearrange("d h p t -> p (d h) t")[:, :, t0:t0 + 512], self.qd_st[:].rearrange("p d h t -> p (d h) t"),
                 reads=[self.qdB], writes=[g2[0][t]])
        self.dma(self.KI.rearrange("d h p t -> p (d h) t")[:, :, t0:t0 + 512], self.ki_st[:].rearrange("p d h t -> p (d h) t"),
                 reads=[self.kiB], writes=[g2[1][t]])
        for d in range(2):
            self.dma(self.KE[d, t0:t0 + 512, :].rearrange("(s p) f -> p s f", p=128), self.ke_st[:, d, :, :], reads=[self.keB],
                     writes=[g2[2][t]], join=(d > 0))
        self.dma(self.DEC.rearrange("d h p n -> p (d h) n")[:, :, t * 4:t * 4 + 4], self.dec_st[:].rearrange("p d h n -> p (d h) n"),
                 reads=[self.decB], writes=[g2[3][t]])
        self.dma(self.VG[t0:t0 + 512, :].rearrange("(s p) f -> p s f", p=128), self.v_st[:], reads=[self.vB], writes=[g2[4][t]])
        self.dma(self.SR[t0:t0 + 512, :].rearrange("(s p) f -> p s f", p=128), self.sr_st[:], reads=[self.srB], writes=[g2[5][t]])

    def gla_scan(self, bg=None, bg_every=5):
        A, P = self.A, self.P
        Lmax = max(self.seqs)
        NBm = Lmax // 128
        qd = [[A.alloc([128, Lmax], BF16) for _ in range(2)] for _ in range(2)]
        ki = [[A.alloc([128, Lmax], BF16) for _ in range(2)] for _ in range(2)]
        ke = [[A.alloc([128, NBm, 128], BF16) for _ in range(2)] for _ in range(2)]
        dec = [[A.alloc([128, NBm], F32) for _ in range(2)] for _ in range(2)]
        inB = [Buf("scan_in0"), Buf("scan_in1")]
        vv = A.alloc([128, NBm, 256], BF16); vvB = Buf()
        sr = A.alloc([128, NBm, 256], BF16); srB = Buf()
        oacc = A.alloc([128, NBm, 256], F32)
        oB = [Buf() for _ in range(NBm)]
        ogst = A.alloc([128, NBm, 256], BF16)
        ogB = Buf()
        S32 = [A.alloc([128, 256], F32) for _ in range(2)]
        S32B = [Buf(), Buf()]
        Sbf = [[A.alloc([128, 256], BF16) for _ in range(3)] for _ in range(2)]
        SbfB = [[Buf(), Buf(), Buf()], [Buf(), Buf(), Buf()]]
        atsb2 = [A.alloc([128, 2, 128], BF16) for _ in range(2)]
        atB = [Buf() for _ in range(2)]
        ssn = A.alloc([128, 2 * NBm], F32)
        ssB = Buf()
        junk = Sbf[0][0]
        junkB = SbfB[0][0]
        g2 = self.glaB2
        units = [(si, h) for si in range(len(self.seqs)) for h in range(4)]

        def load_big(u):
            si, h = units[u]
            L = self.seqs[si]; off = self.offs[si]; NB = L // 128
            tiles = list(range(off // 512, (off + L) // 512))
            b = u % 2
            first = True
            for d in range(2):
                self.dma(qd[b][d][:, 0:L], self.QD[d, h, :, off:off + L], reads=[g2[0][t] for t in tiles], writes=[inB[b]], join=not first)
                first = False
                self.dma(ki[b][d][:, 0:L], self.KI[d, h, :, off:off + L], reads=[g2[1][t] for t in tiles], writes=[inB[b]], join=True)
                self.dma(ke[b][d][:, 0:NB, :], self.KE[d, off:off + L, h * 128:(h + 1) * 128].rearrange("(n p) f -> p n f", p=128),
                         reads=[g2[2][t] for t in tiles], writes=[inB[b]], join=True)
                self.dma(dec[b][d][:, 0:NB], self.DEC[d, h, :, off // 128:off // 128 + NB], reads=[g2[3][t] for t in tiles], writes=[inB[b]], join=True)

        it = 0
        load_big(0)
        for u, (si, h) in enumerate(units):
            L = self.seqs[si]; off = self.offs[si]; NB = L // 128
            tiles = list(range(off // 512, (off + L) // 512))
            b = u % 2
            self.dma(vv[:, 0:NB, :], self.VG[off:off + L, h * 256:(h + 1) * 256].rearrange("(n p) f -> p n f", p=128),
                     reads=[g2[4][t] for t in tiles], writes=[vvB])
            self.dma(sr[:, 0:NB, :], self.SR[off:off + L, h * 256:(h + 1) * 256].rearrange("(n p) f -> p n f", p=128),
                     reads=[g2[5][t] for t in tiles], writes=[srB])
            if u + 1 < len(units):
                load_big(u + 1)
            touched = set()
            chunk = lambda i, d: i if d == 0 else NB - 1 - i
            pend = {}
            for i in range(NB + 1):
                if i < NB:
                    pa, paB = self.ps()
                    for d in range(2):
                        n = chunk(i, d)
                        blk = slice(n * 128, (n + 1) * 128)
                        self.mm(pa[:, d * 128:(d + 1) * 128], ki[b][d][:, blk], qd[b][d][:, blk], True, True, reads=[inB[b]], writes=[paB])
                    ai = i % 2
                    self.tt("dve", atsb2[ai][:], pa[:, 0:256].rearrange("p (d t) -> p d t", d=2), self.mask[:], ALU.mult,
                            reads=[paB, self.constB], writes=[atB[ai]])
                    if i < NB - 1:
                        pd, pdB = self.ps()
                        for d in range(2):
                            n = chunk(i, d)
                            self.mm(pd[:, d * 256:(d + 1) * 256], ke[b][d][:, n, :], vv[:, n, :], True, True, reads=[inB[b], vvB], writes=[pdB])
                        for d in range(2):
                            n = chunk(i, d)
                            pdv = pd[:, d * 256:(d + 1) * 256]
                            if i == 0:
                                P.op("dve", lambda e, d=d, pdv=pdv: e.tensor_copy(out=S32[d][:], in_=pdv), reads=[pdB], writes=[S32B[d]])
                            else:
                                self.stt("dve", S32[d][:], S32[d][:], dec[b][d][:, n:n + 1], pdv, ALU.mult, ALU.add,
                                         reads=[S32B[d], pdB, inB[b]], writes=[S32B[d]])
                            P.op("act", lambda e, d=d, i=i: e.copy(out=Sbf[d][i % 3][:], in_=S32[d][:]), reads=[S32B[d]], writes=[SbfB[d][i % 3]])
                if i >= 1:
                    j = i - 1
                    ai = j % 2
                    po, poB = self.ps()
                    for d in range(2):
                        n = chunk(j, d)
                        blk = slice(n * 128, (n + 1) * 128)
                        pov = po[:, d * 256:(d + 1) * 256]
                        self.mm(pov, atsb2[ai][:, d, :], vv[:, n, :], True, j == 0, reads=[atB[ai], vvB], writes=[poB])
                        if j > 0:
                            self.mm(pov, qd[b][d][:, blk], Sbf[d][(j - 1) % 3][:], False, True, reads=[inB[b], SbfB[d][(j - 1) % 3]], writes=[poB])
                    for d in range(2):
                        n = chunk(j, d)
                        pov = po[:, d * 256:(d + 1) * 256]
                        if n not in touched:
                            touched.add(n)
                            P.op("dve", lambda e, n=n, pov=pov: e.tensor_copy(out=oacc[:, n, :], in_=pov), reads=[poB], writes=[oB[n]])
                        else:
                            self.tt("dve", oacc[:, n, :], oacc[:, n, :], pov, ALU.add, reads=[poB, oB[n]], writes=[oB[n]])
                it += 1
                if bg is not None and it % bg_every == 0:
                    next(bg, None)
            P.op("pool", lambda e: e.memset(ssn[:], 0.0), writes=[ssB])
            for n in range(NB):
                self.act(junk[:], oacc[:, n, :], AF.Square, reads=[oB[n]], writes=[junkB, ssB], accum_out=ssn[:, n:n + 1])
            self.act(ssn[:, NBm:NBm + NB], ssn[:, 0:NB], AF.Sqrt, reads=[ssB], writes=[ssB], scale=1.0 / 256, bias=EPS)
            P.op("dve", lambda e, NB=NB: e.reciprocal(out=ssn[:, NBm:NBm + NB], in_=ssn[:, NBm:NBm + NB]), reads=[ssB], writes=[ssB])
            for n in range(NB):
                self.stt("dve", oacc[:, n, :], oacc[:, n, :], ssn[:, NBm + n:NBm + n + 1], self.g_out[:], ALU.mult, ALU.mult,
                         reads=[oB[n], ssB, self.constB], writes=[oB[n]])
                self.tt("pool", ogst[:, n, :], oacc[:, n, :], sr[:, n, :], ALU.mult, reads=[oB[n], srB], writes=[ogB])
            self.dma(self.OG[off:off + L, h * 256:(h + 1) * 256].rearrange("(n p) f -> p n f", p=128), ogst[:, 0:NB, :],
                     reads=[ogB], writes=[self.ogB[t][h] for t in tiles])
        if bg is not None:
            for _ in bg:
                pass

    def gla_out_descs(self):
        w = self.wbf["gla_w_out"][0].rearrange("(c p) d -> p c d", p=128)
        return [(w[:, :, hf * 512:(hf + 1) * 512], [128, 8, 512], self.wB[("gla_w_out", 0)]) for hf in range(2)]

    def gla_out_load(self, t):
        t0 = t * 512
        self.dma(self.ogt[t % 2][:], self.OG[t0:t0 + 512, :].rearrange("(s p) f -> p s f", p=128), reads=self.ogB[t], writes=[self.ogtB[t % 2]])

    def gla_out(self, xi, t, after=None):
        og, ogB_ = self.ogt[t % 2], self.ogtB[t % 2]
        for s in range(4):
            pt, ptB = self.pst()
            for c in range(8):
                self.tp(pt[:, c * 128:(c + 1) * 128], og[:, s, c * 128:(c + 1) * 128], reads=[ogB_], writes=[ptB])
            self.copy_any(self.hT[:, :, s * 128:(s + 1) * 128], pt[:].rearrange("p (c t) -> p c t", c=8), reads=[ptB], writes=[self.hTB[s]])
        if t + 1 < self.NTILE:
            self.gla_out_load(t + 1)
        self.proj_add(xi, self.hT, self.hTB, after)

    def mla_proj_descs(self):
        wi = self.wbf["mla_w_in"][0].rearrange("(c p) f -> p c f", p=128)
        wq = self.wbf["mla_w_uq"][0].rearrange("(c p) f -> p c f", p=128)
        return [(wi[:, :, 0:384], [128, 8, 384], self.wB[("mla_w_in", 0)]), (wi[:, :, 384:704], [128, 8, 320], self.wB[("mla_w_in", 0)]),
                (wq[:, :, 0:768], [128, 3, 768], self.wB[("mla_w_uq", 0)]), (wq[:, :, 768:1536], [128, 3, 768], self.wB[("mla_w_uq", 0)]),
                (self.wbf["mla_w_ukv"][0].rearrange("(c p) f -> p c f", p=128), [128, 2, 2048], self.wB[("mla_w_ukv", 0)])]

    def alloc_mla_proj(self):
        A = self.A
        self.cq = A.alloc([128, 4, 384], F32); self.cqB = [Buf() for _ in range(4)]
        self.ckv = A.alloc([128, 4, 256], F32); self.ckvB = [Buf() for _ in range(4)]
        self.krs = A.alloc([128, 4, 64], F32); self.krsB = Buf()
        self.cqT = A.alloc([128, 3, 512], BF16); self.cqTB = [Buf() for _ in range(4)]
        self.ckvT = A.alloc([128, 2, 512], BF16); self.ckvTB = [Buf() for _ in range(4)]
        self.cs = A.alloc([128, 4, 64], F32); self.csB = Buf()
        self.ssq = A.alloc([128, 8], F32); self.ssqB = [Buf() for _ in range(4)]
        self.sskv = A.alloc([128, 8], F32); self.sskvB = [Buf() for _ in range(4)]
        self.xnB2 = [Buf() for _ in range(4)]
        self.rt = [A.alloc([128, 8, 32], F32) for _ in range(4)]; self.rtB = [Buf() for _ in range(4)]
        self.qr_tok = A.alloc([128, 4, 512], BF16); self.qrtB = [Buf() for _ in range(4)]
        self.kr_tok = A.alloc([128, 4, 128], BF16); self.krtB = Buf()
        self.qn_st = self.aT[:, 0:8, :]
        self.qr_st = A.alloc([128, 4, 512], BF16); self.qrB = Buf()
        self.kn_st = self.aT[:, 8:16, :]
        self.kr2_st = A.alloc([128, 512], BF16); self.kr2B = Buf()
        self.vm_st = A.alloc([128, 4, D], BF16); self.vmB = Buf()

    def rope(self, x1, x2, s, nh, out1, out2, reads, writes):
        cos = self.cs[:, s, 0:32].unsqueeze(1).to_broadcast([128, nh, 32])
        sin = self.cs[:, s, 32:64].unsqueeze(1).to_broadcast([128, nh, 32])
        r = self.rt
        rB = self.rtB
        rd = list(reads) + [self.csB]
        v = lambda i: r[i][:, 0:nh, :]
        self.tt("dve", v(0), x1, cos, ALU.mult, reads=rd, writes=[rB[0]])
        self.tt("dve", v(1), x2, sin, ALU.mult, reads=rd, writes=[rB[1]])
        self.tt("dve", v(2), x2, cos, ALU.mult, reads=rd, writes=[rB[2]])
        self.tt("dve", v(3), x1, sin, ALU.mult, reads=rd, writes=[rB[3]])
        for o1, o2 in zip(out1, out2):
            self.tt("pool", o1, v(0), v(1), ALU.subtract, reads=[rB[0], rB[1]], writes=writes)
            self.tt("pool", o2, v(2), v(3), ALU.add, reads=[rB[2], rB[3]], writes=writes)

    def mla_proj_load(self, t):
        t0 = t * 512
        si = max(i for i in range(len(self.seqs)) if self.offs[i] <= t0)
        pos0 = t0 - self.offs[si]
        self.dma(self.cs[:], self.c_rope[pos0:pos0 + 512, :].rearrange("(s p) f -> p s f", p=128), writes=[self.csB])

    def mla_proj(self, xi, t):
        xt, xB = self.xt[xi], self.xB[xi]
        t0 = t * 512
        self.xnorm_post(self.gfm[("mix", 1)])
        hT, hTB = self.hT, self.hTB
        win, winB = self.ring_next()
        win2, win2B = self.ring_next()
        for s in range(4):
            p1, p1B = self.ps()
            for c in range(8):
                self.mm(p1[:, 0:384], hT[:, c, s * 128:(s + 1) * 128], win[:, c, :], c == 0, c == 7, reads=[winB] + hTB, writes=[p1B])
            self.copy_any(self.cq[:, s, :], p1[:, 0:384], reads=[p1B], writes=[self.cqB[s]])
            self.norm_pre(s, self.cq[:, s, :], self.cqB[s], 384, self.ssq, self.ssqB, self.xn[:, s, 0:384], self.xnB[s])
            p2, p2B = self.ps()
            for c in range(8):
                self.mm(p2[:, 0:320], hT[:, c, s * 128:(s + 1) * 128], win2[:, c, :], c == 0, c == 7, reads=[win2B] + hTB, writes=[p2B])
            self.P.op("dve", lambda e, s=s, p2=p2: e.tensor_copy(out=self.ckv[:, s, :], in_=p2[:, 0:256]), reads=[p2B], writes=[self.ckvB[s]])
            self.P.op("dve", lambda e, s=s, p2=p2: e.tensor_copy(out=self.krs[:, s, :], in_=p2[:, 256:320]), reads=[p2B], writes=[self.krsB])
            self.norm_pre(s, self.ckv[:, s, :], self.ckvB[s], 256, self.sskv, self.sskvB, self.xn[:, s, 512:768], self.xnB2[s])
        for s in range(4):
            self.norm_post(s, self.gfm["qn"], 384, self.cqT, self.cqTB[s], self.xn[:, s, 0:384], self.xnB[s])
            self.norm_post(s, self.gfm["kvn"], 256, self.ckvT, self.ckvTB[s], self.xn[:, s, 512:768], self.xnB2[s])
        wuqs = [self.ring_next(), self.ring_next()]
        for h in range(8):
            pq, pqB = self.ps()
            wuq, wuqB = wuqs[h // 4]
            hh = h % 4
            for c in range(3):
                self.mm(pq[:], wuq[:, c, hh * 192:hh * 192 + 128], self.cqT[:, c, :], c == 0, c == 2, reads=[wuqB] + self.cqTB, writes=[pqB])
            self.copy_any(self.qn_st[:, h, :], pq[:], reads=[pqB], writes=[self.aTB[h]])
        for s in range(4):
            pr, prB = self.ps()
            for g4 in range(2):
                wuq, wuqB = wuqs[g4]
                for c in range(3):
                    rhs = wuq[:, c, :].rearrange("p (h f) -> p h f", f=192)[:, :, 128:192]
                    self.mm(pr[:, g4 * 256:(g4 + 1) * 256].rearrange("p (h f) -> p h f", f=64), self.cqT[:, c, s * 128:(s + 1) * 128], rhs,
                            c == 0, c == 2, reads=[wuqB] + self.cqTB, writes=[prB])
            prv = pr[:].rearrange("p (h f) -> p h f", f=64)
            qv = self.qr_tok[:, s, :].rearrange("p (h f) -> p h f", f=64)
            self.rope(prv[:, :, 0:32], prv[:, :, 32:64], s, 8, [qv[:, :, 0:32]], [qv[:, :, 32:64]], reads=[prB], writes=[self.qrtB[s]])
            pt, ptB = self.pst()
            for m_ in range(4):
                self.tp(pt[:, m_ * 128:(m_ + 1) * 128], self.qr_tok[:, s, m_ * 128:(m_ + 1) * 128], reads=[self.qrtB[s]], writes=[ptB])
            self.copy_any(self.qr_st[:, :, s * 128:(s + 1) * 128], pt[:, 0:512].rearrange("p (c t) -> p c t", c=4), reads=[ptB], writes=[self.qrB])
        wkv, wkvB = self.ring_next()
        for h in range(8):
            pk, pkB = self.ps()
            for c in range(2):
                self.mm(pk[:], wkv[:, c, h * 256:h * 256 + 128], self.ckvT[:, c, :], c == 0, c == 1, reads=[wkvB] + self.ckvTB, writes=[pkB])
            self.copy_any(self.kn_st[:, h, :], pk[:], reads=[pkB], writes=[self.aTB[8 + h]])
        for s in range(4):
            for half in range(2):
                pv, pvB = self.ps()
                for c in range(2):
                    rhs = wkv[:, c, :].rearrange("p (h f) -> p h f", f=256)[:, 4 * half:4 * half + 4, 128:256]
                    self.mm(pv[:].rearrange("p (h f) -> p h f", f=128), self.ckvT[:, c, s * 128:(s + 1) * 128], rhs, c == 0, c == 1,
                            reads=[wkvB] + self.ckvTB, writes=[pvB])
                self.copy_any(self.vm_st[:, s, half * 512:(half + 1) * 512], pv[:], reads=[pvB], writes=[self.vmB])
            kv_ = self.krs[:, s, :].rearrange("p (h f) -> p h f", h=1)
            ko = self.kr_tok[:, s, :].rearrange("p (h f) -> p h f", h=1)
            self.rope(kv_[:, :, 0:32], kv_[:, :, 32:64], s, 1, [ko[:, :, 0:32], ko[:, :, 64:96]], [ko[:, :, 32:64], ko[:, :, 96:128]],
                      reads=[self.krsB], writes=[self.krtB])
            pt, ptB = self.pst()
            self.tp(pt[:, 0:128], self.kr_tok[:, s, :], reads=[self.krtB], writes=[ptB])
            self.copy_any(self.kr2_st[:, s * 128:(s + 1) * 128], pt[:, 0:128], reads=[ptB], writes=[self.kr2B])
        mb = self.mlaB
        self.dma(self.QN.rearrange("h p t -> p h t")[:, :, t0:t0 + 512], self.qn_st, reads=self.aTB[0:8], writes=[mb[0][t]])
        self.dma(self.QR.rearrange("h p t -> p h t")[:, :, t0:t0 + 512], self.qr_st[:], reads=[self.qrB], writes=[mb[1][t]])
        self.dma(self.KN.rearrange("h p t -> p h t")[:, :, t0:t0 + 512], self.kn_st, reads=self.aTB[8:16], writes=[mb[2][t]])
        self.dma(self.KR2[:, t0:t0 + 512], self.kr2_st[:], reads=[self.kr2B], writes=[mb[3][t]])
        self.dma(self.VM[t0:t0 + 512, :].rearrange("(s p) f -> p s f", p=128), self.vm_st[:], reads=[self.vmB], writes=[mb[4][t]])

    def mla_attn(self, bg=None):
        A, P = self.A, self.P
        m = A.mark()
        Lmax = max(self.seqs)
        NBm = Lmax // 128
        kr2 = [A.alloc([128, Lmax], BF16) for _ in range(2)]; kr2B = [Buf(), Buf()]
        qr = [A.alloc([128, Lmax], BF16) for _ in range(2)]; qrB = [Buf(), Buf()]
        kn = [A.alloc([128, Lmax], BF16) for _ in range(2)]; knB = [Buf(), Buf()]
        qn = [A.alloc([128, Lmax], BF16) for _ in range(2)]; qnB = [Buf(), Buf()]
        vv = [A.alloc([128, NBm, 128], BF16) for _ in range(2)]; vB = [Buf(), Buf()]
        ot = [A.alloc([128, Lmax], BF16) for _ in range(2)]; otB = [Buf(), Buf()]
        pT = [A.alloc([128, 512], BF16) for _ in range(4)]; pTB = [Buf() for _ in range(4)]
        rden = A.alloc([128, 512], F32); rdB = Buf()
        kra = A.alloc([128, Lmax], BF16); krb = A.alloc([128, Lmax], BF16); kraB = Buf(); krbB = Buf()
        accP = [A.alloc([128, 512], F32) for _ in range(2)]; accPB = [Buf(), Buf()]
        accD = [A.alloc([128, 512], F32) for _ in range(2)]; accDB = [Buf(), Buf()]
        ones32 = A.alloc([128, 128], F32); o32B = Buf()
        P.op("pool", lambda e: e.memset(ones32[:], 1.0), writes=[o32B])
        pending = []
        P.op("pool", lambda e: e.memset(kra[:], 0.0), writes=[kraB])
        P.op("pool", lambda e: e.memset(krb[:], 0.0), writes=[krbB])
        scale = 192.0 ** -0.5
        mb = self.mlaB
        hcount = 0
        pcount = 0
        pi = 0
        qt_count = 0
        for si, L in enumerate(self.seqs):
            off = self.offs[si]
            NB = L // 128
            NQ = L // 512
            tiles = list(range(off // 512, (off + L) // 512))
            ks = si % 2
            self.dma(kr2[ks][:, 0:L], self.KR2[:, off:off + L], reads=[mb[3][t] for t in tiles], writes=[kr2B[ks]])
            P.op("dve", lambda e, ks=ks, L=L: e.tensor_copy(out=kra[0:64, 0:L], in_=kr2[ks][0:64, 0:L]), reads=[kr2B[ks]], writes=[kraB])
            P.op("pool", lambda e, ks=ks, L=L: e.tensor_copy(out=krb[64:128, 0:L], in_=kr2[ks][64:128, 0:L]), reads=[kr2B[ks]], writes=[krbB])
            for h in range(8):
                hs = hcount % 2
                hcount += 1
                if h % 2 == 0:
                    ps_ = pcount % 2
                    pcount += 1
                    self.dma(qr[ps_][:, 0:L], self.QR[h // 2, :, off:off + L], reads=[mb[1][t] for t in tiles], writes=[qrB[ps_]])
                self.dma(kn[hs][:, 0:L], self.KN[h, :, off:off + L], reads=[mb[2][t] for t in tiles], writes=[knB[hs]])
                self.dma(qn[hs][:, 0:L], self.QN[h, :, off:off + L], reads=[mb[0][t] for t in tiles], writes=[qnB[hs]])
                self.dma(vv[hs][:, 0:NB, :], self.VM[off:off + L, h * 128:(h + 1) * 128].rearrange("(n p) f -> p n f", p=128),
                         reads=[mb[4][t] for t in tiles], writes=[vB[hs]])
                r0 = 64 * (h % 2)
                for qt in range(NQ):
                    qsl = slice(qt * 512, (qt + 1) * 512)
                    po, poB = self.psF[3 + qt_count % 2], self.psFB[3 + qt_count % 2]
                    aP, aPB = accP[qt_count % 2], accPB[qt_count % 2]
                    aD, aDB = accD[qt_count % 2], accDB[qt_count % 2]
                    qt_count += 1
                    pd, pdB = self.psT[0][:, 0:1024].bitcast(F32), self.psTB[0]
                    krx, krxB = (kra, kraB) if h % 2 == 0 else (krb, krbB)

                    sbank = [0, 1, 2, 5]

                    def qk(kb):
                        j = sbank[kb % 4]
                        psb, psB = self.psF[j], self.psFB[j]
                        ksl = slice(kb * 128, (kb + 1) * 128)
                        self.mm(psb[:], kn[hs][:, ksl], qn[hs][:, qsl], True, False, reads=[knB[hs], qnB[hs]], writes=[psB])
                        self.mm(psb[:], krx[:, ksl], qr[ps_][:, qsl], False, True, reads=[krxB, qrB[ps_]], writes=[psB])

                    def pv(kb):
                        nonlocal pi
                        j = sbank[kb % 4]
                        psb, psB = self.psF[j], self.psFB[j]
                        pt_, ptB_ = pT[pi % 4], pTB[pi % 4]
                        pi += 1
                        self.act(pt_[:], psb[:], AF.Exp, reads=[psB], writes=[ptB_], scale=scale)
                        self.mm(po[:], vv[hs][:, kb, :], pt_[:], kb == 0, kb == NB - 1, reads=[vB[hs], ptB_], writes=[poB])
                        eng, acc, accB = ("pool", aP, aPB) if kb % 2 == 0 else ("dve", aD, aDB)
                        if kb < 2:
                            P.op(eng, lambda e, acc=acc, pt_=pt_: e.tensor_copy(out=acc[:], in_=pt_[:]), reads=[ptB_], writes=[accB])
                        else:
                            self.tt(eng, acc[:], acc[:], pt_[:], ALU.add, reads=[ptB_, accB], writes=[accB])

                    qk(0)
                    if NB > 1:
                        qk(1)
                    for kb in range(NB):
                        if kb + 2 < NB:
                            qk(kb + 2)
                        pv(kb)
                        if kb == min(3, NB - 1) and pending:
                            pending.pop()()
                    def make_ep(aP=aP, aPB=aPB, aD=aD, aDB=aDB, po=po, poB=poB, ot_ap=ot[hs][:, qsl], otB_h=otB[hs], pd=pd, pdB=pdB,
                                last=(qt == NQ - 1), h=h, hs=hs, off=off, L=L, tiles=tiles):
                        def ep():
                            self.tt("dve", aD[:], aD[:], aP[:], ALU.add, reads=[aPB, aDB], writes=[aDB])
                            self.mm(pd, ones32[:], aD[:], True, True, reads=[o32B, aDB], writes=[pdB])
                            P.op("dve", lambda e: e.reciprocal(out=rden[:], in_=pd), reads=[pdB], writes=[rdB])
                            self.tt("dve", ot_ap, po[:], rden[:], ALU.mult, reads=[poB, rdB], writes=[otB_h])
                            if last:
                                self.dma(self.OT[h, :, off:off + L], ot[hs][:, 0:L], reads=[otB_h], writes=[self.otB[t][h] for t in tiles])
                        return ep
                    pending.append(make_ep())
                    if bg is not None and qt_count % 3 == 0:
                        next(bg, None)
        while pending:
            pending.pop()()
        if bg is not None:
            for _ in bg:
                pass
        A.reset(m)

    def mla_out_descs(self):
        w = self.wbf["mla_w_out"][0].rearrange("(c p) d -> p c d", p=128)
        return [(w[:, :, hf * 512:(hf + 1) * 512], [128, 8, 512], self.wB[("mla_w_out", 0)]) for hf in range(2)]

    def mla_out_load(self, t):
        t0 = t * 512
        self.dma(self.ott[t % 2][:], self.OT.rearrange("h p t -> p h t")[:, :, t0:t0 + 512], reads=self.otB[t], writes=[self.ottB[t % 2]])

    def mla_out(self, xi, t, after=None):
        self.proj_add(xi, self.ott[t % 2], [self.ottB[t % 2]], after)

    def token_phase(self, src, ops, descs_fn, tile_loads=(), next_loads=(), first_loads=()):
        for t in range(self.NTILE):
            self.ring_descs.extend(descs_fn())
        srcap = self.xin if src == "xin" else self.y

        def load(t):
            rd = [] if src == "xin" else [self.yB[t]]
            self.dma(self.xt[t % 2][:], srcap[t * 512:t * 512 + 512, :].rearrange("(s p) d -> p s d", p=128), reads=rd, writes=self.xB[t % 2])
        load(0)
        for f in list(next_loads) + list(first_loads):
            f(0)
        for t in range(self.NTILE):
            xi = t % 2
            t0 = t * 512
            for f in tile_loads:
                f(t)
            if t + 1 < self.NTILE:
                load(t + 1)
                for f in next_loads:
                    f(t + 1)
            for k, (pre, body, _tail) in enumerate(ops):
                if pre is not None and (k == 0 or not ops[k - 1][2]):
                    for s in range(4):
                        pre(xi, s)
                nxt = ops[k + 1][0] if k + 1 < len(ops) else None
                after = (lambda s, nxt=nxt, xi=xi: nxt(xi, s)) if (nxt is not None and ops[k][2]) else None
                body(xi, t, after)
            self.dma(self.y[t0:t0 + 512, :].rearrange("(s p) d -> p s d", p=128), self.xt[xi][:], reads=self.xB[xi], writes=[self.yB[t]])

    def build(self):
        A = self.A
        self.setup_consts()
        if self.stages == -1:
            self.P.emit(); return self.nc
        self.P.barrier()
        if self.stages == -2:
            self.P.emit(); return self.nc
        m0 = A.mark()
        for _ in self.conv_gen(self.CONV_ORDER[0:3], 4096, ["pool", "dve", "act"]):
            pass
        A.reset(m0)
        self.P.barrier()
        base = A.mark()
        if self.stages <= 0:
            self.P.emit(); return self.nc
        self.alloc_token_bufs()
        self.ring_init()
        tok_mark = A.mark()
        self.alloc_gla_proj()
        xp = self.xnorm_pre
        self.token_phase("xin", [(xp, lambda xi, t, af: self.ffn(xi, 0, 0, af), True), (xp, lambda xi, t, af: self.gla_proj(xi, t), False)],
                         lambda: self.ffn_descs(0, 0) + self.gla_proj_descs())
        if self.stages <= 1:
            self.P.emit(); return self.nc
        self.P.barrier()
        A.reset(base)
        bg = self.conv_gen(self.CONV_ORDER[3:13], 1024, ["act", "dve"], q="sp")
        next(bg)
        self.gla_scan(bg, bg_every=2)
        if self.stages <= 2:
            self.P.emit(); return self.nc
        self.P.barrier()
        A.reset(tok_mark)
        self.pt = A.alloc([128, 4, 256], F32); self.ptB = Buf()
        self.ptb = A.alloc([128, 4, 256], BF16); self.ptbB = Buf()
        self.pT = A.alloc([128, 2, 512], BF16); self.pTB = Buf()
        self.alloc_mla_proj()
        og1 = A.alloc([128, 4, D], BF16); ogb1 = Buf()
        self.ogt = [og1, og1]; self.ogtB = [ogb1, ogb1]
        xp = self.xnorm_pre
        self.token_phase("y", [(None, self.gla_out, True), (xp, lambda xi, t, af: self.ffn(xi, 0, 1, af, mid=lambda: self.ple_prep(0, t)), True),
                               (xp, lambda xi, t, af: self.ple(xi, 0, t, af), True), (xp, lambda xi, t, af: self.ffn(xi, 1, 0, af), True),
                               (xp, lambda xi, t, af: self.mla_proj(xi, t), False)],
                         lambda: self.gla_out_descs() + self.ffn_descs(0, 1) + self.ple_descs(0) + self.ffn_descs(1, 0) + self.mla_proj_descs(),
                         tile_loads=[lambda t: self.ple_load(0, t), self.mla_proj_load], first_loads=[self.gla_out_load])
        if self.stages <= 3:
            self.P.emit(); return self.nc
        self.P.barrier()
        A.reset(base)
        bg = self.conv_gen(self.CONV_ORDER[13:18], 2048, ["act"], q="sp")
        next(bg)
        self.mla_attn(bg)
        if self.stages <= 4:
            self.P.emit(); return self.nc
        self.P.barrier()
        A.reset(tok_mark)
        self.pt = A.alloc([128, 4, 256], F32); self.ptb = A.alloc([128, 4, 256], BF16); self.pT = A.alloc([128, 2, 512], BF16)
        self.ott = [A.alloc([128, 8, 512], BF16) for _ in range(2)]; self.ottB = [Buf(), Buf()]
        xp = self.xnorm_pre
        self.token_phase("y", [(None, self.mla_out, True), (xp, lambda xi, t, af: self.ffn(xi, 1, 1, af, mid=lambda: self.ple_prep(1, t)), True),
                               (xp, lambda xi, t, af: self.ple(xi, 1, t, af), True),
                               (self.final_pre, lambda xi, t, af: self.final_norm(xi), False)],
                         lambda: self.mla_out_descs() + self.ffn_descs(1, 1) + self.ple_descs(1),
                         tile_loads=[lambda t: self.ple_load(1, t)], next_loads=[self.mla_out_load])
        self.P.emit()
        return self.nc


def make_consts():
    c = {}
    c["c_ident"] = np.eye(128, dtype=np.float32)
    s = np.arange(128)[:, None]
    t = np.arange(128)[None, :]
    g = np.float32(-1.0 / 16.0)
    tri = np.zeros((4, 128, 128), np.float32)
    tri[0] = (s <= t) * g
    tri[1] = (s >= t) * g
    tri[2] = (s > t) * g
    tri[3] = (s < t) * g
    c["c_tri"] = tri
    mask = np.zeros((2, 128, 128), np.float32)
    mask[0] = (s <= t)
    mask[1] = (s > t)
    c["c_mask"] = mask
    inv_freq = (np.float32(10000.0) ** (-np.arange(0, 64, 2, dtype=np.float32) / np.float32(64))).astype(np.float32)
    ang = np.arange(4096, dtype=np.float32)[:, None] * inv_freq[None, :]
    c["c_rope"] = np.concatenate([np.cos(ang), np.sin(ang)], axis=1).astype(np.float32)
    return c


_ALL_W = list(WEIGHT_SHAPES) + list(SMALL_SHAPES)


def kernel(**inputs):
    inputs = {k: np.asarray(v) for k, v in inputs.items()}
    xp, xs = inputs["x_prompt"], inputs["x_sample"]
    pp, psm = inputs["p_prompt"], inputs["p_sample"]
    nc = Builder(SEQS_FULL).build()
    consts = make_consts()
    in_maps = []
    for i in range(NCORES):
        m = {}
        m["xin"] = np.ascontiguousarray(np.concatenate([xp[2 * i], xp[2 * i + 1], xs[i]], axis=0))
        m["pin"] = np.ascontiguousarray(np.concatenate([pp[:, 2 * i], pp[:, 2 * i + 1], psm[:, i]], axis=1))
        for n in _ALL_W:
            m[n] = np.ascontiguousarray(inputs[n]).reshape(WEIGHT_SHAPES.get(n, SMALL_SHAPES.get(n)))
        m.update(consts)
        in_maps.append(m)
    res = run_bass_kernel_spmd(nc, in_maps, core_ids=list(range(NCORES)))
    yp = np.empty((16, 4096, D), np.float32)
    ys = np.empty((8, 2048, D), np.float32)
    for i in range(NCORES):
        y = np.asarray(res.results[i]["y"]).reshape(-1, D)
        yp[2 * i] = y[0:4096]
        yp[2 * i + 1] = y[4096:8192]
        ys[i] = y[8192:10240]
    return (yp, ys)
```

```python
import numpy as np
import concourse.bass as bass
import concourse.mybir as mybir
from concourse.bass_utils import run_bass_kernel_spmd

F32 = mybir.dt.float32
BF16 = mybir.dt.bfloat16
AF = mybir.ActivationFunctionType
ALU = mybir.AluOpType

N_DMA_SEMS = 24
D = 1024
DFF = 2816
NJ = 22
EPS = 1e-6
NCORES = 8
SEQS_FULL = [4096, 4096, 2048]


class Buf:
    __slots__ = ("name", "w", "r")

    def __init__(self, name=""):
        self.name = name
        self.w = ()
        self.r = {}


class Prog:
    COMPUTE = ("pe", "act", "dve", "pool")
    ALL = ("pe", "act", "dve", "pool", "sp")

    def __init__(self, nc):
        self.nc = nc
        self.streams = {e: [] for e in self.ALL}
        self.dma_count = {e: 0 for e in self.ALL}
        self.dma_last = {}

    def _deps(self, reads, writes, join=False):
        deps = set()
        for b in reads:
            deps.update(b.w)
        for b in writes:
            if not join:
                deps.update(b.w)
            deps.update(b.r.values())
        return deps

    def op(self, eng, fn, reads=(), writes=()):
        st = self.streams[eng]
        ev = (eng, len(st))
        deps = self._deps(reads, writes)
        st.append({"fn": fn, "deps": deps, "dma": None})
        for b in reads:
            b.r[eng] = ev
        for b in writes:
            b.w = (ev,)
            b.r = {}
        return ev

    def dma(self, q, out, in_, reads=(), writes=(), join=False, **kw):
        st = self.streams[q]
        k = self.dma_count[q]
        self.dma_count[q] += 1
        slot = k % N_DMA_SEMS
        val = 16 * (k // N_DMA_SEMS + 1)
        ev = ("dma", q, slot, val)
        deps = self._deps(reads, writes, join)
        prev = self.dma_last.get((q, slot))
        if prev is not None:
            deps.add(prev)
        self.dma_last[(q, slot)] = ev
        st.append({"fn": (lambda e, out=out, in_=in_, kw=kw: e.dma_start(out=out, in_=in_, **kw)),
                   "deps": deps, "dma": ev})
        for b in reads:
            b.r[ev] = ev
        for b in writes:
            b.w = (b.w + (ev,)) if join else (ev,)
            b.r = {}
        return ev

    def barrier(self):
        deps = set()
        for e in self.COMPUTE:
            if self.streams[e]:
                j = len(self.streams[e]) - 1
                while j >= 0 and self.streams[e][j]["fn"] is None:
                    j -= 1
                if j >= 0:
                    deps.add((e, j))
        deps.update(self.dma_last.values())
        for e in self.ALL:
            self.streams[e].append({"fn": None, "deps": set(deps), "dma": None})

    def emit(self):
        nc = self.nc
        marked = {e: set() for e in self.COMPUTE}
        for e in self.ALL:
            for ent in self.streams[e]:
                for d in ent["deps"]:
                    if d[0] != "dma":
                        if d[0] == "pe" and e == "pe":
                            continue
                        marked[d[0]].add(d[1])
        rank = {}
        for e in self.COMPUTE:
            rank[e] = {idx: i + 1 for i, idx in enumerate(sorted(marked[e]))}
        from contextlib import ExitStack
        with ExitStack() as es:
            csem = {e: es.enter_context(nc.semaphore("s_" + e)) for e in self.COMPUTE}
            dsem = {}
            for q in self.ALL:
                for s in range(min(N_DMA_SEMS, self.dma_count[q])):
                    dsem[(q, s)] = es.enter_context(nc.semaphore("d_%s_%d" % (q, s)))
            block = es.enter_context(nc.Block())
            streams = self.streams
            final_dma = list(self.dma_last.values())

            def run(ename, eng):
                known = {}
                for idx, ent in enumerate(streams[ename]):
                    waits = {}
                    for d in ent["deps"]:
                        if d[0] == "dma":
                            key = ("dma", d[1], d[2])
                            sem = dsem[(d[1], d[2])]
                            val = d[3]
                        else:
                            if d[0] == "pe" and ename == "pe":
                                continue
                            key = d[0]
                            sem = csem[d[0]]
                            val = rank[d[0]][d[1]]
                        if known.get(key, 0) >= val:
                            continue
                        if key not in waits or waits[key][1] < val:
                            waits[key] = (sem, val)
                    for key, (sem, val) in waits.items():
                        eng.wait_ge(sem, val)
                        known[key] = val
                    if ent["fn"] is None:
                        continue
                    ins = ent["fn"](eng)
                    if ent["dma"] is not None:
                        d = ent["dma"]
                        ins.then_inc(dsem[(d[1], d[2])], 16)
                    elif idx in rank.get(ename, {}):
                        ins.then_inc(csem[ename], 1)
                if ename == "sp":
                    for d in final_dma:
                        key = ("dma", d[1], d[2])
                        if known.get(key, 0) < d[3]:
                            eng.wait_ge(dsem[(d[1], d[2])], d[3])

            @block.tensor
            def _(e):
                run("pe", e)

            @block.scalar
            def _(e):
                run("act", e)

            @block.vector
            def _(e):
                run("dve", e)

            @block.gpsimd
            def _(e):
                run("pool", e)

            @block.sync
            def _(e):
                run("sp", e)


class Arena:
    def __init__(self, nc, limit=229376):
        self.nc = nc
        self.off = 16640
        self.limit = limit
        self.n = 0

    def alloc(self, shape, dtype):
        nbytes = int(np.prod(shape[1:])) * (2 if dtype == BF16 else 4)
        nbytes = (nbytes + 63) // 64 * 64
        off = self.off
        assert off + nbytes <= self.limit, ("SBUF overflow", off, nbytes)
        self.off += nbytes
        self.n += 1
        return self.nc.alloc_sbuf_tensor_at("t%d" % self.n, list(shape), dtype, offset=off)

    def mark(self):
        return self.off

    def reset(self, m):
        self.off = m


WEIGHT_SHAPES = {
    "ffn_w_in": (2, 2, D, 2 * DFF), "ffn_w_out": (2, 2, DFF, D),
    "ple_w_gate": (2, D, D), "ple_w_proj": (2, 256, D),
    "gla_w_in": (1, D, 3104), "gla_w_out": (1, D, D),
    "mla_w_in": (1, D, 704), "mla_w_uq": (1, 384, 1536), "mla_w_ukv": (1, 256, 2048), "mla_w_out": (1, D, D),
}
SMALL_SHAPES = {
    "ffn_norm": (2, 2, D), "mix_norm": (2, D), "ple_norm": (2, D),
    "gla_w_gf_up": (1, 16, 512), "gla_b_gf": (1, 512), "gla_w_gb_up": (1, 16, 512), "gla_b_gb": (1, 512),
    "gla_out_norm": (1, 256), "mla_q_norm": (1, 384), "mla_kv_norm": (1, 256), "final_norm": (D,),
}


class Builder:
    def __init__(self, seqs, dbg=False, stages=99):
        self.seqs = list(seqs)
        self.NT = sum(seqs)
        self.NTILE = self.NT // 512
        self.offs = [sum(seqs[:i]) for i in range(len(seqs))]
        self.dbg = dbg
        self.stages = stages
        nc = bass.Bass("TRN2", target_bir_lowering=False)
        self.nc = nc
        self.P = Prog(nc)
        self.A = Arena(nc)
        NT = self.NT
        inp = lambda n, s, dt=F32: nc.dram_tensor(n, list(s), dt, kind="ExternalInput").ap()
        self.xin = inp("xin", (NT, D))
        self.pin = inp("pin", (2, NT, 256))
        self.win = {n: inp(n, s) for n, s in WEIGHT_SHAPES.items()}
        self.sin = {n: inp(n, s) for n, s in SMALL_SHAPES.items()}
        self.c_ident = inp("c_ident", (128, 128))
        self.c_tri = inp("c_tri", (4, 128, 128))
        self.c_mask = inp("c_mask", (2, 128, 128))
        self.c_rope = inp("c_rope", (4096, 64))
        self.y = nc.dram_tensor("y", [NT, D], F32, kind="ExternalOutput").ap()
        kind = "ExternalOutput" if dbg else "Internal"
        scr = lambda n, s, dt=BF16: nc.dram_tensor(n, list(s), dt, kind=kind).ap()
        self.wbf = {n: nc.dram_tensor("bf_" + n, list(s), BF16).ap() for n, s in WEIGHT_SHAPES.items()}
        self.QD = scr("QD", (2, 4, 128, NT)); self.KI = scr("KI", (2, 4, 128, NT))
        self.KE = scr("KE", (2, NT, 512)); self.DEC = scr("DEC", (2, 4, 128, NT // 128), F32)
        self.VG = scr("VG", (NT, D)); self.SR = scr("SR", (NT, D)); self.OG = scr("OG", (NT, D))
        self.QN = scr("QN", (8, 128, NT)); self.QR = scr("QR", (4, 128, NT)); self.KN = scr("KN", (8, 128, NT))
        self.KR2 = scr("KR2", (128, NT)); self.VM = scr("VM", (NT, D)); self.OT = scr("OT", (8, 128, NT))
        nt = self.NTILE
        self.yB = [Buf("y%d" % t) for t in range(nt)]
        self.glaB = [Buf("gla%d" % t) for t in range(nt)]
        self.glaB2 = [[Buf() for t in range(nt)] for _ in range(6)]
        self.ogB = [[Buf() for h in range(4)] for _ in range(nt)]
        self.mlaB = [[Buf() for t in range(nt)] for _ in range(5)]
        self.otB = [[Buf() for h in range(8)] for _ in range(nt)]
        self.wB = {}
        self.psF = [nc.alloc_psum_tensor("psF%d" % i, [128, 512], F32) for i in range(6)]
        self.psFB = [Buf("psF%d" % i) for i in range(6)]
        self.psT = [nc.alloc_psum_tensor("psT%d" % i, [128, 1024], BF16) for i in range(2)]
        self.psTB = [Buf("psT%d" % i) for i in range(2)]
        self.ps_i = 0
        self.pst_i = 0
        self.cast_i = 0

    def ps(self):
        i = self.ps_i % 6
        self.ps_i += 1
        return self.psF[i], self.psFB[i]

    def pst(self):
        i = self.pst_i % 2
        self.pst_i += 1
        return self.psT[i], self.psTB[i]

    def mm(self, out, lhsT, rhs, start, stop, reads, writes):
        self.P.op("pe", lambda e: e.matmul(out, lhsT=lhsT, rhs=rhs, start=start, stop=stop), reads=reads, writes=writes)

    def tp(self, out, in_, reads, writes):
        idb = self.idb
        self.P.op("pe", lambda e: e.transpose(out=out, in_=in_, identity=idb[:]), reads=list(reads) + [self.constB], writes=writes)

    def act(self, out, in_, func, reads, writes, **kw):
        self.P.op("act", lambda e: e.activation(out=out, in_=in_, func=func, **kw), reads=reads, writes=writes)

    def copy_any(self, out, in_, reads, writes):
        self.cast_i += 1
        if self.cast_i % 2 == 0:
            self.P.op("act", lambda e: e.copy(out=out, in_=in_), reads=reads, writes=writes)
        else:
            self.P.op("dve", lambda e: e.tensor_copy(out=out, in_=in_), reads=reads, writes=writes)

    def tt(self, eng, out, in0, in1, op, reads, writes):
        self.P.op(eng, lambda e: e.tensor_tensor(out=out, in0=in0, in1=in1, op=op), reads=reads, writes=writes)

    def stt(self, eng, out, in0, scalar, in1, op0, op1, reads, writes):
        self.P.op(eng, lambda e: e.scalar_tensor_tensor(out=out, in0=in0, scalar=scalar, in1=in1, op0=op0, op1=op1),
                  reads=reads, writes=writes)

    def dma(self, out, in_, reads=(), writes=(), q="sp", **kw):
        self.P.dma(q, out, in_, reads=reads, writes=writes, **kw)

    def setup_consts(self):
        A, P, nc = self.A, self.P, self.nc
        self.constB = Buf("const")
        cB = self.constB
        tmpf = nc.alloc_sbuf_tensor_at("tmpf", [128, 2560], F32, offset=170048)
        tB = Buf()
        self.idb = A.alloc([128, 128], BF16)
        self.tri = A.alloc([128, 4, 128], F32)
        self.mask = A.alloc([128, 2, 128], BF16)
        self.ones = A.alloc([128, 128], BF16)
        self.dma(tmpf[:, 0:128], self.c_ident[:, :], writes=[tB])
        self.dma(tmpf[:, 128:384].rearrange("p (a b) -> p a b", a=2), self.c_mask.rearrange("a p b -> p a b"), writes=[tB])
        self.dma(self.tri[:], self.c_tri.rearrange("a p b -> p a b"), writes=[cB])
        P.op("dve", lambda e: e.tensor_copy(out=self.idb[:], in_=tmpf[:, 0:128]), reads=[tB], writes=[cB])
        P.op("dve", lambda e: e.tensor_copy(out=self.mask[:], in_=tmpf[:, 128:384].rearrange("p (a b) -> p a b", a=2)), reads=[tB], writes=[cB])
        P.op("dve", lambda e: e.memset(self.ones[:], 1.0), writes=[cB])
        self.gfm = {}
        def fm(name, ap, n):
            t = A.alloc([128, n // 128], F32)
            self.dma(t[:], ap.rearrange("(c p) -> p c", p=128), writes=[cB], allow_slow_non_contiguous=True)
            self.gfm[name] = t
        for l in range(2):
            for a in range(2):
                fm(("ffn", l, a), self.sin["ffn_norm"][l, a], D)
            fm(("mix", l), self.sin["mix_norm"][l], D)
            fm(("ple", l), self.sin["ple_norm"][l], D)
        fm("qn", self.sin["mla_q_norm"][0], 384)
        fm("kvn", self.sin["mla_kv_norm"][0], 256)
        self.g_final = A.alloc([128, D], F32)
        self.dma(self.g_final[:], self.sin["final_norm"].partition_broadcast(128), writes=[cB])
        self.g_out = A.alloc([128, 256], F32)
        self.dma(self.g_out[:], self.sin["gla_out_norm"][0].partition_broadcast(128), writes=[cB])
        self.wup = []
        for nm, bn in (("gla_w_gf_up", "gla_b_gf"), ("gla_w_gb_up", "gla_b_gb")):
            self.dma(tmpf[0:16, 1024:1536], self.sin[nm][0], writes=[tB])
            self.dma(tmpf[16:17, 1024:1536], self.sin[bn][0:1, :], writes=[tB])
            t = A.alloc([32, 512], BF16)
            P.op("dve", lambda e, t=t: e.tensor_copy(out=t[0:17, :], in_=tmpf[0:17, 1024:1536]), reads=[tB], writes=[cB])
            self.wup.append(t)
        self.lo_aug = [A.alloc([32, 512], BF16) for _ in range(2)]
        self.loB = [Buf(), Buf()]
        for i in range(2):
            P.op("dve", lambda e, i=i: e.memset(self.lo_aug[i][:], 1.0), writes=[self.loB[i]])

    CONV_ORDER = [("ffn_w_in", 0), ("ffn_w_out", 0), ("gla_w_in", 0),
                  ("gla_w_out", 0), ("ffn_w_in", 1), ("ffn_w_out", 1), ("ple_w_proj", 0), ("ple_w_gate", 0), ("ffn_w_in", 2), ("ffn_w_out", 2),
                  ("mla_w_in", 0), ("mla_w_uq", 0), ("mla_w_ukv", 0),
                  ("mla_w_out", 0), ("ffn_w_in", 3), ("ffn_w_out", 3), ("ple_w_proj", 1), ("ple_w_gate", 1)]

    def conv_gen(self, order, CH, engs, q="sp"):
        A, P = self.A, self.P
        stf = [A.alloc([128, CH], F32) for _ in range(2)]
        stb = [A.alloc([128, CH], BF16) for _ in range(2)]
        sfB = [Buf(), Buf()]
        sbB = [Buf(), Buf()]
        for name, idx in order:
            self.wB[(name, idx)] = []
        chunks = []
        for name, idx in order:
            shp = WEIGHT_SHAPES[name]
            src = self.win[name]; dst = self.wbf[name]
            if len(shp) == 4:
                src = src[idx // 2, idx % 2]; dst = dst[idx // 2, idx % 2]
            else:
                src = src[idx]; dst = dst[idx]
            R, C = shp[-2], shp[-1]
            sv = src.rearrange("(p a) c -> p (a c)", p=128)
            dv = dst.rearrange("(p a) c -> p (a c)", p=128)
            n = R * C // 128
            for c0 in range(0, n, CH):
                cn = min(CH, n - c0)
                chunks.append((name, idx, sv[:, c0:c0 + cn], dv[:, c0:c0 + cn], cn))

        def load(k):
            name, idx, svc, dvc, cn = chunks[k]
            self.dma(stf[k % 2][:, 0:cn], svc, writes=[sfB[k % 2]], q=q)
        load(0)
        for k, (name, idx, svc, dvc, cn) in enumerate(chunks):
            i = k % 2
            eng = engs[k % len(engs)]
            if eng == "act":
                P.op("act", lambda e, i=i, cn=cn: e.copy(out=stb[i][:, 0:cn], in_=stf[i][:, 0:cn]), reads=[sfB[i]], writes=[sbB[i]])
            else:
                P.op(eng, lambda e, i=i, cn=cn: e.tensor_copy(out=stb[i][:, 0:cn], in_=stf[i][:, 0:cn]), reads=[sfB[i]], writes=[sbB[i]])
            if k + 1 < len(chunks):
                load(k + 1)
            b = Buf()
            self.dma(dvc, stb[i][:, 0:cn], reads=[sbB[i]], writes=[b], q=q)
            self.wB[(name, idx)].append(b)
            yield

    def ring_init(self, nslots=6):
        self.ring_n = nslots
        self.ring_slots = [self.A.alloc([128, 4096], BF16) for _ in range(nslots)]
        self.ring_bufs = [Buf("ring%d" % i) for i in range(nslots)]
        self.ring_descs = []
        self.ring_issued = 0
        self.ring_cur = 0

    def ring_view(self, slot, shape):
        n = int(np.prod(shape[1:]))
        v = self.ring_slots[slot][:, 0:n]
        if len(shape) == 3:
            v = v.rearrange("p (a b) -> p a b", a=shape[1])
        elif len(shape) == 4:
            v = v.rearrange("p (a b c) -> p a b c", a=shape[1], b=shape[2])
        return v

    def ring_next(self):
        while self.ring_issued < len(self.ring_descs) and self.ring_issued < self.ring_cur + self.ring_n - 2:
            k = self.ring_issued
            src, shape, bl = self.ring_descs[k]
            assert int(np.prod(shape[1:])) <= 4096, shape
            rv = self.ring_view(k % self.ring_n, shape)
            if len(shape) == 4:
                for g in range(shape[2]):
                    self.dma(rv[:, :, g, :], src[:, :, g, :], reads=bl, writes=[self.ring_bufs[k % self.ring_n]], join=(g > 0))
            else:
                self.dma(rv, src, reads=bl, writes=[self.ring_bufs[k % self.ring_n]])
            self.ring_issued += 1
        k = self.ring_cur
        self.ring_cur += 1
        assert k < self.ring_issued
        return self.ring_view(k % self.ring_n, self.ring_descs[k][1]), self.ring_bufs[k % self.ring_n]

    def alloc_token_bufs(self):
        A = self.A
        self.xt = [A.alloc([128, 4, D], F32) for _ in range(2)]
        self.xB = [[Buf() for s in range(4)] for _ in range(2)]
        self.xn = A.alloc([128, 4, D], BF16)
        self.xnB = [Buf() for s in range(4)]
        self.hT = A.alloc([128, 8, 512], BF16)
        self.hTB = [Buf() for s in range(4)]
        self.aT = A.alloc([128, NJ, 512], BF16)
        self.aTB = [Buf() for j in range(NJ)]
        self.sg = [A.alloc([128, 512], F32) for _ in range(2)]
        self.sgB = [Buf(), Buf()]
        self.sg_i = 0
        self.ss = A.alloc([128, 8], F32)
        self.ssB = [Buf() for _ in range(4)]

    def rms_pre(self, s, src, srcB_s, ncols, ss=None, ssB=None, junk=None, junkB=None):
        P = self.P
        ss = self.ss if ss is None else ss
        ssB = self.ssB if ssB is None else ssB
        junk = self.xn[:, s, 0:ncols] if junk is None else junk
        junkB = self.xnB[s] if junkB is None else junkB
        P.op("pool", lambda e: e.memset(ss[:, s:s + 1], 0.0), writes=[ssB[s]])
        self.act(junk, src, AF.Square, reads=[srcB_s], writes=[junkB, ssB[s]], accum_out=ss[:, s:s + 1])
        self.act(ss[:, 4 + s:5 + s], ss[:, s:s + 1], AF.Ln, reads=[ssB[s]], writes=[ssB[s]], scale=1.0 / ncols, bias=EPS)
        self.act(ss[:, 4 + s:5 + s], ss[:, 4 + s:5 + s], AF.Exp, reads=[ssB[s]], writes=[ssB[s]], scale=-0.5)

    def norm_pre(self, s, src, srcB_s, ncols, ss=None, ssB=None, xn=None, xnB=None):
        ss = self.ss if ss is None else ss
        ssB = self.ssB if ssB is None else ssB
        xn = self.xn[:, s, 0:ncols] if xn is None else xn
        xnB = self.xnB[s] if xnB is None else xnB
        self.rms_pre(s, src, srcB_s, ncols, ss, ssB, xn, xnB)
        self.act(xn, src, AF.Copy, reads=[srcB_s, ssB[s]], writes=[xnB], scale=ss[:, 4 + s:5 + s])

    def norm_post(self, s, gain, ncols, dstT, dstB_s, xn=None, xnB=None):
        nch = ncols // 128
        xn = self.xn[:, s, 0:ncols] if xn is None else xn
        xnB = self.xnB[s] if xnB is None else xnB
        pt, ptB = self.pst()
        for c in range(nch):
            self.tp(pt[:, c * 128:(c + 1) * 128], xn[:, c * 128:(c + 1) * 128], reads=[xnB], writes=[ptB])
        self.tt("dve", dstT[:, 0:nch, s * 128:(s + 1) * 128], pt[:, 0:nch * 128].rearrange("p (c t) -> p c t", c=nch),
                gain[:].unsqueeze(2).to_broadcast([128, nch, 128]), ALU.mult, reads=[ptB, self.constB], writes=[dstB_s])

    def xnorm_pre(self, xi, s):
        self.norm_pre(s, self.xt[xi][:, s, :], self.xB[xi][s], D)

    def xnorm_post(self, gain):
        for s in range(4):
            self.norm_post(s, gain, D, self.hT, self.hTB[s])

    def ffn_descs(self, l, a):
        d = []
        idx = l * 2 + a
        win = self.wbf["ffn_w_in"][l, a].rearrange("(c p) (g f) -> p c g f", p=128, g=2)
        for j0 in range(0, NJ, 2):
            nj = 2
            d.append((win[:, :, :, j0 * 128:(j0 + nj) * 128], [128, 8, 2, nj * 128], self.wB[("ffn_w_in", idx)]))
        wout = self.wbf["ffn_w_out"][l, a].rearrange("(j p) d -> p j d", p=128)
        for half in range(2):
            for j0 in range(0, NJ, 8):
                nj = min(8, NJ - j0)
                d.append((wout[:, j0:j0 + nj, half * 512:(half + 1) * 512], [128, nj, 512], self.wB[("ffn_w_out", idx)]))
        return d

    def ffn(self, xi, l, a, after=None, mid=None):
        xt, xB = self.xt[xi], self.xB[xi]
        self.xnorm_post(self.gfm[("ffn", l, a)])
        hT, hTB = self.hT, self.hTB
        for j0 in range(0, NJ, 2):
            nj = 2
            w, wb = self.ring_next()
            for jj in range(nj):
                j = j0 + jj
                pg, pgB = self.ps()
                for c in range(8):
                    self.mm(pg[:], w[:, c, 0, jj * 128:(jj + 1) * 128], hT[:, c, :], c == 0, c == 7, reads=[wb] + hTB, writes=[pgB])
                pu, puB = self.ps()
                for c in range(8):
                    self.mm(pu[:], w[:, c, 1, jj * 128:(jj + 1) * 128], hT[:, c, :], c == 0, c == 7, reads=[wb] + hTB, writes=[puB])
                si = self.sg_i % 2
                self.sg_i += 1
                self.act(self.sg[si][:], pg[:], AF.Silu, reads=[pgB], writes=[self.sgB[si]])
                self.tt("dve", self.aT[:, j, :], self.sg[si][:], pu[:], ALU.mult, reads=[self.sgB[si], puB], writes=[self.aTB[j]])
        if mid is not None:
            mid()
        for half in range(2):
            ws = [self.ring_next() for _ in range(3)]
            for s in range(4):
                py, pyB = self.ps()
                for j in range(NJ):
                    w, wb = ws[j // 8]
                    self.mm(py[:], self.aT[:, j, s * 128:(s + 1) * 128], w[:, j % 8, :],
                            j == 0, j == NJ - 1, reads=[self.aTB[j], wb], writes=[pyB])
                xs = xt[:, s, half * 512:(half + 1) * 512]
                self.stt("dve", xs, py[:], 0.5, xs, ALU.mult, ALU.add, reads=[pyB, xB[s]], writes=[xB[s]])
                if half == 1 and after is not None:
                    after(s)

    def proj_add(self, xi, srcT, srcTB, after=None):
        xt, xB = self.xt[xi], self.xB[xi]
        for half in range(2):
            w, wb = self.ring_next()
            for s in range(4):
                py, pyB = self.ps()
                for c in range(8):
                    self.mm(py[:], srcT[:, c, s * 128:(s + 1) * 128], w[:, c, :], c == 0, c == 7,
                            reads=list(srcTB) + [wb], writes=[pyB])
                xs = xt[:, s, half * 512:(half + 1) * 512]
                self.tt("dve", xs, py[:], xs, ALU.add, reads=[pyB, xB[s]], writes=[xB[s]])
                if half == 1 and after is not None:
                    after(s)

    def ple_descs(self, l):
        wg = self.wbf["ple_w_gate"][l].rearrange("(c p) d -> p c d", p=128)
        return [(self.wbf["ple_w_proj"][l].rearrange("(c p) d -> p c d", p=128), [128, 2, D], self.wB[("ple_w_proj", l)])] + \
               [(wg[:, :, hf * 512:(hf + 1) * 512], [128, 8, 512], self.wB[("ple_w_gate", l)]) for hf in range(2)]

    def ple_load(self, l, t):
        t0 = t * 512
        self.dma(self.pt[:], self.pin[l, t0:t0 + 512, :].rearrange("(s p) d -> p s d", p=128), writes=[self.ptB])

    def ple_prep(self, l, t):
        self.P.op("act", lambda e: e.copy(out=self.ptb[:], in_=self.pt[:]), reads=[self.ptB], writes=[self.ptbB])
        for s in range(4):
            pt, ptB = self.pst()
            for c in range(2):
                self.tp(pt[:, c * 128:(c + 1) * 128], self.ptb[:, s, c * 128:(c + 1) * 128], reads=[self.ptbB], writes=[ptB])
            self.copy_any(self.pT[:, :, s * 128:(s + 1) * 128], pt[:, 0:256].rearrange("p (c t) -> p c t", c=2), reads=[ptB], writes=[self.pTB])

    def ple(self, xi, l, t, after=None):
        xt, xB = self.xt[xi], self.xB[xi]
        t0 = t * 512
        self.xnorm_post(self.gfm[("ple", l)])
        wp, wpB = self.ring_next()
        for half in range(2):
            wg, wgB = self.ring_next()
            for s in range(4):
                pg, pgB = self.ps()
                for c in range(8):
                    self.mm(pg[:], self.hT[:, c, s * 128:(s + 1) * 128], wg[:, c, :], c == 0, c == 7,
                            reads=self.hTB + [wgB], writes=[pgB])
                pp, ppB = self.ps()
                for c in range(2):
                    self.mm(pp[:], self.pT[:, c, s * 128:(s + 1) * 128], wp[:, c, half * 512:(half + 1) * 512], c == 0, c == 1,
                            reads=[self.pTB, wpB], writes=[ppB])
                si = self.sg_i % 2
                self.sg_i += 1
                self.act(self.sg[si][:], pg[:], AF.Sigmoid, reads=[pgB], writes=[self.sgB[si]])
                self.tt("dve", self.sg[si][:], self.sg[si][:], pp[:], ALU.mult, reads=[self.sgB[si], ppB], writes=[self.sgB[si]])
                xs = xt[:, s, half * 512:(half + 1) * 512]
                self.tt("pool", xs, xs, self.sg[si][:], ALU.add, reads=[self.sgB[si], xB[s]], writes=[xB[s]])
                if half == 1 and after is not None:
                    after(s)

    def final_pre(self, xi, s):
        self.rms_pre(s, self.xt[xi][:, s, :], self.xB[xi][s], D)

    def final_norm(self, xi):
        xt, xB = self.xt[xi], self.xB[xi]
        for s in range(4):
            self.stt("dve", xt[:, s, :], xt[:, s, :], self.ss[:, 4 + s:5 + s], self.g_final[:], ALU.mult, ALU.mult,
                     reads=[xB[s], self.ssB[s], self.constB], writes=[xB[s]])

    def gla_proj_descs(self):
        w = self.wbf["gla_w_in"][0].rearrange("(c p) f -> p c f", p=128)
        bl = self.wB[("gla_w_in", 0)]
        return [(w[:, :, 3072:3104], [128, 8, 32], bl)] + [(w[:, :, i * 512:(i + 1) * 512], [128, 8, 512], bl) for i in range(6)]

    def alloc_gla_proj(self):
        A = self.A
        self.sp = [A.alloc([128, 4, 512], F32) for _ in range(2)]
        self.spB = [[Buf() for s in range(4)] for _ in range(2)]
        self.etmp = [A.alloc([128, 512], F32) for _ in range(4)]
        self.etmpB = [Buf() for _ in range(4)]
        self.et_i = 0
        self.qd_st = A.alloc([128, 2, 4, 512], BF16); self.qdB = Buf()
        self.ki_st = A.alloc([128, 2, 4, 512], BF16); self.kiB = Buf()
        self.ke_st = A.alloc([128, 2, 4, 512], BF16); self.keB = Buf()
        self.dec_st = A.alloc([128, 2, 4, 4], F32); self.decB = Buf()
        self.v_st = A.alloc([128, 4, D], BF16); self.vB = Buf()
        self.sr_st = A.alloc([128, 4, D], BF16); self.srB = Buf()

    def etmp_next(self):
        i = self.et_i % 4
        self.et_i += 1
        return self.etmp[i], self.etmpB[i]

    def gla_proj(self, xi, t):
        P = self.P
        xt, xB = self.xt[xi], self.xB[xi]
        t0 = t * 512
        self.xnorm_post(self.gfm[("mix", 0)])
        hT, hTB = self.hT, self.hTB
        wlo, wloB = self.ring_next()
        for d in range(2):
            pl, plB = self.ps()
            for c in range(8):
                self.mm(pl[0:16, :], wlo[:, c, d * 16:(d + 1) * 16], hT[:, c, :], c == 0, c == 7, reads=[wloB] + hTB, writes=[plB])
            self.copy_any(self.lo_aug[d][0:16, :], pl[0:16, :], reads=[plB], writes=[self.loB[d]])
            for s in range(4):
                pz, pzB = self.ps()
                self.mm(pz[:], self.lo_aug[d][0:17, s * 128:(s + 1) * 128], self.wup[d][0:17, :], True, True,
                        reads=[self.loB[d], self.constB], writes=[pzB])
                et, etB = self.etmp_next()
                self.act(et[:], pz[:], AF.Exp, reads=[pzB], writes=[etB], scale=-1.0)
                self.act(self.sp[d][:, s, :], et[:], AF.Ln, reads=[etB], writes=[self.spB[d][s]], bias=1.0)
        wq, wqB = self.ring_next()
        wk, wkB = self.ring_next()
        for h in range(4):
            pq, pqB = self.ps()
            for c in range(8):
                self.mm(pq[:], wq[:, c, h * 128:(h + 1) * 128], hT[:, c, :], c == 0, c == 7, reads=[wqB] + hTB, writes=[pqB])
            pk, pkB = self.ps()
            for c in range(8):
                self.mm(pk[:], wk[:, c, h * 128:(h + 1) * 128], hT[:, c, :], c == 0, c == 7, reads=[wkB] + hTB, writes=[pkB])
            for d in range(2):
                pb, pbB = self.ps()
                for s in range(4):
                    self.mm(pb[:, s * 128:(s + 1) * 128], self.sp[d][:, s, h * 128:(h + 1) * 128], self.tri[:, d, :], True, True,
                            reads=[self.spB[d][s], self.constB], writes=[pbB])
                eb, ebB = self.etmp_next()
                self.act(eb[:], pb[:], AF.Exp, reads=[pbB], writes=[ebB])
                ei, eiB = self.etmp_next()
                self.act(ei[:], pb[:], AF.Exp, reads=[pbB], writes=[eiB], scale=-1.0)
                self.stt("dve", self.qd_st[:, d, h, :], pq[:], 128.0 ** -0.5, eb[:], ALU.mult, ALU.mult, reads=[pqB, ebB], writes=[self.qdB])
                self.tt("dve", self.ki_st[:, d, h, :], pk[:], ei[:], ALU.mult, reads=[pkB, eiB], writes=[self.kiB])
                col = 127 if d == 0 else 0
                ebv = eb[:].rearrange("p (s t) -> p s t", s=4)[:, :, col]
                P.op("pool", lambda e, d=d, h=h, ebv=ebv: e.tensor_copy(out=self.dec_st[:, d, h, :], in_=ebv), reads=[ebB], writes=[self.decB])
        for s in range(4):
            pk, pkB = self.ps()
            for c in range(8):
                self.mm(pk[:], hT[:, c, s * 128:(s + 1) * 128], wk[:, c, :], c == 0, c == 7, reads=[wkB] + hTB, writes=[pkB])
            for d in range(2):
                pe_, peB = self.ps()
                self.mm(pe_[:], self.tri[:, 2 + d, :], self.sp[d][:, s, :], True, True, reads=[self.spB[d][s], self.constB], writes=[peB])
                ee, eeB = self.etmp_next()
                self.act(ee[:], pe_[:], AF.Exp, reads=[peB], writes=[eeB])
                self.tt("dve", self.ke_st[:, d, s, :], pk[:], ee[:], ALU.mult, reads=[pkB, eeB], writes=[self.keB])
        for half in range(2):
            wv, wvB = self.ring_next()
            for s in range(4):
                pv, pvB = self.ps()
                for c in range(8):
                    self.mm(pv[:], hT[:, c, s * 128:(s + 1) * 128], wv[:, c, :], c == 0, c == 7, reads=[wvB] + hTB, writes=[pvB])
                self.copy_any(self.v_st[:, s, half * 512:(half + 1) * 512], pv[:], reads=[pvB], writes=[self.vB])
        for half in range(2):
            wr, wrB = self.ring_next()
            for s in range(4):
                pr, prB = self.ps()
                for c in range(8):
                    self.mm(pr[:], hT[:, c, s * 128:(s + 1) * 128], wr[:, c, :], c == 0, c == 7, reads=[wrB] + hTB, writes=[prB])
                self.act(self.sr_st[:, s, half * 512:(half + 1) * 512], pr[:], AF.Silu, reads=[prB], writes=[self.srB])
        g2 = self.glaB2
        self.dma(self.QD.rearrange("d h p t -> p (d h) t")[:, :, t0:t0 + 512], self.qd_st[:].rearrange("p d h t -> p (d h) t"),
                 reads=[self.qdB], writes=[g2[0][t]])
        self.dma(self.KI.rearrange("d h p t -> p (d h) t")[:, :, t0:t0 + 512], self.ki_st[:].rearrange("p d h t -> p (d h) t"),
                 reads=[self.kiB], writes=[g2[1][t]])
        for d in range(2):
            self.dma(self.KE[d, t0:t0 + 512, :].rearrange("(s p) f -> p s f", p=128), self.ke_st[:, d, :, :], reads=[self.keB],
                     writes=[g2[2][t]], join=(d > 0))
        self.dma(self.DEC.rearrange("d h p n -> p (d h) n")[:, :, t * 4:t * 4 + 4], self.dec_st[:].rearrange("p d h n -> p (d h) n"),
                 reads=[self.decB], writes=[g2[3][t]])
        self.dma(self.VG[t0:t0 + 512, :].rearrange("(s p) f -> p s f", p=128), self.v_st[:], reads=[self.vB], writes=[g2[4][t]])
        self.dma(self.SR[t0:t0 + 512, :].rearrange("(s p) f -> p s f", p=128), self.sr_st[:], reads=[self.srB], writes=[g2[5][t]])

    def gla_scan(self, bg=None, bg_every=5):
        A, P = self.A, self.P
        Lmax = max(self.seqs)
        NBm = Lmax // 128
        qd = [[A.alloc([128, Lmax], BF16) for _ in range(2)] for _ in range(2)]
        ki = [[A.alloc([128, Lmax], BF16) for _ in range(2)] for _ in range(2)]
        ke = [[A.alloc([128, NBm, 128], BF16) for _ in range(2)] for _ in range(2)]
        dec = [[A.alloc([128, NBm], F32) for _ in range(2)] for _ in range(2)]
        inB = [Buf("scan_in0"), Buf("scan_in1")]
        vv = A.alloc([128, NBm, 256], BF16); vvB = Buf()
        sr = A.alloc([128, NBm, 256], BF16); srB = Buf()
        oacc = A.alloc([128, NBm, 256], F32)
        oB = [Buf() for _ in range(NBm)]
        ogst = A.alloc([128, NBm, 256], BF16)
        ogB = Buf()
        S32 = [A.alloc([128, 256], F32) for _ in range(2)]
        S32B = [Buf(), Buf()]
        Sbf = [[A.alloc([128, 256], BF16) for _ in range(3)] for _ in range(2)]
        SbfB = [[Buf(), Buf(), Buf()], [Buf(), Buf(), Buf()]]
        atsb2 = [A.alloc([128, 2, 128], BF16) for _ in range(2)]
        atB = [Buf() for _ in range(2)]
        ssn = A.alloc([128, 2 * NBm], F32)
        ssB = Buf()
        junk = Sbf[0][0]
        junkB = SbfB[0][0]
        g2 = self.glaB2
        units = [(si, h) for si in range(len(self.seqs)) for h in range(4)]

        def load_big(u):
            si, h = units[u]
            L = self.seqs[si]; off = self.offs[si]; NB = L // 128
            tiles = list(range(off // 512, (off + L) // 512))
            b = u % 2
            first = True
            for d in range(2):
                self.dma(qd[b][d][:, 0:L], self.QD[d, h, :, off:off + L], reads=[g2[0][t] for t in tiles], writes=[inB[b]], join=not first)
                first = False
                self.dma(ki[b][d][:, 0:L], self.KI[d, h, :, off:off + L], reads=[g2[1][t] for t in tiles], writes=[inB[b]], join=True)
                self.dma(ke[b][d][:, 0:NB, :], self.KE[d, off:off + L, h * 128:(h + 1) * 128].rearrange("(n p) f -> p n f", p=128),
                         reads=[g2[2][t] for t in tiles], writes=[inB[b]], join=True)
                self.dma(dec[b][d][:, 0:NB], self.DEC[d, h, :, off // 128:off // 128 + NB], reads=[g2[3][t] for t in tiles], writes=[inB[b]], join=True)

        def load_vv(u):
            si_, h_ = units[u]
            L_ = self.seqs[si_]; off_ = self.offs[si_]; NB_ = L_ // 128
            tiles_ = list(range(off_ // 512, (off_ + L_) // 512))
            self.dma(vv[:, 0:NB_, :], self.VG[off_:off_ + L_, h_ * 256:(h_ + 1) * 256].rearrange("(n p) f -> p n f", p=128),
                     reads=[g2[4][t] for t in tiles_], writes=[vvB])

        it = 0
        load_big(0)
        for u, (si, h) in enumerate(units):
            L = self.seqs[si]; off = self.offs[si]; NB = L // 128
            tiles = list(range(off // 512, (off + L) // 512))
            b = u % 2
            load_vv(u)
            self.dma(sr[:, 0:NB, :], self.SR[off:off + L, h * 256:(h + 1) * 256].rearrange("(n p) f -> p n f", p=128),
                     reads=[g2[5][t] for t in tiles], writes=[srB])
            if u + 1 < len(units):
                load_big(u + 1)
            touched = set()
            chunk = lambda i, d: i if d == 0 else NB - 1 - i
            pend = {}
            for i in range(NB + 1):
                if i < NB:
                    pa, paB = self.ps()
                    for d in range(2):
                        n = chunk(i, d)
                        blk = slice(n * 128, (n + 1) * 128)
                        self.mm(pa[:, d * 128:(d + 1) * 128], ki[b][d][:, blk], qd[b][d][:, blk], True, True, reads=[inB[b]], writes=[paB])
                    ai = i % 2
                    self.tt("dve", atsb2[ai][:], pa[:, 0:256].rearrange("p (d t) -> p d t", d=2), self.mask[:], ALU.mult,
                            reads=[paB, self.constB], writes=[atB[ai]])
                    if i < NB - 1:
                        pd, pdB = self.ps()
                        for d in range(2):
                            n = chunk(i, d)
                            self.mm(pd[:, d * 256:(d + 1) * 256], ke[b][d][:, n, :], vv[:, n, :], True, True, reads=[inB[b], vvB], writes=[pdB])
                        for d in range(2):
                            n = chunk(i, d)
                            pdv = pd[:, d * 256:(d + 1) * 256]
                            if i == 0:
                                P.op("dve", lambda e, d=d, pdv=pdv: e.tensor_copy(out=S32[d][:], in_=pdv), reads=[pdB], writes=[S32B[d]])
                            else:
                                self.stt("dve", S32[d][:], S32[d][:], dec[b][d][:, n:n + 1], pdv, ALU.mult, ALU.add,
                                         reads=[S32B[d], pdB, inB[b]], writes=[S32B[d]])
                            P.op("act", lambda e, d=d, i=i: e.copy(out=Sbf[d][i % 3][:], in_=S32[d][:]), reads=[S32B[d]], writes=[SbfB[d][i % 3]])
                if i >= 1:
                    j = i - 1
                    ai = j % 2
                    po, poB = self.ps()
                    for d in range(2):
                        n = chunk(j, d)
                        blk = slice(n * 128, (n + 1) * 128)
                        pov = po[:, d * 256:(d + 1) * 256]
                        self.mm(pov, atsb2[ai][:, d, :], vv[:, n, :], True, j == 0, reads=[atB[ai], vvB], writes=[poB])
                        if j > 0:
                            self.mm(pov, qd[b][d][:, blk], Sbf[d][(j - 1) % 3][:], False, True, reads=[inB[b], SbfB[d][(j - 1) % 3]], writes=[poB])
                    for d in range(2):
                        n = chunk(j, d)
                        pov = po[:, d * 256:(d + 1) * 256]
                        if n not in touched:
                            touched.add(n)
                            P.op("dve", lambda e, n=n, pov=pov: e.tensor_copy(out=oacc[:, n, :], in_=pov), reads=[poB], writes=[oB[n]])
                        else:
                            self.tt("dve", oacc[:, n, :], oacc[:, n, :], pov, ALU.add, reads=[poB, oB[n]], writes=[oB[n]])
                it += 1
                if bg is not None and it % bg_every == 0:
                    next(bg, None)
            P.op("pool", lambda e: e.memset(ssn[:], 0.0), writes=[ssB])
            for n in range(NB):
                self.act(junk[:], oacc[:, n, :], AF.Square, reads=[oB[n]], writes=[junkB, ssB], accum_out=ssn[:, n:n + 1])
            self.act(ssn[:, NBm:NBm + NB], ssn[:, 0:NB], AF.Sqrt, reads=[ssB], writes=[ssB], scale=1.0 / 256, bias=EPS)
            P.op("dve", lambda e, NB=NB: e.reciprocal(out=ssn[:, NBm:NBm + NB], in_=ssn[:, NBm:NBm + NB]), reads=[ssB], writes=[ssB])
            for n in range(NB):
                self.stt("dve", oacc[:, n, :], oacc[:, n, :], ssn[:, NBm + n:NBm + n + 1], self.g_out[:], ALU.mult, ALU.mult,
                         reads=[oB[n], ssB, self.constB], writes=[oB[n]])
                self.tt("pool", ogst[:, n, :], oacc[:, n, :], sr[:, n, :], ALU.mult, reads=[oB[n], srB], writes=[ogB])
            self.dma(self.OG[off:off + L, h * 256:(h + 1) * 256].rearrange("(n p) f -> p n f", p=128), ogst[:, 0:NB, :],
                     reads=[ogB], writes=[self.ogB[t][h] for t in tiles])
        if bg is not None:
            for _ in bg:
                pass

    def gla_out_descs(self):
        w = self.wbf["gla_w_out"][0].rearrange("(c p) d -> p c d", p=128)
        return [(w[:, :, hf * 512:(hf + 1) * 512], [128, 8, 512], self.wB[("gla_w_out", 0)]) for hf in range(2)]

    def gla_out_load(self, t):
        t0 = t * 512
        self.dma(self.ogt[t % 2][:], self.OG[t0:t0 + 512, :].rearrange("(s p) f -> p s f", p=128), reads=self.ogB[t], writes=[self.ogtB[t % 2]])

    def gla_out(self, xi, t, after=None):
        og, ogB_ = self.ogt[t % 2], self.ogtB[t % 2]
        for s in range(4):
            pt, ptB = self.pst()
            for c in range(8):
                self.tp(pt[:, c * 128:(c + 1) * 128], og[:, s, c * 128:(c + 1) * 128], reads=[ogB_], writes=[ptB])
            self.copy_any(self.hT[:, :, s * 128:(s + 1) * 128], pt[:].rearrange("p (c t) -> p c t", c=8), reads=[ptB], writes=[self.hTB[s]])
        if t + 1 < self.NTILE:
            self.gla_out_load(t + 1)
        self.proj_add(xi, self.hT, self.hTB, after)

    def mla_proj_descs(self):
        wi = self.wbf["mla_w_in"][0].rearrange("(c p) f -> p c f", p=128)
        wq = self.wbf["mla_w_uq"][0].rearrange("(c p) f -> p c f", p=128)
        return [(wi[:, :, 0:384], [128, 8, 384], self.wB[("mla_w_in", 0)]), (wi[:, :, 384:704], [128, 8, 320], self.wB[("mla_w_in", 0)]),
                (wq[:, :, 0:768], [128, 3, 768], self.wB[("mla_w_uq", 0)]), (wq[:, :, 768:1536], [128, 3, 768], self.wB[("mla_w_uq", 0)]),
                (self.wbf["mla_w_ukv"][0].rearrange("(c p) f -> p c f", p=128), [128, 2, 2048], self.wB[("mla_w_ukv", 0)])]

    def alloc_mla_proj(self):
        A = self.A
        self.cq = A.alloc([128, 4, 384], F32); self.cqB = [Buf() for _ in range(4)]
        self.ckv = A.alloc([128, 4, 256], F32); self.ckvB = [Buf() for _ in range(4)]
        self.krs = A.alloc([128, 4, 64], F32); self.krsB = Buf()
        self.cqT = A.alloc([128, 3, 512], BF16); self.cqTB = [Buf() for _ in range(4)]
        self.ckvT = A.alloc([128, 2, 512], BF16); self.ckvTB = [Buf() for _ in range(4)]
        self.cs = A.alloc([128, 4, 64], F32); self.csB = Buf()
        self.ssq = A.alloc([128, 8], F32); self.ssqB = [Buf() for _ in range(4)]
        self.sskv = A.alloc([128, 8], F32); self.sskvB = [Buf() for _ in range(4)]
        self.xnB2 = [Buf() for _ in range(4)]
        self.rt = [A.alloc([128, 8, 32], F32) for _ in range(4)]; self.rtB = [Buf() for _ in range(4)]
        self.qr_tok = A.alloc([128, 4, 512], BF16); self.qrtB = [Buf() for _ in range(4)]
        self.kr_tok = A.alloc([128, 4, 128], BF16); self.krtB = Buf()
        self.qn_st = self.aT[:, 0:8, :]
        self.qr_st = A.alloc([128, 4, 512], BF16); self.qrB = Buf()
        self.kn_st = self.aT[:, 8:16, :]
        self.kr2_st = A.alloc([128, 512], BF16); self.kr2B = Buf()
        self.vm_st = A.alloc([128, 4, D], BF16); self.vmB = Buf()

    def rope(self, x1, x2, s, nh, out1, out2, reads, writes):
        cos = self.cs[:, s, 0:32].unsqueeze(1).to_broadcast([128, nh, 32])
        sin = self.cs[:, s, 32:64].unsqueeze(1).to_broadcast([128, nh, 32])
        r = self.rt
        rB = self.rtB
        rd = list(reads) + [self.csB]
        v = lambda i: r[i][:, 0:nh, :]
        self.tt("dve", v(0), x1, cos, ALU.mult, reads=rd, writes=[rB[0]])
        self.tt("dve", v(1), x2, sin, ALU.mult, reads=rd, writes=[rB[1]])
        self.tt("dve", v(2), x2, cos, ALU.mult, reads=rd, writes=[rB[2]])
        self.tt("dve", v(3), x1, sin, ALU.mult, reads=rd, writes=[rB[3]])
        for o1, o2 in zip(out1, out2):
            self.tt("pool", o1, v(0), v(1), ALU.subtract, reads=[rB[0], rB[1]], writes=writes)
            self.tt("pool", o2, v(2), v(3), ALU.add, reads=[rB[2], rB[3]], writes=writes)

    def mla_proj_load(self, t):
        t0 = t * 512
        si = max(i for i in range(len(self.seqs)) if self.offs[i] <= t0)
        pos0 = t0 - self.offs[si]
        self.dma(self.cs[:], self.c_rope[pos0:pos0 + 512, :].rearrange("(s p) f -> p s f", p=128), writes=[self.csB])

    def mla_proj(self, xi, t):
        xt, xB = self.xt[xi], self.xB[xi]
        t0 = t * 512
        self.xnorm_post(self.gfm[("mix", 1)])
        hT, hTB = self.hT, self.hTB
        win, winB = self.ring_next()
        win2, win2B = self.ring_next()
        for s in range(4):
            p1, p1B = self.ps()
            for c in range(8):
                self.mm(p1[:, 0:384], hT[:, c, s * 128:(s + 1) * 128], win[:, c, :], c == 0, c == 7, reads=[winB] + hTB, writes=[p1B])
            self.copy_any(self.cq[:, s, :], p1[:, 0:384], reads=[p1B], writes=[self.cqB[s]])
            self.norm_pre(s, self.cq[:, s, :], self.cqB[s], 384, self.ssq, self.ssqB, self.xn[:, s, 0:384], self.xnB[s])
            p2, p2B = self.ps()
            for c in range(8):
                self.mm(p2[:, 0:320], hT[:, c, s * 128:(s + 1) * 128], win2[:, c, :], c == 0, c == 7, reads=[win2B] + hTB, writes=[p2B])
            self.P.op("dve", lambda e, s=s, p2=p2: e.tensor_copy(out=self.ckv[:, s, :], in_=p2[:, 0:256]), reads=[p2B], writes=[self.ckvB[s]])
            self.P.op("dve", lambda e, s=s, p2=p2: e.tensor_copy(out=self.krs[:, s, :], in_=p2[:, 256:320]), reads=[p2B], writes=[self.krsB])
            self.norm_pre(s, self.ckv[:, s, :], self.ckvB[s], 256, self.sskv, self.sskvB, self.xn[:, s, 512:768], self.xnB2[s])
        for s in range(4):
            self.norm_post(s, self.gfm["qn"], 384, self.cqT, self.cqTB[s], self.xn[:, s, 0:384], self.xnB[s])
            self.norm_post(s, self.gfm["kvn"], 256, self.ckvT, self.ckvTB[s], self.xn[:, s, 512:768], self.xnB2[s])
        wuqs = [self.ring_next(), self.ring_next()]
        for h in range(8):
            pq, pqB = self.ps()
            wuq, wuqB = wuqs[h // 4]
            hh = h % 4
            for c in range(3):
                self.mm(pq[:], wuq[:, c, hh * 192:hh * 192 + 128], self.cqT[:, c, :], c == 0, c == 2, reads=[wuqB] + self.cqTB, writes=[pqB])
            self.copy_any(self.qn_st[:, h, :], pq[:], reads=[pqB], writes=[self.aTB[h]])
        for s in range(4):
            pr, prB = self.ps()
            for g4 in range(2):
                wuq, wuqB = wuqs[g4]
                for c in range(3):
                    rhs = wuq[:, c, :].rearrange("p (h f) -> p h f", f=192)[:, :, 128:192]
                    self.mm(pr[:, g4 * 256:(g4 + 1) * 256].rearrange("p (h f) -> p h f", f=64), self.cqT[:, c, s * 128:(s + 1) * 128], rhs,
                            c == 0, c == 2, reads=[wuqB] + self.cqTB, writes=[prB])
            prv = pr[:].rearrange("p (h f) -> p h f", f=64)
            qv = self.qr_tok[:, s, :].rearrange("p (h f) -> p h f", f=64)
            self.rope(prv[:, :, 0:32], prv[:, :, 32:64], s, 8, [qv[:, :, 0:32]], [qv[:, :, 32:64]], reads=[prB], writes=[self.qrtB[s]])
            pt, ptB = self.pst()
            for m_ in range(4):
                self.tp(pt[:, m_ * 128:(m_ + 1) * 128], self.qr_tok[:, s, m_ * 128:(m_ + 1) * 128], reads=[self.qrtB[s]], writes=[ptB])
            self.copy_any(self.qr_st[:, :, s * 128:(s + 1) * 128], pt[:, 0:512].rearrange("p (c t) -> p c t", c=4), reads=[ptB], writes=[self.qrB])
        wkv, wkvB = self.ring_next()
        for h in range(8):
            pk, pkB = self.ps()
            for c in range(2):
                self.mm(pk[:], wkv[:, c, h * 256:h * 256 + 128], self.ckvT[:, c, :], c == 0, c == 1, reads=[wkvB] + self.ckvTB, writes=[pkB])
            self.copy_any(self.kn_st[:, h, :], pk[:], reads=[pkB], writes=[self.aTB[8 + h]])
        for s in range(4):
            for half in range(2):
                pv, pvB = self.ps()
                for c in range(2):
                    rhs = wkv[:, c, :].rearrange("p (h f) -> p h f", f=256)[:, 4 * half:4 * half + 4, 128:256]
                    self.mm(pv[:].rearrange("p (h f) -> p h f", f=128), self.ckvT[:, c, s * 128:(s + 1) * 128], rhs, c == 0, c == 1,
                            reads=[wkvB] + self.ckvTB, writes=[pvB])
                self.copy_any(self.vm_st[:, s, half * 512:(half + 1) * 512], pv[:], reads=[pvB], writes=[self.vmB])
            kv_ = self.krs[:, s, :].rearrange("p (h f) -> p h f", h=1)
            ko = self.kr_tok[:, s, :].rearrange("p (h f) -> p h f", h=1)
            self.rope(kv_[:, :, 0:32], kv_[:, :, 32:64], s, 1, [ko[:, :, 0:32], ko[:, :, 64:96]], [ko[:, :, 32:64], ko[:, :, 96:128]],
                      reads=[self.krsB], writes=[self.krtB])
            pt, ptB = self.pst()
            self.tp(pt[:, 0:128], self.kr_tok[:, s, :], reads=[self.krtB], writes=[ptB])
            self.copy_any(self.kr2_st[:, s * 128:(s + 1) * 128], pt[:, 0:128], reads=[ptB], writes=[self.kr2B])
        mb = self.mlaB
        self.dma(self.QN.rearrange("h p t -> p h t")[:, :, t0:t0 + 512], self.qn_st, reads=self.aTB[0:8], writes=[mb[0][t]])
        self.dma(self.QR.rearrange("h p t -> p h t")[:, :, t0:t0 + 512], self.qr_st[:], reads=[self.qrB], writes=[mb[1][t]])
        self.dma(self.KN.rearrange("h p t -> p h t")[:, :, t0:t0 + 512], self.kn_st, reads=self.aTB[8:16], writes=[mb[2][t]])
        self.dma(self.KR2[:, t0:t0 + 512], self.kr2_st[:], reads=[self.kr2B], writes=[mb[3][t]])
        self.dma(self.VM[t0:t0 + 512, :].rearrange("(s p) f -> p s f", p=128), self.vm_st[:], reads=[self.vmB], writes=[mb[4][t]])

    def mla_attn(self, bg=None):
        A, P = self.A, self.P
        m = A.mark()
        Lmax = max(self.seqs)
        NBm = Lmax // 128
        kr2 = [A.alloc([128, Lmax], BF16) for _ in range(2)]; kr2B = [Buf(), Buf()]
        qr = [A.alloc([128, Lmax], BF16) for _ in range(2)]; qrB = [Buf(), Buf()]
        kn = [A.alloc([128, Lmax], BF16) for _ in range(2)]; knB = [Buf(), Buf()]
        qn = [A.alloc([128, Lmax], BF16) for _ in range(2)]; qnB = [Buf(), Buf()]
        vv = [A.alloc([128, NBm, 128], BF16) for _ in range(2)]; vB = [Buf(), Buf()]
        ot = [A.alloc([128, Lmax], BF16) for _ in range(2)]; otB = [Buf(), Buf()]
        pT = [A.alloc([128, 512], BF16) for _ in range(4)]; pTB = [Buf() for _ in range(4)]
        rden = A.alloc([128, 512], F32); rdB = Buf()
        kra = A.alloc([128, Lmax], BF16); krb = A.alloc([128, Lmax], BF16); kraB = Buf(); krbB = Buf()
        accP = [A.alloc([128, 512], F32) for _ in range(2)]; accPB = [Buf(), Buf()]
        accD = [A.alloc([128, 512], F32) for _ in range(2)]; accDB = [Buf(), Buf()]
        ones32 = A.alloc([128, 128], F32); o32B = Buf()
        P.op("pool", lambda e: e.memset(ones32[:], 1.0), writes=[o32B])
        pending = []
        P.op("pool", lambda e: e.memset(kra[:], 0.0), writes=[kraB])
        P.op("pool", lambda e: e.memset(krb[:], 0.0), writes=[krbB])
        scale = 192.0 ** -0.5
        mb = self.mlaB
        hcount = 0
        pcount = 0
        pi = 0
        qt_count = 0
        for si, L in enumerate(self.seqs):
            off = self.offs[si]
            NB = L // 128
            NQ = L // 512
            tiles = list(range(off // 512, (off + L) // 512))
            ks = si % 2
            self.dma(kr2[ks][:, 0:L], self.KR2[:, off:off + L], reads=[mb[3][t] for t in tiles], writes=[kr2B[ks]])
            P.op("dve", lambda e, ks=ks, L=L: e.tensor_copy(out=kra[0:64, 0:L], in_=kr2[ks][0:64, 0:L]), reads=[kr2B[ks]], writes=[kraB])
            P.op("pool", lambda e, ks=ks, L=L: e.tensor_copy(out=krb[64:128, 0:L], in_=kr2[ks][64:128, 0:L]), reads=[kr2B[ks]], writes=[krbB])
            for h in range(8):
                hs = hcount % 2
                hcount += 1
                if h % 2 == 0:
                    ps_ = pcount % 2
                    pcount += 1
                    self.dma(qr[ps_][:, 0:L], self.QR[h // 2, :, off:off + L], reads=[mb[1][t] for t in tiles], writes=[qrB[ps_]])
                self.dma(kn[hs][:, 0:L], self.KN[h, :, off:off + L], reads=[mb[2][t] for t in tiles], writes=[knB[hs]])
                self.dma(qn[hs][:, 0:L], self.QN[h, :, off:off + L], reads=[mb[0][t] for t in tiles], writes=[qnB[hs]])
                self.dma(vv[hs][:, 0:NB, :], self.VM[off:off + L, h * 128:(h + 1) * 128].rearrange("(n p) f -> p n f", p=128),
                         reads=[mb[4][t] for t in tiles], writes=[vB[hs]])
                r0 = 64 * (h % 2)
                for qt in range(NQ):
                    qsl = slice(qt * 512, (qt + 1) * 512)
                    po, poB = self.psF[3 + qt_count % 2], self.psFB[3 + qt_count % 2]
                    aP, aPB = accP[qt_count % 2], accPB[qt_count % 2]
                    aD, aDB = accD[qt_count % 2], accDB[qt_count % 2]
                    qt_count += 1
                    pd, pdB = self.psT[0][:, 0:1024].bitcast(F32), self.psTB[0]
                    krx, krxB = (kra, kraB) if h % 2 == 0 else (krb, krbB)

                    sbank = [0, 1, 2, 5]

                    def qk(kb):
                        j = sbank[kb % 4]
                        psb, psB = self.psF[j], self.psFB[j]
                        ksl = slice(kb * 128, (kb + 1) * 128)
                        self.mm(psb[:], kn[hs][:, ksl], qn[hs][:, qsl], True, False, reads=[knB[hs], qnB[hs]], writes=[psB])
                        self.mm(psb[:], krx[:, ksl], qr[ps_][:, qsl], False, True, reads=[krxB, qrB[ps_]], writes=[psB])

                    def pv(kb):
                        nonlocal pi
                        j = sbank[kb % 4]
                        psb, psB = self.psF[j], self.psFB[j]
                        pt_, ptB_ = pT[pi % 4], pTB[pi % 4]
                        pi += 1
                        self.act(pt_[:], psb[:], AF.Exp, reads=[psB], writes=[ptB_], scale=scale)
                        self.mm(po[:], vv[hs][:, kb, :], pt_[:], kb == 0, kb == NB - 1, reads=[vB[hs], ptB_], writes=[poB])
                        eng, acc, accB = ("pool", aP, aPB) if kb % 2 == 0 else ("dve", aD, aDB)
                        if kb < 2:
                            P.op(eng, lambda e, acc=acc, pt_=pt_: e.tensor_copy(out=acc[:], in_=pt_[:]), reads=[ptB_], writes=[accB])
                        else:
                            self.tt(eng, acc[:], acc[:], pt_[:], ALU.add, reads=[ptB_, accB], writes=[accB])

                    qk(0)
                    if NB > 1:
                        qk(1)
                    for kb in range(NB):
                        if kb + 2 < NB:
                            qk(kb + 2)
                        pv(kb)
                        if kb == min(3, NB - 1) and pending:
                            pending.pop()()
                    def make_ep(aP=aP, aPB=aPB, aD=aD, aDB=aDB, po=po, poB=poB, ot_ap=ot[hs][:, qsl], otB_h=otB[hs], pd=pd, pdB=pdB,
                                last=(qt == NQ - 1), h=h, hs=hs, off=off, L=L, tiles=tiles):
                        def ep():
                            self.tt("dve", aD[:], aD[:], aP[:], ALU.add, reads=[aPB, aDB], writes=[aDB])
                            self.mm(pd, ones32[:], aD[:], True, True, reads=[o32B, aDB], writes=[pdB])
                            P.op("dve", lambda e: e.reciprocal(out=rden[:], in_=pd), reads=[pdB], writes=[rdB])
                            self.tt("dve", ot_ap, po[:], rden[:], ALU.mult, reads=[poB, rdB], writes=[otB_h])
                            if last:
                                self.dma(self.OT[h, :, off:off + L], ot[hs][:, 0:L], reads=[otB_h], writes=[self.otB[t][h] for t in tiles])
                        return ep
                    pending.append(make_ep())
                    if bg is not None and qt_count % 3 == 0:
                        next(bg, None)
        while pending:
            pending.pop()()
        if bg is not None:
            for _ in bg:
                pass
        A.reset(m)

    def mla_out_descs(self):
        w = self.wbf["mla_w_out"][0].rearrange("(c p) d -> p c d", p=128)
        return [(w[:, :, hf * 512:(hf + 1) * 512], [128, 8, 512], self.wB[("mla_w_out", 0)]) for hf in range(2)]

    def mla_out_load(self, t):
        t0 = t * 512
        self.dma(self.ott[t % 2][:], self.OT.rearrange("h p t -> p h t")[:, :, t0:t0 + 512], reads=self.otB[t], writes=[self.ottB[t % 2]])

    def mla_out(self, xi, t, after=None):
        self.proj_add(xi, self.ott[t % 2], [self.ottB[t % 2]], after)

    def token_phase(self, src, ops, descs_fn, tile_loads=(), next_loads=(), first_loads=()):
        for t in range(self.NTILE):
            self.ring_descs.extend(descs_fn())
        srcap = self.xin if src == "xin" else self.y

        def load(t):
            rd = [] if src == "xin" else [self.yB[t]]
            self.dma(self.xt[t % 2][:], srcap[t * 512:t * 512 + 512, :].rearrange("(s p) d -> p s d", p=128), reads=rd, writes=self.xB[t % 2])
        load(0)
        for f in list(next_loads) + list(first_loads):
            f(0)
        for t in range(self.NTILE):
            xi = t % 2
            t0 = t * 512
            for f in tile_loads:
                f(t)
            if t + 1 < self.NTILE:
                load(t + 1)
                for f in next_loads:
                    f(t + 1)
            for k, (pre, body, _tail) in enumerate(ops):
                if pre is not None and (k == 0 or not ops[k - 1][2]):
                    for s in range(4):
                        pre(xi, s)
                nxt = ops[k + 1][0] if k + 1 < len(ops) else None
                after = (lambda s, nxt=nxt, xi=xi: nxt(xi, s)) if (nxt is not None and ops[k][2]) else None
                body(xi, t, after)
            self.dma(self.y[t0:t0 + 512, :].rearrange("(s p) d -> p s d", p=128), self.xt[xi][:], reads=self.xB[xi], writes=[self.yB[t]])

    def build(self):
        A = self.A
        self.setup_consts()
        if self.stages == -1:
            self.P.emit(); return self.nc
        self.P.barrier()
        if self.stages == -2:
            self.P.emit(); return self.nc
        m0 = A.mark()
        for _ in self.conv_gen(self.CONV_ORDER[0:3], 4096, ["pool", "dve", "act"]):
            pass
        A.reset(m0)
        self.P.barrier()
        base = A.mark()
        if self.stages <= 0:
            self.P.emit(); return self.nc
        self.alloc_token_bufs()
        self.ring_init()
        tok_mark = A.mark()
        self.alloc_gla_proj()
        xp = self.xnorm_pre
        self.token_phase("xin", [(xp, lambda xi, t, af: self.ffn(xi, 0, 0, af), True), (xp, lambda xi, t, af: self.gla_proj(xi, t), False)],
                         lambda: self.ffn_descs(0, 0) + self.gla_proj_descs())
        if self.stages <= 1:
            self.P.emit(); return self.nc
        self.P.barrier()
        A.reset(base)
        bg = self.conv_gen(self.CONV_ORDER[3:13], 1024, ["act", "dve"], q="sp")
        next(bg)
        self.gla_scan(bg, bg_every=2)
        if self.stages <= 2:
            self.P.emit(); return self.nc
        self.P.barrier()
        A.reset(tok_mark)
        self.pt = A.alloc([128, 4, 256], F32); self.ptB = Buf()
        self.ptb = A.alloc([128, 4, 256], BF16); self.ptbB = Buf()
        self.pT = A.alloc([128, 2, 512], BF16); self.pTB = Buf()
        self.alloc_mla_proj()
        og1 = A.alloc([128, 4, D], BF16); ogb1 = Buf()
        self.ogt = [og1, og1]; self.ogtB = [ogb1, ogb1]
        xp = self.xnorm_pre
        self.token_phase("y", [(None, self.gla_out, True), (xp, lambda xi, t, af: self.ffn(xi, 0, 1, af, mid=lambda: self.ple_prep(0, t)), True),
                               (xp, lambda xi, t, af: self.ple(xi, 0, t, af), True), (xp, lambda xi, t, af: self.ffn(xi, 1, 0, af), True),
                               (xp, lambda xi, t, af: self.mla_proj(xi, t), False)],
                         lambda: self.gla_out_descs() + self.ffn_descs(0, 1) + self.ple_descs(0) + self.ffn_descs(1, 0) + self.mla_proj_descs(),
                         tile_loads=[lambda t: self.ple_load(0, t), self.mla_proj_load], first_loads=[self.gla_out_load])
        if self.stages <= 3:
            self.P.emit(); return self.nc
        self.P.barrier()
        A.reset(base)
        bg = self.conv_gen(self.CONV_ORDER[13:18], 2048, ["act"], q="sp")
        next(bg)
        self.mla_attn(bg)
        if self.stages <= 4:
            self.P.emit(); return self.nc
        self.P.barrier()
        A.reset(tok_mark)
        self.pt = A.alloc([128, 4, 256], F32); self.ptb = A.alloc([128, 4, 256], BF16); self.pT = A.alloc([128, 2, 512], BF16)
        self.ott = [A.alloc([128, 8, 512], BF16) for _ in range(2)]; self.ottB = [Buf(), Buf()]
        xp = self.xnorm_pre
        self.token_phase("y", [(None, self.mla_out, True), (xp, lambda xi, t, af: self.ffn(xi, 1, 1, af, mid=lambda: self.ple_prep(1, t)), True),
                               (xp, lambda xi, t, af: self.ple(xi, 1, t, af), True),
                               (self.final_pre, lambda xi, t, af: self.final_norm(xi), False)],
                         lambda: self.mla_out_descs() + self.ffn_descs(1, 1) + self.ple_descs(1),
                         tile_loads=[lambda t: self.ple_load(1, t)], next_loads=[self.mla_out_load])
        self.P.emit()
        return self.nc


def make_consts():
    c = {}
    c["c_ident"] = np.eye(128, dtype=np.float32)
    s = np.arange(128)[:, None]
    t = np.arange(128)[None, :]
    g = np.float32(-1.0 / 16.0)
    tri = np.zeros((4, 128, 128), np.float32)
    tri[0] = (s <= t) * g
    tri[1] = (s >= t) * g
    tri[2] = (s > t) * g
    tri[3] = (s < t) * g
    c["c_tri"] = tri
    mask = np.zeros((2, 128, 128), np.float32)
    mask[0] = (s <= t)
    mask[1] = (s > t)
    c["c_mask"] = mask
    inv_freq = (np.float32(10000.0) ** (-np.arange(0, 64, 2, dtype=np.float32) / np.float32(64))).astype(np.float32)
    ang = np.arange(4096, dtype=np.float32)[:, None] * inv_freq[None, :]
    c["c_rope"] = np.concatenate([np.cos(ang), np.sin(ang)], axis=1).astype(np.float32)
    return c


_ALL_W = list(WEIGHT_SHAPES) + list(SMALL_SHAPES)


def kernel(**inputs):
    inputs = {k: np.asarray(v) for k, v in inputs.items()}
    xp, xs = inputs["x_prompt"], inputs["x_sample"]
    pp, psm = inputs["p_prompt"], inputs["p_sample"]
    nc = Builder(SEQS_FULL).build()
    consts = make_consts()
    in_maps = []
    for i in range(NCORES):
        m = {}
        m["xin"] = np.ascontiguousarray(np.concatenate([xp[2 * i], xp[2 * i + 1], xs[i]], axis=0))
        m["pin"] = np.ascontiguousarray(np.concatenate([pp[:, 2 * i], pp[:, 2 * i + 1], psm[:, i]], axis=1))
        for n in _ALL_W:
            m[n] = np.ascontiguousarray(inputs[n]).reshape(WEIGHT_SHAPES.get(n, SMALL_SHAPES.get(n)))
        m.update(consts)
        in_maps.append(m)
    res = run_bass_kernel_spmd(nc, in_maps, core_ids=list(range(NCORES)))
    yp = np.empty((16, 4096, D), np.float32)
    ys = np.empty((8, 2048, D), np.float32)
    for i in range(NCORES):
        y = np.asarray(res.results[i]["y"]).reshape(-1, D)
        yp[2 * i] = y[0:4096]
        yp[2 * i + 1] = y[4096:8192]
        ys[i] = y[8192:10240]
    return (yp, ys)
```

```python
import numpy as np
import concourse.bass as bass
import concourse.mybir as mybir
from concourse.bass_utils import run_bass_kernel_spmd

F32 = mybir.dt.float32
BF16 = mybir.dt.bfloat16
AF = mybir.ActivationFunctionType
ALU = mybir.AluOpType

N_DMA_SEMS = 24
D = 1024
DFF = 2816
NJ = 22
EPS = 1e-6
NCORES = 8
SEQS_FULL = [4096, 4096, 2048]


class Buf:
    __slots__ = ("name", "w", "r")

    def __init__(self, name=""):
        self.name = name
        self.w = ()
        self.r = {}


class Prog:
    COMPUTE = ("pe", "act", "dve", "pool")
    ALL = ("pe", "act", "dve", "pool", "sp")

    def __init__(self, nc):
        self.nc = nc
        self.streams = {e: [] for e in self.ALL}
        self.dma_count = {e: 0 for e in self.ALL}
        self.dma_last = {}

    def _deps(self, reads, writes, join=False):
        deps = set()
        for b in reads:
            deps.update(b.w)
        for b in writes:
            if not join:
                deps.update(b.w)
            deps.update(b.r.values())
        return deps

    def op(self, eng, fn, reads=(), writes=()):
        st = self.streams[eng]
        ev = (eng, len(st))
        deps = self._deps(reads, writes)
        st.append({"fn": fn, "deps": deps, "dma": None})
        for b in reads:
            b.r[eng] = ev
        for b in writes:
            b.w = (ev,)
            b.r = {}
        return ev

    def dma(self, q, out, in_, reads=(), writes=(), join=False, **kw):
        st = self.streams[q]
        k = self.dma_count[q]
        self.dma_count[q] += 1
        slot = k % N_DMA_SEMS
        val = 16 * (k // N_DMA_SEMS + 1)
        ev = ("dma", q, slot, val)
        deps = self._deps(reads, writes, join)
        prev = self.dma_last.get((q, slot))
        if prev is not None:
            deps.add(prev)
        self.dma_last[(q, slot)] = ev
        st.append({"fn": (lambda e, out=out, in_=in_, kw=kw: e.dma_start(out=out, in_=in_, **kw)),
                   "deps": deps, "dma": ev})
        for b in reads:
            b.r[ev] = ev
        for b in writes:
            b.w = (b.w + (ev,)) if join else (ev,)
            b.r = {}
        return ev

    def barrier(self):
        deps = set()
        for e in self.COMPUTE:
            if self.streams[e]:
                j = len(self.streams[e]) - 1
                while j >= 0 and self.streams[e][j]["fn"] is None:
                    j -= 1
                if j >= 0:
                    deps.add((e, j))
        deps.update(self.dma_last.values())
        for e in self.ALL:
            self.streams[e].append({"fn": None, "deps": set(deps), "dma": None})

    def emit(self):
        nc = self.nc
        marked = {e: set() for e in self.COMPUTE}
        for e in self.ALL:
            for ent in self.streams[e]:
                for d in ent["deps"]:
                    if d[0] != "dma":
                        if d[0] == "pe" and e == "pe":
                            continue
                        marked[d[0]].add(d[1])
        rank = {}
        for e in self.COMPUTE:
            rank[e] = {idx: i + 1 for i, idx in enumerate(sorted(marked[e]))}
        from contextlib import ExitStack
        with ExitStack() as es:
            csem = {e: es.enter_context(nc.semaphore("s_" + e)) for e in self.COMPUTE}
            dsem = {}
            for q in self.ALL:
                for s in range(min(N_DMA_SEMS, self.dma_count[q])):
                    dsem[(q, s)] = es.enter_context(nc.semaphore("d_%s_%d" % (q, s)))
            block = es.enter_context(nc.Block())
            streams = self.streams
            final_dma = list(self.dma_last.values())

            def run(ename, eng):
                known = {}
                for idx, ent in enumerate(streams[ename]):
                    waits = {}
                    for d in ent["deps"]:
                        if d[0] == "dma":
                            key = ("dma", d[1], d[2])
                            sem = dsem[(d[1], d[2])]
                            val = d[3]
                        else:
                            if d[0] == "pe" and ename == "pe":
                                continue
                            key = d[0]
                            sem = csem[d[0]]
                            val = rank[d[0]][d[1]]
                        if known.get(key, 0) >= val:
                            continue
                        if key not in waits or waits[key][1] < val:
                            waits[key] = (sem, val)
                    for key, (sem, val) in waits.items():
                        eng.wait_ge(sem, val)
                        known[key] = val
                    if ent["fn"] is None:
                        continue
                    ins = ent["fn"](eng)
                    if ent["dma"] is not None:
                        d = ent["dma"]
                        ins.then_inc(dsem[(d[1], d[2])], 16)
                    elif idx in rank.get(ename, {}):
                        ins.then_inc(csem[ename], 1)
                if ename == "sp":
                    for d in final_dma:
                        key = ("dma", d[1], d[2])
                        if known.get(key, 0) < d[3]:
                            eng.wait_ge(dsem[(d[1], d[2])], d[3])

            @block.tensor
            def _(e):
                run("pe", e)

            @block.scalar
            def _(e):
                run("act", e)

            @block.vector
            def _(e):
                run("dve", e)

            @block.gpsimd
            def _(e):
                run("pool", e)

            @block.sync
            def _(e):
                run("sp", e)


class Arena:
    def __init__(self, nc, limit=229376):
        self.nc = nc
        self.off = 16640
        self.limit = limit
        self.n = 0

    def alloc(self, shape, dtype):
        nbytes = int(np.prod(shape[1:])) * (2 if dtype == BF16 else 4)
        nbytes = (nbytes + 63) // 64 * 64
        off = self.off
        assert off + nbytes <= self.limit, ("SBUF overflow", off, nbytes)
        self.off += nbytes
        self.n += 1
        return self.nc.alloc_sbuf_tensor_at("t%d" % self.n, list(shape), dtype, offset=off)

    def mark(self):
        return self.off

    def reset(self, m):
        self.off = m


WEIGHT_SHAPES = {
    "ffn_w_in": (2, 2, D, 2 * DFF), "ffn_w_out": (2, 2, DFF, D),
    "ple_w_gate": (2, D, D), "ple_w_proj": (2, 256, D),
    "gla_w_in": (1, D, 3104), "gla_w_out": (1, D, D),
    "mla_w_in": (1, D, 704), "mla_w_uq": (1, 384, 1536), "mla_w_ukv": (1, 256, 2048), "mla_w_out": (1, D, D),
}
SMALL_SHAPES = {
    "ffn_norm": (2, 2, D), "mix_norm": (2, D), "ple_norm": (2, D),
    "gla_w_gf_up": (1, 16, 512), "gla_b_gf": (1, 512), "gla_w_gb_up": (1, 16, 512), "gla_b_gb": (1, 512),
    "gla_out_norm": (1, 256), "mla_q_norm": (1, 384), "mla_kv_norm": (1, 256), "final_norm": (D,),
}


class Builder:
    def __init__(self, seqs, dbg=False, stages=99):
        self.seqs = list(seqs)
        self.NT = sum(seqs)
        self.NTILE = self.NT // 512
        self.offs = [sum(seqs[:i]) for i in range(len(seqs))]
        self.dbg = dbg
        self.stages = stages
        nc = bass.Bass("TRN2", target_bir_lowering=False)
        self.nc = nc
        self.P = Prog(nc)
        self.A = Arena(nc)
        NT = self.NT
        inp = lambda n, s, dt=F32: nc.dram_tensor(n, list(s), dt, kind="ExternalInput").ap()
        self.xin = inp("xin", (NT, D))
        self.pin = inp("pin", (2, NT, 256))
        self.win = {n: inp(n, s) for n, s in WEIGHT_SHAPES.items()}
        self.sin = {n: inp(n, s) for n, s in SMALL_SHAPES.items()}
        self.c_ident = inp("c_ident", (128, 128))
        self.c_tri = inp("c_tri", (4, 128, 128))
        self.c_mask = inp("c_mask", (2, 128, 128))
        self.c_rope = inp("c_rope", (4096, 64))
        self.y = nc.dram_tensor("y", [NT, D], F32, kind="ExternalOutput").ap()
        kind = "ExternalOutput" if dbg else "Internal"
        scr = lambda n, s, dt=BF16: nc.dram_tensor(n, list(s), dt, kind=kind).ap()
        self.wbf = {n: nc.dram_tensor("bf_" + n, list(s), BF16).ap() for n, s in WEIGHT_SHAPES.items()}
        self.QD = scr("QD", (2, 4, 128, NT)); self.KI = scr("KI", (2, 4, 128, NT))
        self.KE = scr("KE", (2, NT, 512)); self.DEC = scr("DEC", (2, 4, 128, NT // 128), F32)
        self.VG = scr("VG", (NT, D)); self.SR = scr("SR", (NT, D)); self.OG = scr("OG", (NT, D))
        self.QN = scr("QN", (8, 128, NT)); self.QR = scr("QR", (4, 128, NT)); self.KN = scr("KN", (8, 128, NT))
        self.KR2 = scr("KR2", (128, NT)); self.VM = scr("VM", (NT, D)); self.OT = scr("OT", (8, 128, NT))
        nt = self.NTILE
        self.yB = [Buf("y%d" % t) for t in range(nt)]
        self.glaB = [Buf("gla%d" % t) for t in range(nt)]
        self.glaB2 = [[Buf() for t in range(nt)] for _ in range(6)]
        self.ogB = [[Buf() for h in range(4)] for _ in range(nt)]
        self.mlaB = [[Buf() for t in range(nt)] for _ in range(5)]
        self.otB = [[Buf() for h in range(8)] for _ in range(nt)]
        self.wB = {}
        self.psF = [nc.alloc_psum_tensor("psF%d" % i, [128, 512], F32) for i in range(6)]
        self.psFB = [Buf("psF%d" % i) for i in range(6)]
        self.psT = [nc.alloc_psum_tensor("psT%d" % i, [128, 1024], BF16) for i in range(2)]
        self.psTB = [Buf("psT%d" % i) for i in range(2)]
        self.ps_i = 0
        self.pst_i = 0
        self.cast_i = 0

    def ps(self):
        i = self.ps_i % 6
        self.ps_i += 1
        return self.psF[i], self.psFB[i]

    def pst(self):
        i = self.pst_i % 2
        self.pst_i += 1
        return self.psT[i], self.psTB[i]

    def mm(self, out, lhsT, rhs, start, stop, reads, writes):
        self.P.op("pe", lambda e: e.matmul(out, lhsT=lhsT, rhs=rhs, start=start, stop=stop), reads=reads, writes=writes)

    def tp(self, out, in_, reads, writes):
        idb = self.idb
        self.P.op("pe", lambda e: e.transpose(out=out, in_=in_, identity=idb[:]), reads=list(reads) + [self.constB], writes=writes)

    def act(self, out, in_, func, reads, writes, **kw):
        self.P.op("act", lambda e: e.activation(out=out, in_=in_, func=func, **kw), reads=reads, writes=writes)

    def copy_any(self, out, in_, reads, writes):
        self.cast_i += 1
        if self.cast_i % 2 == 0:
            self.P.op("act", lambda e: e.copy(out=out, in_=in_), reads=reads, writes=writes)
        else:
            self.P.op("dve", lambda e: e.tensor_copy(out=out, in_=in_), reads=reads, writes=writes)

    def tt(self, eng, out, in0, in1, op, reads, writes):
        self.P.op(eng, lambda e: e.tensor_tensor(out=out, in0=in0, in1=in1, op=op), reads=reads, writes=writes)

    def stt(self, eng, out, in0, scalar, in1, op0, op1, reads, writes):
        self.P.op(eng, lambda e: e.scalar_tensor_tensor(out=out, in0=in0, scalar=scalar, in1=in1, op0=op0, op1=op1),
                  reads=reads, writes=writes)

    def dma(self, out, in_, reads=(), writes=(), q="sp", **kw):
        self.P.dma(q, out, in_, reads=reads, writes=writes, **kw)

    def setup_consts(self):
        A, P, nc = self.A, self.P, self.nc
        self.constB = Buf("const")
        cB = self.constB
        tmpf = nc.alloc_sbuf_tensor_at("tmpf", [128, 2560], F32, offset=170048)
        tB = Buf()
        self.idb = A.alloc([128, 128], BF16)
        self.tri = A.alloc([128, 4, 128], F32)
        self.mask = A.alloc([128, 2, 128], BF16)
        self.ones = A.alloc([128, 128], BF16)
        self.dma(tmpf[:, 0:128], self.c_ident[:, :], writes=[tB])
        self.dma(tmpf[:, 128:384].rearrange("p (a b) -> p a b", a=2), self.c_mask.rearrange("a p b -> p a b"), writes=[tB])
        self.dma(self.tri[:], self.c_tri.rearrange("a p b -> p a b"), writes=[cB])
        P.op("dve", lambda e: e.tensor_copy(out=self.idb[:], in_=tmpf[:, 0:128]), reads=[tB], writes=[cB])
        P.op("dve", lambda e: e.tensor_copy(out=self.mask[:], in_=tmpf[:, 128:384].rearrange("p (a b) -> p a b", a=2)), reads=[tB], writes=[cB])
        P.op("dve", lambda e: e.memset(self.ones[:], 1.0), writes=[cB])
        self.gfm = {}
        def fm(name, ap, n):
            t = A.alloc([128, n // 128], F32)
            self.dma(t[:], ap.rearrange("(c p) -> p c", p=128), writes=[cB], allow_slow_non_contiguous=True)
            self.gfm[name] = t
        for l in range(2):
            for a in range(2):
                fm(("ffn", l, a), self.sin["ffn_norm"][l, a], D)
            fm(("mix", l), self.sin["mix_norm"][l], D)
            fm(("ple", l), self.sin["ple_norm"][l], D)
        fm("qn", self.sin["mla_q_norm"][0], 384)
        fm("kvn", self.sin["mla_kv_norm"][0], 256)
        self.g_final = A.alloc([128, D], F32)
        self.dma(self.g_final[:], self.sin["final_norm"].partition_broadcast(128), writes=[cB])
        self.g_out = A.alloc([128, 256], F32)
        self.dma(self.g_out[:], self.sin["gla_out_norm"][0].partition_broadcast(128), writes=[cB])
        self.wup = []
        for nm, bn in (("gla_w_gf_up", "gla_b_gf"), ("gla_w_gb_up", "gla_b_gb")):
            self.dma(tmpf[0:16, 1024:1536], self.sin[nm][0], writes=[tB])
            self.dma(tmpf[16:17, 1024:1536], self.sin[bn][0:1, :], writes=[tB])
            t = A.alloc([32, 512], BF16)
            P.op("dve", lambda e, t=t: e.tensor_copy(out=t[0:17, :], in_=tmpf[0:17, 1024:1536]), reads=[tB], writes=[cB])
            self.wup.append(t)
        self.lo_aug = [A.alloc([32, 512], BF16) for _ in range(2)]
        self.loB = [Buf(), Buf()]
        for i in range(2):
            P.op("dve", lambda e, i=i: e.memset(self.lo_aug[i][:], 1.0), writes=[self.loB[i]])

    CONV_ORDER = [("ffn_w_in", 0), ("ffn_w_out", 0), ("gla_w_in", 0),
                  ("gla_w_out", 0), ("ffn_w_in", 1), ("ffn_w_out", 1), ("ple_w_proj", 0), ("ple_w_gate", 0), ("ffn_w_in", 2), ("ffn_w_out", 2),
                  ("mla_w_in", 0), ("mla_w_uq", 0), ("mla_w_ukv", 0),
                  ("mla_w_out", 0), ("ffn_w_in", 3), ("ffn_w_out", 3), ("ple_w_proj", 1), ("ple_w_gate", 1)]

    def conv_gen(self, order, CH, engs, q="sp"):
        A, P = self.A, self.P
        stf = [A.alloc([128, CH], F32) for _ in range(2)]
        stb = [A.alloc([128, CH], BF16) for _ in range(2)]
        sfB = [Buf(), Buf()]
        sbB = [Buf(), Buf()]
        for name, idx in order:
            self.wB[(name, idx)] = []
        chunks = []
        for name, idx in order:
            shp = WEIGHT_SHAPES[name]
            src = self.win[name]; dst = self.wbf[name]
            if len(shp) == 4:
                src = src[idx // 2, idx % 2]; dst = dst[idx // 2, idx % 2]
            else:
                src = src[idx]; dst = dst[idx]
            R, C = shp[-2], shp[-1]
            sv = src.rearrange("(p a) c -> p (a c)", p=128)
            dv = dst.rearrange("(p a) c -> p (a c)", p=128)
            n = R * C // 128
            for c0 in range(0, n, CH):
                cn = min(CH, n - c0)
                chunks.append((name, idx, sv[:, c0:c0 + cn], dv[:, c0:c0 + cn], cn))

        def load(k):
            name, idx, svc, dvc, cn = chunks[k]
            self.dma(stf[k % 2][:, 0:cn], svc, writes=[sfB[k % 2]], q=q)
        load(0)
        for k, (name, idx, svc, dvc, cn) in enumerate(chunks):
            i = k % 2
            eng = engs[k % len(engs)]
            if eng == "act":
                P.op("act", lambda e, i=i, cn=cn: e.copy(out=stb[i][:, 0:cn], in_=stf[i][:, 0:cn]), reads=[sfB[i]], writes=[sbB[i]])
            else:
                P.op(eng, lambda e, i=i, cn=cn: e.tensor_copy(out=stb[i][:, 0:cn], in_=stf[i][:, 0:cn]), reads=[sfB[i]], writes=[sbB[i]])
            if k + 1 < len(chunks):
                load(k + 1)
            b = Buf()
            self.dma(dvc, stb[i][:, 0:cn], reads=[sbB[i]], writes=[b], q=q)
            self.wB[(name, idx)].append(b)
            yield

    def ring_init(self, nslots=6):
        self.ring_n = nslots
        self.ring_slots = [self.A.alloc([128, 4096], BF16) for _ in range(nslots)]
        self.ring_bufs = [Buf("ring%d" % i) for i in range(nslots)]
        self.ring_descs = []
        self.ring_issued = 0
        self.ring_cur = 0

    def ring_view(self, slot, shape):
        n = int(np.prod(shape[1:]))
        v = self.ring_slots[slot][:, 0:n]
        if len(shape) == 3:
            v = v.rearrange("p (a b) -> p a b", a=shape[1])
        elif len(shape) == 4:
            v = v.rearrange("p (a b c) -> p a b c", a=shape[1], b=shape[2])
        return v

    def ring_next(self):
        while self.ring_issued < len(self.ring_descs) and self.ring_issued < self.ring_cur + self.ring_n - 2:
            k = self.ring_issued
            src, shape, bl = self.ring_descs[k]
            assert int(np.prod(shape[1:])) <= 4096, shape
            rv = self.ring_view(k % self.ring_n, shape)
            if len(shape) == 4:
                for g in range(shape[2]):
                    self.dma(rv[:, :, g, :], src[:, :, g, :], reads=bl, writes=[self.ring_bufs[k % self.ring_n]], join=(g > 0))
            else:
                self.dma(rv, src, reads=bl, writes=[self.ring_bufs[k % self.ring_n]])
            self.ring_issued += 1
        k = self.ring_cur
        self.ring_cur += 1
        assert k < self.ring_issued
        return self.ring_view(k % self.ring_n, self.ring_descs[k][1]), self.ring_bufs[k % self.ring_n]

    def alloc_token_bufs(self):
        A = self.A
        self.xt = [A.alloc([128, 4, D], F32) for _ in range(2)]
        self.xB = [[Buf() for s in range(4)] for _ in range(2)]
        self.xn = A.alloc([128, 4, D], BF16)
        self.xnB = [Buf() for s in range(4)]
        self.hT = A.alloc([128, 8, 512], BF16)
        self.hTB = [Buf() for s in range(4)]
        self.aT = A.alloc([128, NJ, 512], BF16)
        self.aTB = [Buf() for j in range(NJ)]
        self.sg = [A.alloc([128, 512], F32) for _ in range(2)]
        self.sgB = [Buf(), Buf()]
        self.sg_i = 0
        self.ss = A.alloc([128, 8], F32)
        self.ssB = [Buf() for _ in range(4)]

    def rms_pre(self, s, src, srcB_s, ncols, ss=None, ssB=None, junk=None, junkB=None):
        P = self.P
        ss = self.ss if ss is None else ss
        ssB = self.ssB if ssB is None else ssB
        junk = self.xn[:, s, 0:ncols] if junk is None else junk
        junkB = self.xnB[s] if junkB is None else junkB
        P.op("pool", lambda e: e.memset(ss[:, s:s + 1], 0.0), writes=[ssB[s]])
        self.act(junk, src, AF.Square, reads=[srcB_s], writes=[junkB, ssB[s]], accum_out=ss[:, s:s + 1])
        self.act(ss[:, 4 + s:5 + s], ss[:, s:s + 1], AF.Ln, reads=[ssB[s]], writes=[ssB[s]], scale=1.0 / ncols, bias=EPS)
        self.act(ss[:, 4 + s:5 + s], ss[:, 4 + s:5 + s], AF.Exp, reads=[ssB[s]], writes=[ssB[s]], scale=-0.5)

    def norm_pre(self, s, src, srcB_s, ncols, ss=None, ssB=None, xn=None, xnB=None):
        ss = self.ss if ss is None else ss
        ssB = self.ssB if ssB is None else ssB
        xn = self.xn[:, s, 0:ncols] if xn is None else xn
        xnB = self.xnB[s] if xnB is None else xnB
        self.rms_pre(s, src, srcB_s, ncols, ss, ssB, xn, xnB)
        self.act(xn, src, AF.Copy, reads=[srcB_s, ssB[s]], writes=[xnB], scale=ss[:, 4 + s:5 + s])

    def norm_post(self, s, gain, ncols, dstT, dstB_s, xn=None, xnB=None):
        nch = ncols // 128
        xn = self.xn[:, s, 0:ncols] if xn is None else xn
        xnB = self.xnB[s] if xnB is None else xnB
        pt, ptB = self.pst()
        for c in range(nch):
            self.tp(pt[:, c * 128:(c + 1) * 128], xn[:, c * 128:(c + 1) * 128], reads=[xnB], writes=[ptB])
        self.tt("dve", dstT[:, 0:nch, s * 128:(s + 1) * 128], pt[:, 0:nch * 128].rearrange("p (c t) -> p c t", c=nch),
                gain[:].unsqueeze(2).to_broadcast([128, nch, 128]), ALU.mult, reads=[ptB, self.constB], writes=[dstB_s])

    def xnorm_pre(self, xi, s):
        self.norm_pre(s, self.xt[xi][:, s, :], self.xB[xi][s], D)

    def xnorm_post(self, gain):
        for s in range(4):
            self.norm_post(s, gain, D, self.hT, self.hTB[s])

    def ffn_descs(self, l, a):
        d = []
        idx = l * 2 + a
        win = self.wbf["ffn_w_in"][l, a].rearrange("(c p) (g f) -> p c g f", p=128, g=2)
        for j0 in range(0, NJ, 2):
            nj = 2
            d.append((win[:, :, :, j0 * 128:(j0 + nj) * 128], [128, 8, 2, nj * 128], self.wB[("ffn_w_in", idx)]))
        wout = self.wbf["ffn_w_out"][l, a].rearrange("(j p) d -> p j d", p=128)
        for half in range(2):
            for j0 in range(0, NJ, 8):
                nj = min(8, NJ - j0)
                d.append((wout[:, j0:j0 + nj, half * 512:(half + 1) * 512], [128, nj, 512], self.wB[("ffn_w_out", idx)]))
        return d

    def ffn(self, xi, l, a, after=None, mid=None):
        xt, xB = self.xt[xi], self.xB[xi]
        self.xnorm_post(self.gfm[("ffn", l, a)])
        hT, hTB = self.hT, self.hTB
        for j0 in range(0, NJ, 2):
            nj = 2
            w, wb = self.ring_next()
            for jj in range(nj):
                j = j0 + jj
                pg, pgB = self.ps()
                for c in range(8):
                    self.mm(pg[:], w[:, c, 0, jj * 128:(jj + 1) * 128], hT[:, c, :], c == 0, c == 7, reads=[wb] + hTB, writes=[pgB])
                pu, puB = self.ps()
                for c in range(8):
                    self.mm(pu[:], w[:, c, 1, jj * 128:(jj + 1) * 128], hT[:, c, :], c == 0, c == 7, reads=[wb] + hTB, writes=[puB])
                si = self.sg_i % 2
                self.sg_i += 1
                self.act(self.sg[si][:], pg[:], AF.Silu, reads=[pgB], writes=[self.sgB[si]])
                self.tt("dve", self.aT[:, j, :], self.sg[si][:], pu[:], ALU.mult, reads=[self.sgB[si], puB], writes=[self.aTB[j]])
        if mid is not None:
            mid()
        for half in range(2):
            ws = [self.ring_next() for _ in range(3)]
            for s in range(4):
                py, pyB = self.ps()
                for j in range(NJ):
                    w, wb = ws[j // 8]
                    self.mm(py[:], self.aT[:, j, s * 128:(s + 1) * 128], w[:, j % 8, :],
                            j == 0, j == NJ - 1, reads=[self.aTB[j], wb], writes=[pyB])
                xs = xt[:, s, half * 512:(half + 1) * 512]
                self.stt("dve", xs, py[:], 0.5, xs, ALU.mult, ALU.add, reads=[pyB, xB[s]], writes=[xB[s]])
                if half == 1 and after is not None:
                    after(s)

    def proj_add(self, xi, srcT, srcTB, after=None):
        xt, xB = self.xt[xi], self.xB[xi]
        for half in range(2):
            w, wb = self.ring_next()
            for s in range(4):
                py, pyB = self.ps()
                for c in range(8):
                    self.mm(py[:], srcT[:, c, s * 128:(s + 1) * 128], w[:, c, :], c == 0, c == 7,
                            reads=list(srcTB) + [wb], writes=[pyB])
                xs = xt[:, s, half * 512:(half + 1) * 512]
                self.tt("dve", xs, py[:], xs, ALU.add, reads=[pyB, xB[s]], writes=[xB[s]])
                if half == 1 and after is not None:
                    after(s)

    def ple_descs(self, l):
        wg = self.wbf["ple_w_gate"][l].rearrange("(c p) d -> p c d", p=128)
        return [(self.wbf["ple_w_proj"][l].rearrange("(c p) d -> p c d", p=128), [128, 2, D], self.wB[("ple_w_proj", l)])] + \
               [(wg[:, :, hf * 512:(hf + 1) * 512], [128, 8, 512], self.wB[("ple_w_gate", l)]) for hf in range(2)]

    def ple_load(self, l, t):
        t0 = t * 512
        self.dma(self.pt[:], self.pin[l, t0:t0 + 512, :].rearrange("(s p) d -> p s d", p=128), writes=[self.ptB])

    def ple_prep(self, l, t):
        self.P.op("act", lambda e: e.copy(out=self.ptb[:], in_=self.pt[:]), reads=[self.ptB], writes=[self.ptbB])
        for s in range(4):
            pt, ptB = self.pst()
            for c in range(2):
                self.tp(pt[:, c * 128:(c + 1) * 128], self.ptb[:, s, c * 128:(c + 1) * 128], reads=[self.ptbB], writes=[ptB])
            self.copy_any(self.pT[:, :, s * 128:(s + 1) * 128], pt[:, 0:256].rearrange("p (c t) -> p c t", c=2), reads=[ptB], writes=[self.pTB])

    def ple(self, xi, l, t, after=None):
        xt, xB = self.xt[xi], self.xB[xi]
        t0 = t * 512
        self.xnorm_post(self.gfm[("ple", l)])
        wp, wpB = self.ring_next()
        for half in range(2):
            wg, wgB = self.ring_next()
            for s in range(4):
                pg, pgB = self.ps()
                for c in range(8):
                    self.mm(pg[:], self.hT[:, c, s * 128:(s + 1) * 128], wg[:, c, :], c == 0, c == 7,
                            reads=self.hTB + [wgB], writes=[pgB])
                pp, ppB = self.ps()
                for c in range(2):
                    self.mm(pp[:], self.pT[:, c, s * 128:(s + 1) * 128], wp[:, c, half * 512:(half + 1) * 512], c == 0, c == 1,
                            reads=[self.pTB, wpB], writes=[ppB])
                si = self.sg_i % 2
                self.sg_i += 1
                self.act(self.sg[si][:], pg[:], AF.Sigmoid, reads=[pgB], writes=[self.sgB[si]])
                self.tt("dve", self.sg[si][:], self.sg[si][:], pp[:], ALU.mult, reads=[self.sgB[si], ppB], writes=[self.sgB[si]])
                xs = xt[:, s, half * 512:(half + 1) * 512]
                self.tt("pool", xs, xs, self.sg[si][:], ALU.add, reads=[self.sgB[si], xB[s]], writes=[xB[s]])
                if half == 1 and after is not None:
                    after(s)

    def final_pre(self, xi, s):
        self.rms_pre(s, self.xt[xi][:, s, :], self.xB[xi][s], D)

    def final_norm(self, xi):
        xt, xB = self.xt[xi], self.xB[xi]
        for s in range(4):
            self.stt("dve", xt[:, s, :], xt[:, s, :], self.ss[:, 4 + s:5 + s], self.g_final[:], ALU.mult, ALU.mult,
                     reads=[xB[s], self.ssB[s], self.constB], writes=[xB[s]])

    def gla_proj_descs(self):
        w = self.wbf["gla_w_in"][0].rearrange("(c p) f -> p c f", p=128)
        bl = self.wB[("gla_w_in", 0)]
        return [(w[:, :, 3072:3104], [128, 8, 32], bl)] + [(w[:, :, i * 512:(i + 1) * 512], [128, 8, 512], bl) for i in range(6)]

    def alloc_gla_proj(self):
        A = self.A
        self.sp = [A.alloc([128, 4, 512], F32) for _ in range(2)]
        self.spB = [[Buf() for s in range(4)] for _ in range(2)]
        self.etmp = [A.alloc([128, 512], F32) for _ in range(4)]
        self.etmpB = [Buf() for _ in range(4)]
        self.et_i = 0
        self.qd_st = A.alloc([128, 2, 4, 512], BF16); self.qdB = Buf()
        self.ki_st = A.alloc([128, 2, 4, 512], BF16); self.kiB = Buf()
        self.ke_st = A.alloc([128, 2, 4, 512], BF16); self.keB = Buf()
        self.dec_st = A.alloc([128, 2, 4, 4], F32); self.decB = Buf()
        self.v_st = A.alloc([128, 4, D], BF16); self.vB = Buf()
        self.sr_st = A.alloc([128, 4, D], BF16); self.srB = Buf()

    def etmp_next(self):
        i = self.et_i % 4
        self.et_i += 1
        return self.etmp[i], self.etmpB[i]

    def gla_proj(self, xi, t):
        P = self.P
        xt, xB = self.xt[xi], self.xB[xi]
        t0 = t * 512
        self.xnorm_post(self.gfm[("mix", 0)])
        hT, hTB = self.hT, self.hTB
        wlo, wloB = self.ring_next()
        for d in range(2):
            pl, plB = self.ps()
            for c in range(8):
                self.mm(pl[0:16, :], wlo[:, c, d * 16:(d + 1) * 16], hT[:, c, :], c == 0, c == 7, reads=[wloB] + hTB, writes=[plB])
            self.copy_any(self.lo_aug[d][0:16, :], pl[0:16, :], reads=[plB], writes=[self.loB[d]])
            for s in range(4):
                pz, pzB = self.ps()
                self.mm(pz[:], self.lo_aug[d][0:17, s * 128:(s + 1) * 128], self.wup[d][0:17, :], True, True,
                        reads=[self.loB[d], self.constB], writes=[pzB])
                et, etB = self.etmp_next()
                self.act(et[:], pz[:], AF.Exp, reads=[pzB], writes=[etB], scale=-1.0)
                self.act(self.sp[d][:, s, :], et[:], AF.Ln, reads=[etB], writes=[self.spB[d][s]], bias=1.0)
        wq, wqB = self.ring_next()
        wk, wkB = self.ring_next()
        for h in range(4):
            pq, pqB = self.ps()
            for c in range(8):
                self.mm(pq[:], wq[:, c, h * 128:(h + 1) * 128], hT[:, c, :], c == 0, c == 7, reads=[wqB] + hTB, writes=[pqB])
            pk, pkB = self.ps()
            for c in range(8):
                self.mm(pk[:], wk[:, c, h * 128:(h + 1) * 128], hT[:, c, :], c == 0, c == 7, reads=[wkB] + hTB, writes=[pkB])
            for d in range(2):
                pb, pbB = self.ps()
                for s in range(4):
                    self.mm(pb[:, s * 128:(s + 1) * 128], self.sp[d][:, s, h * 128:(h + 1) * 128], self.tri[:, d, :], True, True,
                            reads=[self.spB[d][s], self.constB], writes=[pbB])
                eb, ebB = self.etmp_next()
                self.act(eb[:], pb[:], AF.Exp, reads=[pbB], writes=[ebB])
                ei, eiB = self.etmp_next()
                self.act(ei[:], pb[:], AF.Exp, reads=[pbB], writes=[eiB], scale=-1.0)
                self.stt("dve", self.qd_st[:, d, h, :], pq[:], 128.0 ** -0.5, eb[:], ALU.mult, ALU.mult, reads=[pqB, ebB], writes=[self.qdB])
                self.tt("dve", self.ki_st[:, d, h, :], pk[:], ei[:], ALU.mult, reads=[pkB, eiB], writes=[self.kiB])
                col = 127 if d == 0 else 0
                ebv = eb[:].rearrange("p (s t) -> p s t", s=4)[:, :, col]
                P.op("pool", lambda e, d=d, h=h, ebv=ebv: e.tensor_copy(out=self.dec_st[:, d, h, :], in_=ebv), reads=[ebB], writes=[self.decB])
        for s in range(4):
            pk, pkB = self.ps()
            for c in range(8):
                self.mm(pk[:], hT[:, c, s * 128:(s + 1) * 128], wk[:, c, :], c == 0, c == 7, reads=[wkB] + hTB, writes=[pkB])
            for d in range(2):
                pe_, peB = self.ps()
                self.mm(pe_[:], self.tri[:, 2 + d, :], self.sp[d][:, s, :], True, True, reads=[self.spB[d][s], self.constB], writes=[peB])
                ee, eeB = self.etmp_next()
                self.act(ee[:], pe_[:], AF.Exp, reads=[peB], writes=[eeB])
                self.tt("dve", self.ke_st[:, d, s, :], pk[:], ee[:], ALU.mult, reads=[pkB, eeB], writes=[self.keB])
        for half in range(2):
            wv, wvB = self.ring_next()
            for s in range(4):
                pv, pvB = self.ps()
                for c in range(8):
                    self.mm(pv[:], hT[:, c, s * 128:(s + 1) * 128], wv[:, c, :], c == 0, c == 7, reads=[wvB] + hTB, writes=[pvB])
                self.copy_any(self.v_st[:, s, half * 512:(half + 1) * 512], pv[:], reads=[pvB], writes=[self.vB])
        for half in range(2):
            wr, wrB = self.ring_next()
            for s in range(4):
                pr, prB = self.ps()
                for c in range(8):
                    self.mm(pr[:], hT[:, c, s * 128:(s + 1) * 128], wr[:, c, :], c == 0, c == 7, reads=[wrB] + hTB, writes=[prB])
                self.act(self.sr_st[:, s, half * 512:(half + 1) * 512], pr[:], AF.Silu, reads=[prB], writes=[self.srB])
        g2 = self.glaB2
        self.dma(self.QD.rearrange("d h p t -> p (d h) t")[:, :, t0:t0 + 512], self.qd_st[:].rearrange("p d h t -> p (d h) t"),
                 reads=[self.qdB], writes=[g2[0][t]])
        self.dma(self.KI.rearrange("d h p t -> p (d h) t")[:, :, t0:t0 + 512], self.ki_st[:].rearrange("p d h t -> p (d h) t"),
                 reads=[self.kiB], writes=[g2[1][t]])
        for d in range(2):
            self.dma(self.KE[d, t0:t0 + 512, :].rearrange("(s p) f -> p s f", p=128), self.ke_st[:, d, :, :], reads=[self.keB],
                     writes=[g2[2][t]], join=(d > 0))
        self.dma(self.DEC.rearrange("d h p n -> p (d h) n")[:, :, t * 4:t * 4 + 4], self.dec_st[:].rearrange("p d h n -> p (d h) n"),
                 reads=[self.decB], writes=[g2[3][t]])
        self.dma(self.VG[t0:t0 + 512, :].rearrange("(s p) f -> p s f", p=128), self.v_st[:], reads=[self.vB], writes=[g2[4][t]])
        self.dma(self.SR[t0:t0 + 512, :].rearrange("(s p) f -> p s f", p=128), self.sr_st[:], reads=[self.srB], writes=[g2[5][t]])

    def gla_scan(self, bg=None, bg_every=5):
        A, P = self.A, self.P
        Lmax = max(self.seqs)
        NBm = Lmax // 128
        qd = [[A.alloc([128, Lmax], BF16) for _ in range(2)] for _ in range(2)]
        ki = [[A.alloc([128, Lmax], BF16) for _ in range(2)] for _ in range(2)]
        ke = [[A.alloc([128, NBm, 128], BF16) for _ in range(2)] for _ in range(2)]
        dec = [[A.alloc([128, NBm], F32) for _ in range(2)] for _ in range(2)]
        inB = [Buf("scan_in0"), Buf("scan_in1")]
        vv2 = [A.alloc([128, NBm, 256], BF16) for _ in range(2)]
        sr = A.alloc([128, NBm, 256], BF16); srB = Buf()
        oacc = A.alloc([128, NBm, 256], F32)
        oB = [Buf() for _ in range(NBm)]
        S32 = [A.alloc([128, 256], F32) for _ in range(2)]
        S32B = [Buf(), Buf()]
        Sbf = [[A.alloc([128, 256], BF16) for _ in range(3)] for _ in range(2)]
        SbfB = [[Buf(), Buf(), Buf()], [Buf(), Buf(), Buf()]]
        atsb2 = [A.alloc([128, 2, 128], BF16) for _ in range(2)]
        atB = [Buf() for _ in range(2)]
        ssn = A.alloc([128, 2 * NBm], F32)
        ssB = Buf()
        junk = Sbf[0][0]
        junkB = SbfB[0][0]
        g2 = self.glaB2
        units = [(si, h) for si in range(len(self.seqs)) for h in range(4)]

        def load_big(u):
            si, h = units[u]
            L = self.seqs[si]; off = self.offs[si]; NB = L // 128
            tiles = list(range(off // 512, (off + L) // 512))
            b = u % 2
            first = True
            for d in range(2):
                self.dma(qd[b][d][:, 0:L], self.QD[d, h, :, off:off + L], reads=[g2[0][t] for t in tiles], writes=[inB[b]], join=not first)
                first = False
                self.dma(ki[b][d][:, 0:L], self.KI[d, h, :, off:off + L], reads=[g2[1][t] for t in tiles], writes=[inB[b]], join=True)
                self.dma(ke[b][d][:, 0:NB, :], self.KE[d, off:off + L, h * 128:(h + 1) * 128].rearrange("(n p) f -> p n f", p=128),
                         reads=[g2[2][t] for t in tiles], writes=[inB[b]], join=True)
                self.dma(dec[b][d][:, 0:NB], self.DEC[d, h, :, off // 128:off // 128 + NB], reads=[g2[3][t] for t in tiles], writes=[inB[b]], join=True)
            self.dma(vv2[b][:, 0:NB, :], self.VG[off:off + L, h * 256:(h + 1) * 256].rearrange("(n p) f -> p n f", p=128),
                     reads=[g2[4][t] for t in tiles], writes=[inB[b]], join=True)

        it = 0
        load_big(0)
        for u, (si, h) in enumerate(units):
            L = self.seqs[si]; off = self.offs[si]; NB = L // 128
            tiles = list(range(off // 512, (off + L) // 512))
            b = u % 2
            vv = vv2[b]; vvB = inB[b]
            if u + 1 < len(units):
                load_big(u + 1)
            touched = set()
            chunk = lambda i, d: i if d == 0 else NB - 1 - i
            pend = {}
            for i in range(NB + 1):
                if i < NB:
                    pa, paB = self.ps()
                    for d in range(2):
                        n = chunk(i, d)
                        blk = slice(n * 128, (n + 1) * 128)
                        self.mm(pa[:, d * 128:(d + 1) * 128], ki[b][d][:, blk], qd[b][d][:, blk], True, True, reads=[inB[b]], writes=[paB])
                    ai = i % 2
                    self.tt("dve", atsb2[ai][:], pa[:, 0:256].rearrange("p (d t) -> p d t", d=2), self.mask[:], ALU.mult,
                            reads=[paB, self.constB], writes=[atB[ai]])
                    if i < NB - 1:
                        pd, pdB = self.ps()
                        for d in range(2):
                            n = chunk(i, d)
                            self.mm(pd[:, d * 256:(d + 1) * 256], ke[b][d][:, n, :], vv[:, n, :], True, True, reads=[inB[b], vvB], writes=[pdB])
                        for d in range(2):
                            n = chunk(i, d)
                            pdv = pd[:, d * 256:(d + 1) * 256]
                            if i == 0:
                                P.op("dve", lambda e, d=d, pdv=pdv: e.tensor_copy(out=S32[d][:], in_=pdv), reads=[pdB], writes=[S32B[d]])
                            else:
                                self.stt("dve", S32[d][:], S32[d][:], dec[b][d][:, n:n + 1], pdv, ALU.mult, ALU.add,
                                         reads=[S32B[d], pdB, inB[b]], writes=[S32B[d]])
                            P.op("act", lambda e, d=d, i=i: e.copy(out=Sbf[d][i % 3][:], in_=S32[d][:]), reads=[S32B[d]], writes=[SbfB[d][i % 3]])
                if i >= 1:
                    j = i - 1
                    ai = j % 2
                    po, poB = self.ps()
                    for d in range(2):
                        n = chunk(j, d)
                        blk = slice(n * 128, (n + 1) * 128)
                        pov = po[:, d * 256:(d + 1) * 256]
                        self.mm(pov, atsb2[ai][:, d, :], vv[:, n, :], True, j == 0, reads=[atB[ai], vvB], writes=[poB])
                        if j > 0:
                            self.mm(pov, qd[b][d][:, blk], Sbf[d][(j - 1) % 3][:], False, True, reads=[inB[b], SbfB[d][(j - 1) % 3]], writes=[poB])
                    for d in range(2):
                        n = chunk(j, d)
                        pov = po[:, d * 256:(d + 1) * 256]
                        if n not in touched:
                            touched.add(n)
                            P.op("dve", lambda e, n=n, pov=pov: e.tensor_copy(out=oacc[:, n, :], in_=pov), reads=[poB], writes=[oB[n]])
                        else:
                            self.tt("dve", oacc[:, n, :], oacc[:, n, :], pov, ALU.add, reads=[poB, oB[n]], writes=[oB[n]])
                it += 1
                if bg is not None and it % bg_every == 0:
                    next(bg, None)
            self.dma(sr[:, 0:NB, :], self.SR[off:off + L, h * 256:(h + 1) * 256].rearrange("(n p) f -> p n f", p=128),
                     reads=[g2[5][t] for t in tiles], writes=[srB])
            order_n = []
            for i in range(NB):
                for n in (i, NB - 1 - i):
                    if n not in order_n:
                        order_n.append(n)
            P.op("pool", lambda e: e.memset(ssn[:], 0.0), writes=[ssB])
            for n in order_n:
                self.act(junk[:], oacc[:, n, :], AF.Square, reads=[oB[n]], writes=[junkB, ssB], accum_out=ssn[:, n:n + 1])
            self.act(ssn[:, NBm:NBm + NB], ssn[:, 0:NB], AF.Sqrt, reads=[ssB], writes=[ssB], scale=1.0 / 256, bias=EPS)
            P.op("dve", lambda e, NB=NB: e.reciprocal(out=ssn[:, NBm:NBm + NB], in_=ssn[:, NBm:NBm + NB]), reads=[ssB], writes=[ssB])
            for n in order_n:
                self.stt("dve", oacc[:, n, :], oacc[:, n, :], ssn[:, NBm + n:NBm + n + 1], self.g_out[:], ALU.mult, ALU.mult,
                         reads=[oB[n], ssB, self.constB], writes=[oB[n]])
                self.tt("pool", sr[:, n, :], oacc[:, n, :], sr[:, n, :], ALU.mult, reads=[oB[n], srB], writes=[srB])
            self.dma(self.OG[off:off + L, h * 256:(h + 1) * 256].rearrange("(n p) f -> p n f", p=128), sr[:, 0:NB, :],
                     reads=[srB], writes=[self.ogB[t][h] for t in tiles])
        if bg is not None:
            for _ in bg:
                pass

    def gla_out_descs(self):
        w = self.wbf["gla_w_out"][0].rearrange("(c p) d -> p c d", p=128)
        return [(w[:, :, hf * 512:(hf + 1) * 512], [128, 8, 512], self.wB[("gla_w_out", 0)]) for hf in range(2)]

    def gla_out_load(self, t):
        t0 = t * 512
        self.dma(self.ogt[t % 2][:], self.OG[t0:t0 + 512, :].rearrange("(s p) f -> p s f", p=128), reads=self.ogB[t], writes=[self.ogtB[t % 2]])

    def gla_out(self, xi, t, after=None):
        og, ogB_ = self.ogt[t % 2], self.ogtB[t % 2]
        for s in range(4):
            pt, ptB = self.pst()
            for c in range(8):
                self.tp(pt[:, c * 128:(c + 1) * 128], og[:, s, c * 128:(c + 1) * 128], reads=[ogB_], writes=[ptB])
            self.copy_any(self.hT[:, :, s * 128:(s + 1) * 128], pt[:].rearrange("p (c t) -> p c t", c=8), reads=[ptB], writes=[self.hTB[s]])
        if t + 1 < self.NTILE:
            self.gla_out_load(t + 1)
        self.proj_add(xi, self.hT, self.hTB, after)

    def mla_proj_descs(self):
        wi = self.wbf["mla_w_in"][0].rearrange("(c p) f -> p c f", p=128)
        wq = self.wbf["mla_w_uq"][0].rearrange("(c p) f -> p c f", p=128)
        return [(wi[:, :, 0:384], [128, 8, 384], self.wB[("mla_w_in", 0)]), (wi[:, :, 384:704], [128, 8, 320], self.wB[("mla_w_in", 0)]),
                (wq[:, :, 0:768], [128, 3, 768], self.wB[("mla_w_uq", 0)]), (wq[:, :, 768:1536], [128, 3, 768], self.wB[("mla_w_uq", 0)]),
                (self.wbf["mla_w_ukv"][0].rearrange("(c p) f -> p c f", p=128), [128, 2, 2048], self.wB[("mla_w_ukv", 0)])]

    def alloc_mla_proj(self):
        A = self.A
        self.cq = A.alloc([128, 4, 384], F32); self.cqB = [Buf() for _ in range(4)]
        self.ckv = A.alloc([128, 4, 256], F32); self.ckvB = [Buf() for _ in range(4)]
        self.krs = A.alloc([128, 4, 64], F32); self.krsB = Buf()
        self.cqT = A.alloc([128, 3, 512], BF16); self.cqTB = [Buf() for _ in range(4)]
        self.ckvT = A.alloc([128, 2, 512], BF16); self.ckvTB = [Buf() for _ in range(4)]
        self.cs = A.alloc([128, 4, 64], F32); self.csB = Buf()
        self.ssq = A.alloc([128, 8], F32); self.ssqB = [Buf() for _ in range(4)]
        self.sskv = A.alloc([128, 8], F32); self.sskvB = [Buf() for _ in range(4)]
        self.xnB2 = [Buf() for _ in range(4)]
        self.rt = [A.alloc([128, 8, 32], F32) for _ in range(4)]; self.rtB = [Buf() for _ in range(4)]
        self.qr_tok = A.alloc([128, 4, 512], BF16); self.qrtB = [Buf() for _ in range(4)]
        self.kr_tok = A.alloc([128, 4, 128], BF16); self.krtB = Buf()
        self.qn_st = self.aT[:, 0:8, :]
        self.qr_st = A.alloc([128, 4, 512], BF16); self.qrB = Buf()
        self.kn_st = self.aT[:, 8:16, :]
        self.kr2_st = A.alloc([128, 512], BF16); self.kr2B = Buf()
        self.vm_st = A.alloc([128, 4, D], BF16); self.vmB = Buf()

    def rope(self, x1, x2, s, nh, out1, out2, reads, writes):
        cos = self.cs[:, s, 0:32].unsqueeze(1).to_broadcast([128, nh, 32])
        sin = self.cs[:, s, 32:64].unsqueeze(1).to_broadcast([128, nh, 32])
        r = self.rt
        rB = self.rtB
        rd = list(reads) + [self.csB]
        v = lambda i: r[i][:, 0:nh, :]
        self.tt("dve", v(0), x1, cos, ALU.mult, reads=rd, writes=[rB[0]])
        self.tt("dve", v(1), x2, sin, ALU.mult, reads=rd, writes=[rB[1]])
        self.tt("dve", v(2), x2, cos, ALU.mult, reads=rd, writes=[rB[2]])
        self.tt("dve", v(3), x1, sin, ALU.mult, reads=rd, writes=[rB[3]])
        for o1, o2 in zip(out1, out2):
            self.tt("pool", o1, v(0), v(1), ALU.subtract, reads=[rB[0], rB[1]], writes=writes)
            self.tt("pool", o2, v(2), v(3), ALU.add, reads=[rB[2], rB[3]], writes=writes)

    def mla_proj_load(self, t):
        t0 = t * 512
        si = max(i for i in range(len(self.seqs)) if self.offs[i] <= t0)
        pos0 = t0 - self.offs[si]
        self.dma(self.cs[:], self.c_rope[pos0:pos0 + 512, :].rearrange("(s p) f -> p s f", p=128), writes=[self.csB])

    def mla_proj(self, xi, t):
        xt, xB = self.xt[xi], self.xB[xi]
        t0 = t * 512
        self.xnorm_post(self.gfm[("mix", 1)])
        hT, hTB = self.hT, self.hTB
        win, winB = self.ring_next()
        win2, win2B = self.ring_next()
        for s in range(4):
            p1, p1B = self.ps()
            for c in range(8):
                self.mm(p1[:, 0:384], hT[:, c, s * 128:(s + 1) * 128], win[:, c, :], c == 0, c == 7, reads=[winB] + hTB, writes=[p1B])
            self.copy_any(self.cq[:, s, :], p1[:, 0:384], reads=[p1B], writes=[self.cqB[s]])
            self.norm_pre(s, self.cq[:, s, :], self.cqB[s], 384, self.ssq, self.ssqB, self.xn[:, s, 0:384], self.xnB[s])
            p2, p2B = self.ps()
            for c in range(8):
                self.mm(p2[:, 0:320], hT[:, c, s * 128:(s + 1) * 128], win2[:, c, :], c == 0, c == 7, reads=[win2B] + hTB, writes=[p2B])
            self.P.op("dve", lambda e, s=s, p2=p2: e.tensor_copy(out=self.ckv[:, s, :], in_=p2[:, 0:256]), reads=[p2B], writes=[self.ckvB[s]])
            self.P.op("dve", lambda e, s=s, p2=p2: e.tensor_copy(out=self.krs[:, s, :], in_=p2[:, 256:320]), reads=[p2B], writes=[self.krsB])
            self.norm_pre(s, self.ckv[:, s, :], self.ckvB[s], 256, self.sskv, self.sskvB, self.xn[:, s, 512:768], self.xnB2[s])
        for s in range(4):
            self.norm_post(s, self.gfm["qn"], 384, self.cqT, self.cqTB[s], self.xn[:, s, 0:384], self.xnB[s])
            self.norm_post(s, self.gfm["kvn"], 256, self.ckvT, self.ckvTB[s], self.xn[:, s, 512:768], self.xnB2[s])
        wuqs = [self.ring_next(), self.ring_next()]
        for h in range(8):
            pq, pqB = self.ps()
            wuq, wuqB = wuqs[h // 4]
            hh = h % 4
            for c in range(3):
                self.mm(pq[:], wuq[:, c, hh * 192:hh * 192 + 128], self.cqT[:, c, :], c == 0, c == 2, reads=[wuqB] + self.cqTB, writes=[pqB])
            self.copy_any(self.qn_st[:, h, :], pq[:], reads=[pqB], writes=[self.aTB[h]])
        for s in range(4):
            pr, prB = self.ps()
            for g4 in range(2):
                wuq, wuqB = wuqs[g4]
                for c in range(3):
                    rhs = wuq[:, c, :].rearrange("p (h f) -> p h f", f=192)[:, :, 128:192]
                    self.mm(pr[:, g4 * 256:(g4 + 1) * 256].rearrange("p (h f) -> p h f", f=64), self.cqT[:, c, s * 128:(s + 1) * 128], rhs,
                            c == 0, c == 2, reads=[wuqB] + self.cqTB, writes=[prB])
            prv = pr[:].rearrange("p (h f) -> p h f", f=64)
            qv = self.qr_tok[:, s, :].rearrange("p (h f) -> p h f", f=64)
            self.rope(prv[:, :, 0:32], prv[:, :, 32:64], s, 8, [qv[:, :, 0:32]], [qv[:, :, 32:64]], reads=[prB], writes=[self.qrtB[s]])
            pt, ptB = self.pst()
            for m_ in range(4):
                self.tp(pt[:, m_ * 128:(m_ + 1) * 128], self.qr_tok[:, s, m_ * 128:(m_ + 1) * 128], reads=[self.qrtB[s]], writes=[ptB])
            self.copy_any(self.qr_st[:, :, s * 128:(s + 1) * 128], pt[:, 0:512].rearrange("p (c t) -> p c t", c=4), reads=[ptB], writes=[self.qrB])
        wkv, wkvB = self.ring_next()
        for h in range(8):
            pk, pkB = self.ps()
            for c in range(2):
                self.mm(pk[:], wkv[:, c, h * 256:h * 256 + 128], self.ckvT[:, c, :], c == 0, c == 1, reads=[wkvB] + self.ckvTB, writes=[pkB])
            self.copy_any(self.kn_st[:, h, :], pk[:], reads=[pkB], writes=[self.aTB[8 + h]])
        for s in range(4):
            for half in range(2):
                pv, pvB = self.ps()
                for c in range(2):
                    rhs = wkv[:, c, :].rearrange("p (h f) -> p h f", f=256)[:, 4 * half:4 * half + 4, 128:256]
                    self.mm(pv[:].rearrange("p (h f) -> p h f", f=128), self.ckvT[:, c, s * 128:(s + 1) * 128], rhs, c == 0, c == 1,
                            reads=[wkvB] + self.ckvTB, writes=[pvB])
                self.copy_any(self.vm_st[:, s, half * 512:(half + 1) * 512], pv[:], reads=[pvB], writes=[self.vmB])
            kv_ = self.krs[:, s, :].rearrange("p (h f) -> p h f", h=1)
            ko = self.kr_tok[:, s, :].rearrange("p (h f) -> p h f", h=1)
            self.rope(kv_[:, :, 0:32], kv_[:, :, 32:64], s, 1, [ko[:, :, 0:32], ko[:, :, 64:96]], [ko[:, :, 32:64], ko[:, :, 96:128]],
                      reads=[self.krsB], writes=[self.krtB])
            pt, ptB = self.pst()
            self.tp(pt[:, 0:128], self.kr_tok[:, s, :], reads=[self.krtB], writes=[ptB])
            self.copy_any(self.kr2_st[:, s * 128:(s + 1) * 128], pt[:, 0:128], reads=[ptB], writes=[self.kr2B])
        mb = self.mlaB
        self.dma(self.QN.rearrange("h p t -> p h t")[:, :, t0:t0 + 512], self.qn_st, reads=self.aTB[0:8], writes=[mb[0][t]])
        self.dma(self.QR.rearrange("h p t -> p h t")[:, :, t0:t0 + 512], self.qr_st[:], reads=[self.qrB], writes=[mb[1][t]])
        self.dma(self.KN.rearrange("h p t -> p h t")[:, :, t0:t0 + 512], self.kn_st, reads=self.aTB[8:16], writes=[mb[2][t]])
        self.dma(self.KR2[:, t0:t0 + 512], self.kr2_st[:], reads=[self.kr2B], writes=[mb[3][t]])
        self.dma(self.VM[t0:t0 + 512, :].rearrange("(s p) f -> p s f", p=128), self.vm_st[:], reads=[self.vmB], writes=[mb[4][t]])

    def mla_attn(self, bg=None):
        A, P = self.A, self.P
        m = A.mark()
        Lmax = max(self.seqs)
        NBm = Lmax // 128
        kr2 = [A.alloc([128, Lmax], BF16) for _ in range(2)]; kr2B = [Buf(), Buf()]
        qr = [A.alloc([128, Lmax], BF16) for _ in range(2)]; qrB = [Buf(), Buf()]
        kn = [A.alloc([128, Lmax], BF16) for _ in range(2)]; knB = [Buf(), Buf()]
        qn = [A.alloc([128, Lmax], BF16) for _ in range(2)]; qnB = [Buf(), Buf()]
        vv = [A.alloc([128, NBm, 128], BF16) for _ in range(2)]; vB = [Buf(), Buf()]
        ot = [A.alloc([128, Lmax], BF16) for _ in range(2)]; otB = [Buf(), Buf()]
        pT = [A.alloc([128, 512], BF16) for _ in range(4)]; pTB = [Buf() for _ in range(4)]
        rden = A.alloc([128, 512], F32); rdB = Buf()
        kra = A.alloc([128, Lmax], BF16); krb = A.alloc([128, Lmax], BF16); kraB = Buf(); krbB = Buf()
        accP = [A.alloc([128, 512], F32) for _ in range(2)]; accPB = [Buf(), Buf()]
        accD = [A.alloc([128, 512], F32) for _ in range(2)]; accDB = [Buf(), Buf()]
        ones32 = A.alloc([128, 128], F32); o32B = Buf()
        P.op("pool", lambda e: e.memset(ones32[:], 1.0), writes=[o32B])
        pending = []
        P.op("pool", lambda e: e.memset(kra[:], 0.0), writes=[kraB])
        P.op("pool", lambda e: e.memset(krb[:], 0.0), writes=[krbB])
        scale = 192.0 ** -0.5
        mb = self.mlaB
        hcount = 0
        pcount = 0
        pi = 0
        qt_count = 0
        for si, L in enumerate(self.seqs):
            off = self.offs[si]
            NB = L // 128
            NQ = L // 512
            tiles = list(range(off // 512, (off + L) // 512))
            ks = si % 2
            self.dma(kr2[ks][:, 0:L], self.KR2[:, off:off + L], reads=[mb[3][t] for t in tiles], writes=[kr2B[ks]])
            P.op("dve", lambda e, ks=ks, L=L: e.tensor_copy(out=kra[0:64, 0:L], in_=kr2[ks][0:64, 0:L]), reads=[kr2B[ks]], writes=[kraB])
            P.op("pool", lambda e, ks=ks, L=L: e.tensor_copy(out=krb[64:128, 0:L], in_=kr2[ks][64:128, 0:L]), reads=[kr2B[ks]], writes=[krbB])
            for h in range(8):
                hs = hcount % 2
                hcount += 1
                if h % 2 == 0:
                    ps_ = pcount % 2
                    pcount += 1
                    self.dma(qr[ps_][:, 0:L], self.QR[h // 2, :, off:off + L], reads=[mb[1][t] for t in tiles], writes=[qrB[ps_]])
                self.dma(kn[hs][:, 0:L], self.KN[h, :, off:off + L], reads=[mb[2][t] for t in tiles], writes=[knB[hs]])
                self.dma(qn[hs][:, 0:L], self.QN[h, :, off:off + L], reads=[mb[0][t] for t in tiles], writes=[qnB[hs]])
                self.dma(vv[hs][:, 0:NB, :], self.VM[off:off + L, h * 128:(h + 1) * 128].rearrange("(n p) f -> p n f", p=128),
                         reads=[mb[4][t] for t in tiles], writes=[vB[hs]])
                r0 = 64 * (h % 2)
                for qt in range(NQ):
                    qsl = slice(qt * 512, (qt + 1) * 512)
                    po, poB = self.psF[3 + qt_count % 2], self.psFB[3 + qt_count % 2]
                    aP, aPB = accP[qt_count % 2], accPB[qt_count % 2]
                    aD, aDB = accD[qt_count % 2], accDB[qt_count % 2]
                    qt_count += 1
                    pd, pdB = self.psT[0][:, 0:1024].bitcast(F32), self.psTB[0]
                    krx, krxB = (kra, kraB) if h % 2 == 0 else (krb, krbB)

                    sbank = [0, 1, 2, 5]

                    def qk(kb):
                        j = sbank[kb % 4]
                        psb, psB = self.psF[j], self.psFB[j]
                        ksl = slice(kb * 128, (kb + 1) * 128)
                        self.mm(psb[:], kn[hs][:, ksl], qn[hs][:, qsl], True, False, reads=[knB[hs], qnB[hs]], writes=[psB])
                        self.mm(psb[:], krx[:, ksl], qr[ps_][:, qsl], False, True, reads=[krxB, qrB[ps_]], writes=[psB])

                    def pv(kb):
                        nonlocal pi
                        j = sbank[kb % 4]
                        psb, psB = self.psF[j], self.psFB[j]
                        pt_, ptB_ = pT[pi % 4], pTB[pi % 4]
                        pi += 1
                        self.act(pt_[:], psb[:], AF.Exp, reads=[psB], writes=[ptB_], scale=scale)
                        self.mm(po[:], vv[hs][:, kb, :], pt_[:], kb == 0, kb == NB - 1, reads=[vB[hs], ptB_], writes=[poB])
                        eng, acc, accB = ("pool", aP, aPB) if kb % 2 == 0 else ("dve", aD, aDB)
                        if kb < 2:
                            P.op(eng, lambda e, acc=acc, pt_=pt_: e.tensor_copy(out=acc[:], in_=pt_[:]), reads=[ptB_], writes=[accB])
                        else:
                            self.tt(eng, acc[:], acc[:], pt_[:], ALU.add, reads=[ptB_, accB], writes=[accB])

                    qk(0)
                    if NB > 1:
                        qk(1)
                    for kb in range(NB):
                        if kb + 2 < NB:
                            qk(kb + 2)
                        pv(kb)
                        if kb == min(3, NB - 1) and pending:
                            pending.pop()()
                    def make_ep(aP=aP, aPB=aPB, aD=aD, aDB=aDB, po=po, poB=poB, ot_ap=ot[hs][:, qsl], otB_h=otB[hs], pd=pd, pdB=pdB,
                                last=(qt == NQ - 1), h=h, hs=hs, off=off, L=L, tiles=tiles):
                        def ep():
                            self.tt("dve", aD[:], aD[:], aP[:], ALU.add, reads=[aPB, aDB], writes=[aDB])
                            self.mm(pd, ones32[:], aD[:], True, True, reads=[o32B, aDB], writes=[pdB])
                            P.op("dve", lambda e: e.reciprocal(out=rden[:], in_=pd), reads=[pdB], writes=[rdB])
                            self.tt("dve", ot_ap, po[:], rden[:], ALU.mult, reads=[poB, rdB], writes=[otB_h])
                            if last:
                                self.dma(self.OT[h, :, off:off + L], ot[hs][:, 0:L], reads=[otB_h], writes=[self.otB[t][h] for t in tiles])
                        return ep
                    pending.append(make_ep())
                    if bg is not None and qt_count % 3 == 0:
                        next(bg, None)
        while pending:
            pending.pop()()
        if bg is not None:
            for _ in bg:
                pass
        A.reset(m)

    def mla_out_descs(self):
        w = self.wbf["mla_w_out"][0].rearrange("(c p) d -> p c d", p=128)
        return [(w[:, :, hf * 512:(hf + 1) * 512], [128, 8, 512], self.wB[("mla_w_out", 0)]) for hf in range(2)]

    def mla_out_load(self, t):
        t0 = t * 512
        self.dma(self.ott[t % 2][:], self.OT.rearrange("h p t -> p h t")[:, :, t0:t0 + 512], reads=self.otB[t], writes=[self.ottB[t % 2]])

    def mla_out(self, xi, t, after=None):
        self.proj_add(xi, self.ott[t % 2], [self.ottB[t % 2]], after)

    def token_phase(self, src, ops, descs_fn, tile_loads=(), next_loads=(), first_loads=()):
        for t in range(self.NTILE):
            self.ring_descs.extend(descs_fn())
        srcap = self.xin if src == "xin" else self.y

        def load(t):
            rd = [] if src == "xin" else [self.yB[t]]
            self.dma(self.xt[t % 2][:], srcap[t * 512:t * 512 + 512, :].rearrange("(s p) d -> p s d", p=128), reads=rd, writes=self.xB[t % 2])
        load(0)
        for f in list(next_loads) + list(first_loads):
            f(0)
        for t in range(self.NTILE):
            xi = t % 2
            t0 = t * 512
            for f in tile_loads:
                f(t)
            if t + 1 < self.NTILE:
                load(t + 1)
                for f in next_loads:
                    f(t + 1)
            for k, (pre, body, _tail) in enumerate(ops):
                if pre is not None and (k == 0 or not ops[k - 1][2]):
                    for s in range(4):
                        pre(xi, s)
                nxt = ops[k + 1][0] if k + 1 < len(ops) else None
                after = (lambda s, nxt=nxt, xi=xi: nxt(xi, s)) if (nxt is not None and ops[k][2]) else None
                body(xi, t, after)
            self.dma(self.y[t0:t0 + 512, :].rearrange("(s p) d -> p s d", p=128), self.xt[xi][:], reads=self.xB[xi], writes=[self.yB[t]])

    def build(self):
        A = self.A
        self.setup_consts()
        if self.stages == -1:
            self.P.emit(); return self.nc
        self.P.barrier()
        if self.stages == -2:
            self.P.emit(); return self.nc
        m0 = A.mark()
        for _ in self.conv_gen(self.CONV_ORDER[0:3], 4096, ["pool", "dve", "act"]):
            pass
        A.reset(m0)
        self.P.barrier()
        base = A.mark()
        if self.stages <= 0:
            self.P.emit(); return self.nc
        self.alloc_token_bufs()
        self.ring_init()
        tok_mark = A.mark()
        self.alloc_gla_proj()
        xp = self.xnorm_pre
        self.token_phase("xin", [(xp, lambda xi, t, af: self.ffn(xi, 0, 0, af), True), (xp, lambda xi, t, af: self.gla_proj(xi, t), False)],
                         lambda: self.ffn_descs(0, 0) + self.gla_proj_descs())
        if self.stages <= 1:
            self.P.emit(); return self.nc
        self.P.barrier()
        A.reset(base)
        bg = self.conv_gen(self.CONV_ORDER[3:13], 1024, ["act", "dve"], q="sp")
        next(bg)
        self.gla_scan(bg, bg_every=2)
        if self.stages <= 2:
            self.P.emit(); return self.nc
        self.P.barrier()
        A.reset(tok_mark)
        self.pt = A.alloc([128, 4, 256], F32); self.ptB = Buf()
        self.ptb = A.alloc([128, 4, 256], BF16); self.ptbB = Buf()
        self.pT = A.alloc([128, 2, 512], BF16); self.pTB = Buf()
        self.alloc_mla_proj()
        og1 = A.alloc([128, 4, D], BF16); ogb1 = Buf()
        self.ogt = [og1, og1]; self.ogtB = [ogb1, ogb1]
        xp = self.xnorm_pre
        self.token_phase("y", [(None, self.gla_out, True), (xp, lambda xi, t, af: self.ffn(xi, 0, 1, af, mid=lambda: self.ple_prep(0, t)), True),
                               (xp, lambda xi, t, af: self.ple(xi, 0, t, af), True), (xp, lambda xi, t, af: self.ffn(xi, 1, 0, af), True),
                               (xp, lambda xi, t, af: self.mla_proj(xi, t), False)],
                         lambda: self.gla_out_descs() + self.ffn_descs(0, 1) + self.ple_descs(0) + self.ffn_descs(1, 0) + self.mla_proj_descs(),
                         tile_loads=[lambda t: self.ple_load(0, t), self.mla_proj_load], first_loads=[self.gla_out_load])
        if self.stages <= 3:
            self.P.emit(); return self.nc
        self.P.barrier()
        A.reset(base)
        bg = self.conv_gen(self.CONV_ORDER[13:18], 2048, ["act"], q="sp")
        next(bg)
        self.mla_attn(bg)
        if self.stages <= 4:
            self.P.emit(); return self.nc
        self.P.barrier()
        A.reset(tok_mark)
        self.pt = A.alloc([128, 4, 256], F32); self.ptb = A.alloc([128, 4, 256], BF16); self.pT = A.alloc([128, 2, 512], BF16)
        self.ott = [A.alloc([128, 8, 512], BF16) for _ in range(2)]; self.ottB = [Buf(), Buf()]
        xp = self.xnorm_pre
        self.token_phase("y", [(None, self.mla_out, True), (xp, lambda xi, t, af: self.ffn(xi, 1, 1, af, mid=lambda: self.ple_prep(1, t)), True),
                               (xp, lambda xi, t, af: self.ple(xi, 1, t, af), True),
                               (self.final_pre, lambda xi, t, af: self.final_norm(xi), False)],
                         lambda: self.mla_out_descs() + self.ffn_descs(1, 1) + self.ple_descs(1),
                         tile_loads=[lambda t: self.ple_load(1, t)], next_loads=[self.mla_out_load])
        self.P.emit()
        return self.nc


def make_consts():
    c = {}
    c["c_ident"] = np.eye(128, dtype=np.float32)
    s = np.arange(128)[:, None]
    t = np.arange(128)[None, :]
    g = np.float32(-1.0 / 16.0)
    tri = np.zeros((4, 128, 128), np.float32)
    tri[0] = (s <= t) * g
    tri[1] = (s >= t) * g
    tri[2] = (s > t) * g
    tri[3] = (s < t) * g
    c["c_tri"] = tri
    mask = np.zeros((2, 128, 128), np.float32)
    mask[0] = (s <= t)
    mask[1] = (s > t)
    c["c_mask"] = mask
    inv_freq = (np.float32(10000.0) ** (-np.arange(0, 64, 2, dtype=np.float32) / np.float32(64))).astype(np.float32)
    ang = np.arange(4096, dtype=np.float32)[:, None] * inv_freq[None, :]
    c["c_rope"] = np.concatenate([np.cos(ang), np.sin(ang)], axis=1).astype(np.float32)
    return c


_ALL_W = list(WEIGHT_SHAPES) + list(SMALL_SHAPES)


def kernel(**inputs):
    inputs = {k: np.asarray(v) for k, v in inputs.items()}
    xp, xs = inputs["x_prompt"], inputs["x_sample"]
    pp, psm = inputs["p_prompt"], inputs["p_sample"]
    nc = Builder(SEQS_FULL).build()
    consts = make_consts()
    in_maps = []
    for i in range(NCORES):
        m = {}
        m["xin"] = np.ascontiguousarray(np.concatenate([xp[2 * i], xp[2 * i + 1], xs[i]], axis=0))
        m["pin"] = np.ascontiguousarray(np.concatenate([pp[:, 2 * i], pp[:, 2 * i + 1], psm[:, i]], axis=1))
        for n in _ALL_W:
            m[n] = np.ascontiguousarray(inputs[n]).reshape(WEIGHT_SHAPES.get(n, SMALL_SHAPES.get(n)))
        m.update(consts)
        in_maps.append(m)
    res = run_bass_kernel_spmd(nc, in_maps, core_ids=list(range(NCORES)))
    yp = np.empty((16, 4096, D), np.float32)
    ys = np.empty((8, 2048, D), np.float32)
    for i in range(NCORES):
        y = np.asarray(res.results[i]["y"]).reshape(-1, D)
        yp[2 * i] = y[0:4096]
        yp[2 * i + 1] = y[4096:8192]
        ys[i] = y[8192:10240]
    return (yp, ys)
```
